# Optimizing a Trainium2 kernel written in Bass

```python
import jax, jax.numpy as jnp
from jax import lax
import numpy as np

D_MODEL = 1024
BATCH = 8
SEQ = 4096
DEPTH = 2

CTX_LEN = 256
GRID_W = 64
HEAD_DIM = 64
ROPE_BASE = 10000.0
NORM_EPS = 1e-6
NEG_INF = -1e30
NA_HEADS = 8
NA_WIN_ROWS = 8
NA_WIN_COLS = 16
SW_Q_HEADS = 8
SW_KV_HEADS = 2
SW_WINDOW = 128
SW_BLOCK = 128
MLA_HEADS = 8
MLA_Q_RANK = 384
MLA_KV_RANK = 256
MLA_NOPE = 64
MLA_ROPE = 32
MLA_V = 64
MLA_BLOCK = 128
N_BRANCH = 3
BRANCH_W = 512
D_FF = 2816
CONV_W = 3

IN_SIZES = (NA_HEADS * HEAD_DIM, NA_HEADS * HEAD_DIM, NA_HEADS * HEAD_DIM,
            SW_Q_HEADS * HEAD_DIM, SW_KV_HEADS * HEAD_DIM, SW_KV_HEADS * HEAD_DIM,
            MLA_Q_RANK, MLA_KV_RANK, MLA_ROPE, N_BRANCH * D_MODEL)
IN_SPLITS = tuple(sum(IN_SIZES[:i + 1]) for i in range(len(IN_SIZES) - 1))
D_IN = sum(IN_SIZES)

kernel_name = "hybrid_na_swa_mla_convffn_prefix_dit"

f32 = jnp.float32


def rms_norm(x, g):
    xf = x.astype(f32)
    y = xf * lax.rsqrt(jnp.mean(xf * xf, axis=-1, keepdims=True) + NORM_EPS)
    return (y * g.astype(f32)).astype(x.dtype)


def modulate(h, shift, scale):
    return h * (1 + scale) + shift


def joint_softmax(parts, dtype):
    sizes = [p.shape[-1] for p in parts]
    s = jnp.concatenate([p.astype(f32) for p in parts], axis=-1)
    p = jax.nn.softmax(s, axis=-1).astype(dtype)
    idx = [sum(sizes[:i + 1]) for i in range(len(sizes) - 1)]
    return jnp.split(p, idx, axis=-1)


def _rope_1d(xp, pos):
    n = xp.shape[-1] // 2
    inv = ROPE_BASE ** (-jnp.arange(n, dtype=f32) / n)
    ang = pos.astype(f32)[:, None] * inv[None, :]
    cos = jnp.cos(ang)[None, :, None, :].astype(xp.dtype)
    sin = jnp.sin(ang)[None, :, None, :].astype(xp.dtype)
    x1, x2 = xp[..., :n], xp[..., n:]
    return jnp.concatenate([x1 * cos - x2 * sin, x1 * sin + x2 * cos], axis=-1)


def axial_rope(x, row, col):
    half = x.shape[-1] // 2
    return jnp.concatenate([_rope_1d(x[..., :half], row), _rope_1d(x[..., half:], col)], axis=-1)


def project_streams(h, w_in, na_qn, na_kn, sw_qn, sw_kn, mla_qrn, mla_kvrn, w_uq, w_ukv, mla_qn, mla_kn):
    B, L, _ = h.shape
    (na_q, na_k, na_v, sw_q, sw_k, sw_v, c_q, c_kv, k_r, gates) = jnp.split(h @ w_in, IN_SPLITS, axis=-1)
    heads = lambda t, n: t.reshape(B, L, n, -1)
    na_q = rms_norm(heads(na_q, NA_HEADS), na_qn)
    na_k = rms_norm(heads(na_k, NA_HEADS), na_kn)
    na_v = heads(na_v, NA_HEADS)
    sw_q = rms_norm(heads(sw_q, SW_Q_HEADS), sw_qn)
    sw_k = rms_norm(heads(sw_k, SW_KV_HEADS), sw_kn)
    sw_v = heads(sw_v, SW_KV_HEADS)
    mq = heads(rms_norm(c_q, mla_qrn) @ w_uq, MLA_HEADS)
    kv = heads(rms_norm(c_kv, mla_kvrn) @ w_ukv, MLA_HEADS)
    k_rope = jnp.broadcast_to(k_r[:, :, None, :], (B, L, MLA_HEADS, MLA_ROPE))
    mk = jnp.concatenate([kv[..., :MLA_NOPE], k_rope], axis=-1)
    mq = rms_norm(mq, mla_qn)
    mk = rms_norm(mk, mla_kn)
    mv = kv[..., MLA_NOPE:]
    return (na_q, na_k, na_v, sw_q, sw_k, sw_v, mq, mk, mv, gates)


def ctx_attend(q, k, v, sink=None):
    B, L, Hq, d = q.shape
    Hkv = k.shape[2]
    G = Hq // Hkv
    qg = q.reshape(B, L, Hkv, G, d)
    s = jnp.einsum('blkgd,bmkd->bkglm', qg, k) * (d ** -0.5)
    parts = [s]
    if sink is not None:
        parts.append(jnp.broadcast_to(sink.reshape(1, Hkv, G, 1, 1), (B, Hkv, G, L, 1)))
    p = joint_softmax(parts, v.dtype)[0]
    o = jnp.einsum('bkglm,bmkd->blkgd', p, v)
    return o.reshape(B, L, Hq * v.shape[-1])


def na_latent(q, k, v, kc, vc, rpb):
    B, S, H, d = q.shape
    rows = S // GRID_W
    wr = min(NA_WIN_ROWS, rows)
    r = jnp.arange(rows)
    key_rows = jnp.clip(r - wr // 2, 0, rows - wr)[:, None] + jnp.arange(wr)[None, :]
    col = jnp.arange(GRID_W)
    c0 = jnp.clip(col - NA_WIN_COLS // 2, 0, GRID_W - NA_WIN_COLS)
    in_win = (col[None, :] >= c0[:, None]) & (col[None, :] < c0[:, None] + NA_WIN_COLS)
    dr = key_rows - r[:, None] + (NA_WIN_ROWS - 1)
    dc = jnp.clip(col[None, :] - col[:, None], -(NA_WIN_COLS - 1), NA_WIN_COLS - 1) + (NA_WIN_COLS - 1)
    bias = rpb[:, dr[:, None, :, None], dc[None, :, None, :]]
    qg = q.reshape(B, rows, GRID_W, H, d)
    kg = k.reshape(B, rows, GRID_W, H, d)[:, key_rows]
    vg = v.reshape(B, rows, GRID_W, H, d)[:, key_rows].reshape(B, rows, wr * GRID_W, H, d)
    scale = d ** -0.5
    s_loc = jnp.einsum('brqhd,brikhd->bhrqik', qg, kg).astype(f32) * scale + bias.astype(f32)[None]
    s_loc = jnp.where(in_win[:, None, :], s_loc, NEG_INF).reshape(B, H, rows, GRID_W, wr * GRID_W)
    s_ctx = jnp.einsum('brqhd,bchd->bhrqc', qg, kc) * scale
    p_loc, p_ctx = joint_softmax([s_loc, s_ctx], v.dtype)
    o = jnp.einsum('bhrqn,brnhd->brqhd', p_loc, vg) + jnp.einsum('bhrqc,bchd->brqhd', p_ctx, vc)
    return o.reshape(B, S, H * d)


def sw_latent(q, k, v, kc, vc, sink):
    B, S, Hq, d = q.shape
    Hkv = k.shape[2]
    G = Hq // Hkv
    nb = S // SW_BLOCK
    qb = q.reshape(B, nb, SW_BLOCK, Hkv, G, d)

    def band(t):
        tp = jnp.pad(t, ((0, 0), (SW_BLOCK, SW_BLOCK), (0, 0), (0, 0)))
        tp = tp.reshape(B, nb + 2, SW_BLOCK, Hkv, t.shape[-1])
        return jnp.concatenate([tp[:, :-2], tp[:, 1:-1], tp[:, 2:]], axis=2)

    kb, vb = band(k), band(v)
    i = jnp.arange(SW_BLOCK)[:, None]
    j = jnp.arange(3 * SW_BLOCK)[None, :]
    kpos = jnp.arange(nb)[:, None, None] * SW_BLOCK + j[None] - SW_BLOCK
    mask = (jnp.abs(j - SW_BLOCK - i)[None] <= SW_WINDOW) & (kpos >= 0) & (kpos < S)
    scale = d ** -0.5
    s_loc = jnp.einsum('bnqkgd,bnjkd->bkgnqj', qb, kb).astype(f32) * scale
    s_loc = jnp.where(mask, s_loc, NEG_INF)
    s_ctx = jnp.einsum('bnqkgd,bckd->bkgnqc', qb, kc) * scale
    s_sink = jnp.broadcast_to(sink.reshape(1, Hkv, G, 1, 1, 1), (B, Hkv, G, nb, SW_BLOCK, 1))
    p_loc, p_ctx, _ = joint_softmax([s_loc, s_ctx, s_sink], v.dtype)
    o = jnp.einsum('bkgnqj,bnjkd->bnqkgd', p_loc, vb) + jnp.einsum('bkgnqc,bckd->bnqkgd', p_ctx, vc)
    return o.reshape(B, S, Hq * d)


def mla_latent(q, k, v, kc, vc):
    B, S, H, dq = q.shape
    nb = S // MLA_BLOCK
    scale = dq ** -0.5
    qb = jnp.moveaxis(q.reshape(B, nb, MLA_BLOCK, H, dq), 1, 0)

    def one_block(qi):
        s_loc = jnp.einsum('bqhd,bkhd->bhqk', qi, k) * scale
        s_ctx = jnp.einsum('bqhd,bchd->bhqc', qi, kc) * scale
        p_loc, p_ctx = joint_softmax([s_loc, s_ctx], v.dtype)
        return jnp.einsum('bhqk,bkhd->bqhd', p_loc, v) + jnp.einsum('bhqc,bchd->bqhd', p_ctx, vc)

    o = lax.map(one_block, qb)
    return jnp.moveaxis(o, 0, 1).reshape(B, S, H * v.shape[-1])


def merge_branches(o_na, o_sw, o_mla, gates, w_branch, w_out):
    B, L, _ = gates.shape
    g = jax.nn.sigmoid(gates.astype(f32)).astype(gates.dtype).reshape(B, L, N_BRANCH, D_MODEL)
    o = jnp.stack([o_na, o_sw, o_mla], axis=2)
    y = jnp.einsum('blnc,ncd->blnd', o, w_branch)
    return jnp.einsum('blnd,blnd->bld', g, y) @ w_out


def conv_ffn(h, w_up, conv_w, conv_b, w_down):
    u = h @ w_up
    C = u.shape[-1]
    u = lax.conv_general_dilated(u, conv_w[:, None, :].astype(u.dtype), window_strides=(1,),
                                 padding=[(CONV_W // 2, CONV_W // 2)],
                                 dimension_numbers=('NWC', 'WIO', 'NWC'),
                                 feature_group_count=C) + conv_b
    g, val = jnp.split(u, 2, axis=-1)
    return (jax.nn.silu(g) * val) @ w_down


def trunk_layer(x, ctx, mod, mod_c, g_mix, g_ffn, w_in, na_qn, na_kn, rpb, sw_qn, sw_kn, sink,
                mla_qrn, mla_kvrn, w_uq, w_ukv, mla_qn, mla_kn, w_branch, w_out,
                w_up, conv_w, conv_b, w_down, need_ctx):
    B, S, _ = x.shape
    shift_a, scale_a, gate_a, shift_f, scale_f, gate_f = jnp.split(mod, 6, axis=-1)
    cshift_a, cscale_a, cgate_a, cshift_f, cscale_f, cgate_f = jnp.split(mod_c, 6, axis=-1)
    proj = lambda h: project_streams(h, w_in, na_qn, na_kn, sw_qn, sw_kn, mla_qrn, mla_kvrn,
                                     w_uq, w_ukv, mla_qn, mla_kn)
    hx = modulate(rms_norm(x, g_mix), shift_a, scale_a)
    hc = modulate(rms_norm(ctx, g_mix), cshift_a, cscale_a)
    na_q, na_k, na_v, sw_q, sw_k, sw_v, mq, mk, mv, gates = proj(hx)
    cna_q, cna_k, cna_v, csw_q, csw_k, csw_v, cmq, cmk, cmv, cgates = proj(hc)
    t = jnp.arange(S)
    row, col = t // GRID_W, t % GRID_W
    sw_q = axial_rope(sw_q, row, col)
    sw_k = axial_rope(sw_k, row, col)
    mq = jnp.concatenate([mq[..., :MLA_NOPE], axial_rope(mq[..., MLA_NOPE:], row, col)], axis=-1)
    mk = jnp.concatenate([mk[..., :MLA_NOPE], axial_rope(mk[..., MLA_NOPE:], row, col)], axis=-1)
    o_na = na_latent(na_q, na_k, na_v, cna_k, cna_v, rpb)
    o_sw = sw_latent(sw_q, sw_k, sw_v, csw_k, csw_v, sink)
    o_mla = mla_latent(mq, mk, mv, cmk, cmv)
    x = x + gate_a * merge_branches(o_na, o_sw, o_mla, gates, w_branch, w_out)
    hx = modulate(rms_norm(x, g_ffn), shift_f, scale_f)
    x = x + gate_f * conv_ffn(hx, w_up, conv_w, conv_b, w_down)
    if need_ctx:
        c_na = ctx_attend(cna_q, cna_k, cna_v)
        c_sw = ctx_attend(csw_q, csw_k, csw_v, sink)
        c_mla = ctx_attend(cmq, cmk, cmv)
        ctx = ctx + cgate_a * merge_branches(c_na, c_sw, c_mla, cgates, w_branch, w_out)
        hc = modulate(rms_norm(ctx, g_ffn), cshift_f, cscale_f)
        ctx = ctx + cgate_f * conv_ffn(hc, w_up, conv_w, conv_b, w_down)
    return x, ctx


def setup_inputs(seed: int = 0) -> dict:
    key = jax.random.key(seed)
    ks = iter(jax.random.split(key, 32))
    nrm = lambda shape, s: jax.random.normal(next(ks), shape, f32) * s
    L, D = DEPTH, D_MODEL
    gain = lambda shape: 1.0 + nrm(shape, 0.05)
    return {
        "x": nrm((BATCH, SEQ, D), 1.0),
        "c": nrm((BATCH, D), 1.0),
        "ctx": nrm((BATCH, CTX_LEN, D), 1.0),
        "c_ctx": nrm((D,), 1.0),
        "w_ada": nrm((L, D, 6 * D), 0.5 * D ** -0.5),
        "b_ada": nrm((L, 6 * D), 0.02),
        "g_mix": gain((L, D)),
        "g_ffn": gain((L, D)),
        "w_in": nrm((L, D, D_IN), D ** -0.5),
        "na_q_norm": gain((L, HEAD_DIM)),
        "na_k_norm": gain((L, HEAD_DIM)),
        "na_rpb": nrm((L, NA_HEADS, 2 * NA_WIN_ROWS - 1, 2 * NA_WIN_COLS - 1), 0.5),
        "sw_q_norm": gain((L, HEAD_DIM)),
        "sw_k_norm": gain((L, HEAD_DIM)),
        "sw_sink": nrm((L, SW_Q_HEADS), 0.5),
        "mla_q_rank_norm": gain((L, MLA_Q_RANK)),
        "mla_kv_rank_norm": gain((L, MLA_KV_RANK)),
        "w_uq": nrm((L, MLA_Q_RANK, MLA_HEADS * (MLA_NOPE + MLA_ROPE)), MLA_Q_RANK ** -0.5),
        "w_ukv": nrm((L, MLA_KV_RANK, MLA_HEADS * (MLA_NOPE + MLA_V)), MLA_KV_RANK ** -0.5),
        "mla_q_norm": gain((L, MLA_NOPE + MLA_ROPE)),
        "mla_k_norm": gain((L, MLA_NOPE + MLA_ROPE)),
        "w_branch": nrm((L, N_BRANCH, BRANCH_W, D), BRANCH_W ** -0.5),
        "w_out": nrm((L, D, D), D ** -0.5),
        "w_up": nrm((L, D, 2 * D_FF), D ** -0.5),
        "conv_w": nrm((L, CONV_W, 2 * D_FF), 0.5),
        "conv_b": nrm((L, 2 * D_FF), 0.02),
        "w_down": nrm((L, D_FF, D), D_FF ** -0.5),
    }


def reference(x, c, ctx, c_ctx, w_ada, b_ada, g_mix, g_ffn, w_in, na_q_norm, na_k_norm, na_rpb,
              sw_q_norm, sw_k_norm, sw_sink, mla_q_rank_norm, mla_kv_rank_norm, w_uq, w_ukv,
              mla_q_norm, mla_k_norm, w_branch, w_out, w_up, conv_w, conv_b, w_down):
    sc = jax.nn.silu(c)
    scc = jax.nn.silu(c_ctx)
    for l in range(DEPTH):
        mod = (sc @ w_ada[l] + b_ada[l])[:, None, :]
        mod_c = (scc @ w_ada[l] + b_ada[l])[None, None, :]
        x, ctx = trunk_layer(x, ctx, mod, mod_c, g_mix[l], g_ffn[l], w_in[l],
                             na_q_norm[l], na_k_norm[l], na_rpb[l],
                             sw_q_norm[l], sw_k_norm[l], sw_sink[l],
                             mla_q_rank_norm[l], mla_kv_rank_norm[l], w_uq[l], w_ukv[l],
                             mla_q_norm[l], mla_k_norm[l], w_branch[l], w_out[l],
                             w_up[l], conv_w[l], conv_b[l], w_down[l],
                             need_ctx=(l < DEPTH - 1))
    return x
```

```python
import numpy as np
from contextlib import ExitStack
import concourse.bass as bass
import concourse.mybir as mybir
from concourse.bass_utils import run_bass_kernel_spmd

F32 = mybir.dt.float32
BF16 = mybir.dt.bfloat16
AF = mybir.ActivationFunctionType
ALU = mybir.AluOpType

D = 1024
SEQ = 4096
CTX = 256
T = SEQ + CTX
NT = T // 128
DEPTH = 2
D_IN = 6048
DFF = 2816
GRID = 64
NEG = -30000.0
EPS = 1e-6
C_NA = 0
C_SW = 1536
C_ML = 2304
C_G = 2976
SEM_EPOCH = 8000

DEBUG = False
LAYERS = DEPTH


class Buf:
    __slots__ = ("name", "w", "r")

    def __init__(self, name=""):
        self.name = name
        self.w = None
        self.r = {}


class Sched:
    def __init__(self, nc, n_dma_sems=24):
        self.nc = nc
        self.eng = {"pe": nc.tensor, "act": nc.scalar, "dve": nc.vector,
                    "pool": nc.gpsimd, "sp": nc.sync}
        self.cur = {}
        self.cnt = {}
        self.nsem = 0
        self.known = {e: {} for e in self.eng}
        self.allsems = []
        for e in self.eng:
            self._new_sem(e)
        self.dma_sems = []
        for i in range(n_dma_sems):
            s = nc.alloc_semaphore(f"dq{i}")
            self.dma_sems.append([s, 0])
        self.dma_i = 0
        self.bar_sem = nc.alloc_semaphore("barsem")
        self.bar_n = 0
        self.n_ops = 0
        self.n_waits = 0

    def _new_sem(self, e):
        self.cur[e] = self.nc.alloc_semaphore(f"s_{e}_{self.nsem}")
        self.nsem += 1
        self.cnt[e] = 0

    def _wait(self, e, tk):
        sem, val = tk
        k = id(sem)
        kn = self.known[e]
        if kn.get(k, 0) >= val:
            return
        self.eng[e].wait_ge(sem, val)
        kn[k] = val
        self.n_waits += 1

    def _deps(self, e, reads, writes, skip_same_engine=False):
        deps = {}
        for b in reads:
            tk = b.w
            if tk is not None:
                k = id(tk[0])
                if k not in deps or deps[k][1] < tk[1]:
                    deps[k] = tk
        for b in writes:
            tk = b.w
            if tk is not None:
                k = id(tk[0])
                if k not in deps or deps[k][1] < tk[1]:
                    deps[k] = tk
            for k, tk in b.r.items():
                if k not in deps or deps[k][1] < tk[1]:
                    deps[k] = tk
        for tk in deps.values():
            if skip_same_engine and tk[0] is self.cur[e]:
                continue
            self._wait(e, tk)

    def _commit(self, tk, reads, writes):
        k = id(tk[0])
        for b in reads:
            b.r[k] = tk
        for b in writes:
            b.w = tk
            b.r = {}

    def _ticket(self, e, ins):
        if self.cnt[e] >= SEM_EPOCH:
            self._new_sem(e)
        self.cnt[e] += 1
        tk = (self.cur[e], self.cnt[e])
        ins.then_inc(tk[0], 1)
        return tk

    def op(self, e, fn, reads=(), writes=()):
        self._deps(e, reads, writes, skip_same_engine=(e == "pe"))
        ins = fn(self.eng[e])
        self.n_ops += 1
        tk = self._ticket(e, ins)
        self._commit(tk, reads, writes)
        return tk

    def pe_group(self, fns, reads=(), writes=()):
        self._deps("pe", reads, writes, skip_same_engine=True)
        ins = None
        for fn in fns:
            ins = fn(self.eng["pe"])
        self.n_ops += len(fns)
        tk = self._ticket("pe", ins)
        self._commit(tk, reads, writes)
        return tk

    def dma(self, out, in_, reads=(), writes=(), q="sp"):
        slot = self.dma_sems[self.dma_i % len(self.dma_sems)]
        self.dma_i += 1
        sem, v = slot
        if v > 0:
            self._wait(q, (sem, v))
        self._deps(q, reads, writes)
        ins = self.eng[q].dma_start(out=out, in_=in_)
        slot[1] = v + 16
        tk = (sem, v + 16)
        ins.then_inc(sem, 16)
        self._commit(tk, reads, writes)
        self.n_ops += 1
        return tk

    def barrier(self):
        for sem, v in self.dma_sems:
            if v > 0:
                self._wait("sp", (sem, v))
        for e in self.eng:
            if e != "sp" and self.cnt[e] > 0:
                self._wait("sp", (self.cur[e], self.cnt[e]))
        self.bar_n += 1
        self.eng["sp"].sem_inc(self.bar_sem, 1)
        for e in self.eng:
            if e == "sp":
                continue
            self.eng[e].wait_ge(self.bar_sem, self.bar_n)
            kn = self.known[e]
            for sem, v in self.dma_sems:
                kn[id(sem)] = v
            for f in self.eng:
                kn[id(self.cur[f])] = self.cnt[f]


class RR:
    def __init__(self, tiles):
        self.t = tiles
        self.b = [Buf() for _ in tiles]
        self.i = 0

    def get(self):
        j = self.i % len(self.t)
        self.i += 1
        return self.t[j], self.b[j]


def mm(out, lhsT, rhs, start, stop):
    return lambda e: e.matmul(out, lhsT=lhsT, rhs=rhs, start=start, stop=stop)


def build_program(layers=DEPTH, debug=False):
    nc = bass.Bass("TRN2", target_bir_lowering=False)
    S = Sched(nc)
    okind = "ExternalOutput" if debug else "Internal"

    def din(name, shape, dt=F32):
        return nc.dram_tensor(name, list(shape), dt, kind="ExternalInput")

    x_d = din("x", [SEQ, D]).ap()
    ctx_d = din("ctx", [CTX, D]).ap()
    cT_d = din("cT", [128, 8, 2]).ap()
    w_ada_d = din("w_ada", [DEPTH, D, 6 * D]).ap()
    b_ada_h = din("b_ada", [DEPTH, 6 * D])
    g_mix_h = din("g_mix", [DEPTH, D])
    g_ffn_h = din("g_ffn", [DEPTH, D])
    w_in_d = din("w_in", [DEPTH, D, D_IN]).ap()
    rpbg_d = din("rpbg", [DEPTH, 128, 8 * 16 * 64]).ap()
    rpbg2_d = din("rpbg2", [DEPTH, 128, 8 * 22 * 64]).ap()
    negm2_d = din("negm2", [128, 22 * 64]).ap()
    svec_d = din("svec", [DEPTH, 128, 16]).ap()
    sink_h = din("sw_sink", [DEPTH, 8])
    w_uq_d = din("w_uq", [DEPTH, 384, 768]).ap()
    w_ukv_d = din("w_ukv", [DEPTH, 256, 1024]).ap()
    w_br_d = din("w_branch", [DEPTH, 3, 512, D]).ap()
    w_out_d = din("w_out", [DEPTH, D, D]).ap()
    w_up_d = din("w_up", [DEPTH, D, 2 * DFF]).ap()
    cw_d = din("cw", [DEPTH, 128, 44 * 3]).ap()
    cb_d = din("cb", [DEPTH, 128, 44]).ap()
    w_dn_d = din("w_down", [DEPTH, DFF, D]).ap()
    cm_d = din("cmats", [6, 128, 128]).ap()
    cs_sw_d = din("cs_sw", [2, 128, SEQ]).ap()
    cs_ml_d = din("cs_ml", [2, 128, SEQ]).ap()
    msk_sw_d = din("msk_sw", [2, 128, 128]).ap()
    negm_d = din("negm", [128, 64]).ap()

    out_d = nc.dram_tensor("out", [SEQ, D], F32, kind="ExternalOutput").ap()
    mod_h = nc.dram_tensor("mod_s", [DEPTH, 2, 6 * D], F32, kind=okind)
    mod_d = mod_h.ap()
    xs_d = nc.dram_tensor("xs_s", [T, D], F32, kind=okind).ap()
    nod = DEPTH if debug else 1
    o_d = nc.dram_tensor("o_s", [nod, 3, 512, T], BF16, kind=okind).ap()
    hT_d = nc.dram_tensor("hT_s", [128, 8, T], BF16, kind=okind).ap()
    NRS = 6
    rs_h = nc.dram_tensor("rs_s", [NRS, 512], F32, kind="Internal")
    rs_d = rs_h.ap()
    B_rsd = [Buf() for _ in range(NRS)]
    rs_i = [0]

    B_mod = [Buf() for _ in range(DEPTH)]
    B_xs = [Buf() for _ in range(NT)]
    B_o = [[Buf() for _ in range(NT)] for _ in range(3)]
    B_out = Buf()

    _uid = [0]

    def sb(es, name, shape, dt):
        _uid[0] += 1
        return es.enter_context(nc.sbuf_tensor(f"sb_{name}_{_uid[0]}", list(shape), dt))

    def rr(es, name, shape, dt, n):
        return RR([sb(es, f"{name}{i}", shape, dt) for i in range(n)])

    def bcast_ap(handle, offset, n):
        return bass.AP(tensor=handle, offset=offset, ap=[[0, 128], [1, n]])

    with ExitStack() as top:
        psb = [top.enter_context(nc.psum_tensor(f"psb{i}", [128, 512], F32)) for i in range(7)]
        psT = top.enter_context(nc.psum_tensor("psT", [128, 1024], BF16))
        B_psT = Buf()
        B_hT = [Buf() for _ in range(NT)]
        cm = sb(top, "cm", [128, 6, 128], BF16)
        B_cm = Buf()
        S.dma(cm[:], cm_d.rearrange("m p n -> p m n"), writes=[B_cm], q="pool")
        ident = cm[:, 0, :]
        blockones = cm[:, 1, :]
        allones = cm[:, 2, :]
        perm_sw = cm[:, 3, :]
        perm_ml = cm[:, 4, :]
        shiftm = cm[:, 5, :]
        onesf = sb(top, "onesf", [128, 64], F32)
        epst = sb(top, "epst", [128, 1], F32)
        B_const = Buf()
        S.op("dve", lambda e: e.memset(onesf[:], 1.0), writes=[B_const])
        S.op("dve", lambda e: e.memset(epst[:], EPS), writes=[B_const])
        svec = sb(top, "svec", [128, 16], F32)
        svq = sb(top, "svq", [128, 16], F32)
        B_sv = Buf()

        def hbufs(a, b):
            return B_hT[a // 128:(b + 127) // 128]

        def xbufs(a, b):
            return B_xs[a // 128:(b + 127) // 128]

        def hload(pool, a, n):
            ht, Bht = pool.get()
            S.dma(ht[:, :, 0:n], hT_d[:, :, a:a + n], reads=hbufs(a, a + n), writes=[Bht])
            return ht, Bht

        def hstore(hst, Bhst, a, n):
            S.dma(hT_d[:, :, a:a + n], hst[:, :, 0:n], reads=[Bhst], writes=hbufs(a, a + n))

        def norm_to_hT(es_pools, xt, Bx, G, SH, Bg, hst, Bhst, slot):
            P = es_pools
            st, Bst = P["stat"].get()
            jk, Bjk = P["junk"].get()
            S.op("act", lambda e: e.activation(out=jk[:], in_=xt[:], func=AF.Square, accum_out=st[:, 0:1]),
                 reads=[Bx], writes=[Bjk, Bst])
            S.op("act", lambda e: e.activation(out=st[:, 1:2], in_=st[:, 0:1], func=AF.Sqrt,
                                               scale=1.0 / D, bias=epst[:, 0:1]),
                 reads=[Bst, B_const], writes=[Bst])
            S.op("dve", lambda e: e.reciprocal(out=st[:, 2:3], in_=st[:, 1:2]), reads=[Bst], writes=[Bst])
            tm, Btm = P["tmpf"].get()
            S.op("dve", lambda e: e.scalar_tensor_tensor(out=tm[:], in0=xt[:], scalar=st[:, 2:3], in1=G[:],
                                                         op0=ALU.mult, op1=ALU.mult),
                 reads=[Bx, Bst, Bg], writes=[Btm])
            hb, Bhb = P["hb"].get()
            S.op("pool", lambda e: e.tensor_tensor(out=hb[:], in0=tm[:], in1=SH[:], op=ALU.add),
                 reads=[Btm, Bg], writes=[Bhb])
            S.pe_group([(lambda e, k=k: e.transpose(psT[:, k * 128:(k + 1) * 128], hb[:, k * 128:(k + 1) * 128], ident))
                        for k in range(8)], reads=[Bhb, B_cm], writes=[B_psT])
            S.op("act", lambda e: e.activation(out=hst[:, :, slot * 128:(slot + 1) * 128],
                                               in_=psT[:, :].rearrange("p (k n) -> p k n", k=8), func=AF.Identity),
                 reads=[B_psT], writes=[Bhst])

        def load_mod_bcast(tile, Bt, l, row, which):
            S.dma(tile[:], bcast_ap(mod_h, (l * 2 + row) * 6 * D + which * D, D), reads=[B_mod[l]], writes=[Bt])

        def make_G(es, l, row, which_scale, ghandle, name):
            Gt = sb(es, name, [128, D], F32)
            Bg = Buf()
            gt = sb(es, name + "g", [128, D], F32)
            Bgt = Buf()
            load_mod_bcast(Gt, Bg, l, row, which_scale)
            S.dma(gt[:], bcast_ap(ghandle, l * D, D), writes=[Bgt])
            S.op("dve", lambda e: e.scalar_tensor_tensor(out=Gt[:], in0=Gt[:], scalar=1.0, in1=gt[:],
                                                         op0=ALU.add, op1=ALU.mult),
                 reads=[Bg, Bgt], writes=[Bg])
            return Gt, Bg

        def norm_pools(es, depth=2):
            return {"stat": rr(es, "nstat", [128, 4], F32, depth + 1),
                    "junk": rr(es, "njunk", [128, D], BF16, depth),
                    "tmpf": rr(es, "ntmpf", [128, D], F32, depth),
                    "hb": rr(es, "nhb", [128, D], BF16, depth)}

        def load_w(dst, src_rows, Bw):
            S.dma(dst, src_rows.rearrange("(kc p) n -> p kc n", p=128), writes=[Bw], q="pool")

        def phase_ada(l):
            with ExitStack() as es:
                cT = sb(es, "cT", [128, 8, 2], F32)
                Bc = Buf()
                S.dma(cT[:], cT_d, writes=[Bc])
                S.op("act", lambda e: e.activation(out=cT[:], in_=cT[:], func=AF.Silu), reads=[Bc], writes=[Bc])
                bada = sb(es, "bada", [2, 6 * D], F32)
                Bb = Buf()
                S.dma(bada[:], bass.AP(tensor=b_ada_h, offset=l * 6 * D, ap=[[0, 2], [1, 6 * D]]), writes=[Bb])
                modsb = sb(es, "modsb", [2, 6 * D], F32)
                Bm = Buf()
                wa = rr(es, "wa", [128, 8, 512], F32, 4)
                pp = RR(psb[0:2])
                for j in range(12):
                    wt, Bw = wa.get()
                    S.dma(wt[:], w_ada_d[l, :, j * 512:(j + 1) * 512].rearrange("(kc p) n -> p kc n", p=128),
                          writes=[Bw])
                    ps, Bp = pp.get()
                    S.pe_group([mm(ps[0:2, :], cT[:, k, :], wt[:, k, :], k == 0, k == 7) for k in range(8)],
                               reads=[Bc, Bw], writes=[Bp])
                    S.op("dve", lambda e: e.tensor_tensor(out=modsb[:, j * 512:(j + 1) * 512], in0=ps[0:2, :],
                                                          in1=bada[:, j * 512:(j + 1) * 512], op=ALU.add),
                         reads=[Bp, Bb], writes=[Bm])
                S.dma(mod_d[l], modsb[:], reads=[Bm], writes=[B_mod[l]])
                S.dma(svec[:], svec_d[l], writes=[B_sv])
                S.op("dve", lambda e: e.tensor_scalar(out=svq[:, 0:4], in0=svec[:, 0:4], scalar1=0.125, scalar2=None,
                                                      op0=ALU.mult), reads=[B_sv], writes=[B_sv])
                S.op("dve", lambda e: e.tensor_scalar(out=svq[:, 9:10], in0=svec[:, 9:10], scalar1=96.0 ** -0.5,
                                                      scalar2=None, op0=ALU.mult), reads=[B_sv], writes=[B_sv])
                S.barrier()

        def phase_n1(l):
            with ExitStack() as es:
                P = norm_pools(es, 4)
                xin = rr(es, "n1x", [128, D], F32, 6)
                Gl, Bgl = make_G(es, l, 0, 1, g_mix_h, "n1Gl")
                Gc, Bgc = make_G(es, l, 1, 1, g_mix_h, "n1Gc")
                SHl = sb(es, "n1SHl", [128, D], F32)
                SHc = sb(es, "n1SHc", [128, D], F32)
                load_mod_bcast(SHl, Bgl, l, 0, 0)
                load_mod_bcast(SHc, Bgc, l, 1, 0)
                hstp = rr(es, "n1hst", [128, 8, 512], BF16, 2)
                for t in range(NT):
                    xt, Bx = xin.get()
                    if l == 0:
                        src = x_d[t * 128:(t + 1) * 128, :] if t < 32 else ctx_d[(t - 32) * 128:(t - 31) * 128, :]
                        S.dma(xt[:], src, writes=[Bx])
                    else:
                        S.dma(xt[:], xs_d[t * 128:(t + 1) * 128, :], reads=[B_xs[t]], writes=[Bx])
                    if t % 4 == 0:
                        hst, Bhst = hstp.get()
                    if t < 32:
                        norm_to_hT(P, xt, Bx, Gl, SHl, Bgl, hst, Bhst, t % 4)
                    else:
                        norm_to_hT(P, xt, Bx, Gc, SHc, Bgc, hst, Bhst, t % 4)
                    if t % 4 == 3 or t == NT - 1:
                        a0 = (t // 4) * 512
                        hstore(hst, Bhst, a0, (t + 1) * 128 - a0)
                S.barrier()

        def run_tasks(gens, depth=3):
            active = []
            it = iter(gens)
            while True:
                while len(active) < depth:
                    g = next(it, None)
                    if g is None:
                        break
                    active.append(g)
                if not active:
                    break
                for g in list(active):
                    try:
                        next(g)
                    except StopIteration:
                        active.remove(g)

        def qk_task(P, mmf, mrd, npart, n, onesm, inv_d, gcol, dst, Bdst, rope=None):
            ps, Bps = P["ps1"].get()
            S.pe_group(mmf(ps), reads=mrd, writes=[Bps])
            sq, Bsq = P["sq"].get()
            S.op("act", lambda e: e.activation(out=sq[0:npart, 0:n], in_=ps[0:npart, 0:n], func=AF.Square),
                 reads=[Bps], writes=[Bsq])
            yield
            p2, Bp2 = P["ps2"].get()
            S.pe_group([mm(p2[0:npart, 0:n], onesm[0:npart, 0:npart], sq[0:npart, 0:n], True, True)],
                       reads=[Bsq, B_cm], writes=[Bp2])
            yield
            rt, Brt = P["rt"].get()
            S.op("act", lambda e: e.activation(out=rt[0:npart, 0:n], in_=p2[0:npart, 0:n], func=AF.Ln,
                                               scale=inv_d, bias=epst[0:npart, 0:1]),
                 reads=[Bp2, B_const], writes=[Brt])
            yield
            S.op("act", lambda e: e.activation(out=rt[0:npart, 0:n], in_=rt[0:npart, 0:n], func=AF.Exp, scale=-0.5),
                 reads=[Brt], writes=[Brt])
            if rope is None:
                S.op("dve", lambda e: e.scalar_tensor_tensor(out=dst, in0=ps[0:npart, 0:n], scalar=gcol,
                                                             in1=rt[0:npart, 0:n], op0=ALU.mult, op1=ALU.mult),
                     reads=[Bps, Brt, B_sv], writes=[Bdst])
                return
            perm, cos_ap, sin_ap, Bcs = rope
            qn, Bqn = P["qn"].get()
            S.op("dve", lambda e: e.scalar_tensor_tensor(out=qn[0:npart, 0:n], in0=ps[0:npart, 0:n], scalar=gcol,
                                                         in1=rt[0:npart, 0:n], op0=ALU.mult, op1=ALU.mult),
                 reads=[Bps, Brt, B_sv], writes=[Bqn])
            yield
            p3, Bp3 = P["ps2"].get()
            S.pe_group([mm(p3[0:npart, 0:n], perm[0:npart, 0:npart], qn[0:npart, 0:n], True, True)],
                       reads=[Bqn, B_cm], writes=[Bp3])
            t1, Bt1 = P["rt"].get()
            S.op("pool", lambda e: e.tensor_tensor(out=t1[0:npart, 0:n], in0=qn[0:npart, 0:n], in1=cos_ap, op=ALU.mult),
                 reads=[Bqn, Bcs], writes=[Bt1])
            yield
            t2, Bt2 = P["rt"].get()
            S.op("dve", lambda e: e.tensor_tensor(out=t2[0:npart, 0:n], in0=p3[0:npart, 0:n], in1=sin_ap, op=ALU.mult),
                 reads=[Bp3, Bcs], writes=[Bt2])
            S.op("pool", lambda e: e.tensor_tensor(out=dst, in0=t1[0:npart, 0:n], in1=t2[0:npart, 0:n], op=ALU.add),
                 reads=[Bt1, Bt2], writes=[Bdst])

        def v_task(P, mmf, mrd, ncol, dst, Bdst):
            ps, Bps = P["ps1"].get()
            S.pe_group(mmf(ps), reads=mrd, writes=[Bps])
            S.op("act", lambda e: e.activation(out=dst, in_=ps[:, 0:ncol].rearrange("p (h d) -> p h d", d=64),
                                               func=AF.Identity), reads=[Bps], writes=[Bdst])
            yield

        def proj_pools(es):
            return {"sq": rr(es, "psq", [128, 512], BF16, 4),
                    "rt": rr(es, "prt", [128, 512], F32, 8),
                    "qn": rr(es, "pqn", [128, 512], BF16, 4),
                    "ps1": RR(psb[0:3]),
                    "ps2": RR(psb[3:7])}

        def attn_pools(es, nsc):
            return {"sc": RR(psb[0:nsc]), "acc": RR(psb[4:7]),
                    "pT": rr(es, "apT", [128, 512], BF16, nsc + 1),
                    "rs": rr(es, "ars", [128, 512], F32, 3),
                    "bcs": rr(es, "abcs", [64, 512], F32, 3)}

        pend_pv = []
        LOOK = 2
        FILL = [0, None]

        def flush_pv(keep=0):
            while len(pend_pv) > keep:
                fns, rds, BpsO_ = pend_pv.pop(0)
                S.pe_group(fns, reads=rds, writes=[BpsO_])

        def attend_groups(P, psO, BpsO, col0, n, groups, first):
            started = not first
            ng = len(groups)
            for gi, grp in enumerate(groups):
                sc, Bsc = P["sc"].get()
                fns = []
                rds = []
                for ti, tl in enumerate(grp):
                    fns += tl[0](sc[:, ti * n:(ti + 1) * n])
                    rds += tl[1]
                S.pe_group(fns, reads=rds, writes=[Bsc])
                pT, BpT = P["pT"].get()
                w = len(grp) * n
                S.op("act", lambda e: e.activation(out=pT[:, 0:w], in_=sc[:, 0:w], func=AF.Exp),
                     reads=[Bsc], writes=[BpT])
                for ti, tl in enumerate(grp):
                    if len(tl) > 5 and tl[5] is not None:
                        b_ap, b_rd = tl[5]
                        S.op("dve", lambda e: e.tensor_tensor(out=pT[:, ti * n:(ti + 1) * n], in0=pT[:, ti * n:(ti + 1) * n],
                                                              in1=b_ap, op=ALU.mult), reads=[BpT] + b_rd, writes=[BpT])
                fns = []
                rds = [BpT]
                for ti, tl in enumerate(grp):
                    (p0, p1), vl, vrd = tl[2], tl[3], tl[4]
                    last = (gi == ng - 1) and (ti == len(grp) - 1)
                    fns.append(mm(psO[0:65, col0:col0 + n], vl, pT[p0:p1, ti * n:(ti + 1) * n], not started, last))
                    started = True
                    rds += vrd
                pend_pv.append((fns, rds, BpsO))
                flush_pv(LOOK)
                for _ in range(FILL[0]):
                    nc.tensor.matmul(psb[3][:, :], lhsT=ident, rhs=FILL[1], start=True, stop=True)

        pend_norm = []

        def flush_norm():
            while pend_norm:
                pend_norm.pop(0)()

        def normalize(P, psO, BpsO, n, dst, Bdst, sink_ap=None):
            flush_pv()
            rs, Brs = P["rs"].get()
            if sink_ap is not None:
                S.op("act", lambda e: e.activation(out=rs[64:65, 0:n], in_=psO[64:65, 0:n], func=AF.Ln, bias=sink_ap),
                     reads=[BpsO, B_sv], writes=[Brs])
            else:
                S.op("act", lambda e: e.activation(out=rs[64:65, 0:n], in_=psO[64:65, 0:n], func=AF.Ln),
                     reads=[BpsO], writes=[Brs])
            S.op("act", lambda e: e.activation(out=rs[64:65, 0:n], in_=rs[64:65, 0:n], func=AF.Exp, scale=-1.0),
                 reads=[Brs], writes=[Brs])
            slot = rs_i[0] % NRS
            rs_i[0] += 1
            S.dma(rs_d[slot:slot + 1, 0:n], rs[64:65, 0:n], reads=[Brs], writes=[B_rsd[slot]])
            bc, Bbc = P["bcs"].get()
            S.dma(bc[0:64, 0:n], bass.AP(tensor=rs_h, offset=slot * 512, ap=[[0, 64], [1, n]]),
                  reads=[B_rsd[slot]], writes=[Bbc])
            flush_norm()
            pend_norm.append(lambda: S.op("dve", lambda e: e.tensor_tensor(out=dst, in0=psO[0:64, 0:n], in1=bc[0:64, 0:n],
                                                                           op=ALU.mult),
                                          reads=[BpsO, Bbc], writes=[Bdst]))

        def close_acc(psO, BpsO, col0, n, vl_dummy=None):
            pass

        def store_o(l, br, ost, Bost, a, n):
            li = l if debug else 0
            flush_norm()
            dst = o_d[li, br, :, a:a + n].rearrange("(h d) n -> d h n", d=64)
            S.dma(dst, ost[0:64, :, 0:n], reads=[Bost], writes=B_o[br][a // 128:(a + n + 127) // 128])

        def phase_na(l, need_ctx):
            with ExitStack() as es:
                Wt = sb(es, "naW", [128, 8, 1536], BF16)
                Bw = Buf()
                for c3 in range(3):
                    load_w(Wt[:, :, c3 * 512:(c3 + 1) * 512], w_in_d[l, :, C_NA + c3 * 512:C_NA + (c3 + 1) * 512], Bw)
                QT = sb(es, "naQ", [128, 4, T], BF16)
                KT = sb(es, "naK", [128, 4, T], BF16)
                Vp = sb(es, "naV", [128, NT, 8, 65], BF16)
                Bq = [Buf() for _ in range(9)]
                Bk = [Buf() for _ in range(9)]
                Bv = [Buf() for _ in range(NT)]
                Bvo = Buf()
                S.op("pool", lambda e: e.memset(Vp[:, :, :, 64:65], 1.0), writes=[Bvo])
                Ct = sb(es, "naC", [128, 8 * 16 * 64], BF16)
                Ct2 = sb(es, "naC2", [128, 8 * 22 * 64], BF16)
                Bc = Buf()
                with ExitStack() as es2:
                    Cg = sb(es2, "naCg", [128, 8 * 16, 64], F32)
                    ng = sb(es2, "naNg", [128, 64], F32)
                    Bcg = Buf()
                    S.dma(Cg[:], rpbg_d[l].rearrange("p (a q) -> p a q", q=64), writes=[Bcg])
                    S.dma(ng[:], negm_d, writes=[Bcg])
                    ngb = bass.AP(tensor=ng, offset=0, ap=[[64, 128], [0, 128], [1, 64]])
                    S.op("dve", lambda e: e.tensor_tensor(out=Ct[:].rearrange("p (a q) -> p a q", q=64), in0=Cg[:],
                                                          in1=ngb, op=ALU.add),
                         reads=[Bcg], writes=[Bc])
                    ng2 = sb(es2, "naNg2", [128, 22 * 64], F32)
                    S.dma(ng2[:], negm2_d, writes=[Bcg])
                    TW = 22 * 64
                    for hh2 in range(2):
                        S.dma(Cg[:, 0:88, :], rpbg2_d[l, :, hh2 * 4 * TW:(hh2 + 1) * 4 * TW].rearrange("p (a q) -> p a q", q=64),
                              reads=[Bcg], writes=[Bcg])
                        ngb2 = bass.AP(tensor=ng2, offset=0, ap=[[TW, 128], [0, 4], [1, TW]])
                        S.op("dve", lambda e: e.tensor_tensor(
                            out=Ct2[:, hh2 * 4 * TW:(hh2 + 1) * 4 * TW].rearrange("p (h q) -> p h q", q=TW),
                            in0=Cg[:, 0:88, :].rearrange("p (h u) q -> p h (u q)", h=4), in1=ngb2, op=ALU.add),
                            reads=[Bcg], writes=[Bc])
                    for hh2 in range(4):
                        S.op("act", lambda e: e.activation(out=Ct2[:, hh2 * 2 * TW:(hh2 + 1) * 2 * TW],
                                                           in_=Ct2[:, hh2 * 2 * TW:(hh2 + 1) * 2 * TW], func=AF.Exp),
                             reads=[Bc], writes=[Bc])
                    S.barrier()
                with ExitStack() as es2:
                    P = proj_pools(es2)
                    hbp = rr(es2, "nahb", [128, 8, 512], BF16, 2)
                    for tb in range(9):
                        a = tb * 512
                        n = 512 if tb < 8 else 256
                        hT, Bh = hload(hbp, a, n)
                        hb = [Bh]
                        tasks = []
                        for which, dstT, Bd, gc in ((0, QT, Bq, svq[:, 0:1]), (1, KT, Bk, svec[:, 1:2])):
                            if which == 0 and tb == 8 and not need_ctx:
                                continue
                            for c in range(4):
                                col = which * 512 + c * 128
                                mmf = (lambda ps, col=col, hT=hT, n=n: [mm(ps[:, 0:n], Wt[:, k, col:col + 128], hT[:, k, 0:n], k == 0, k == 7)
                                                                       for k in range(8)])
                                tasks.append(qk_task(P, mmf, [Bw] + hb, 128, n, blockones, 1.0 / 64, gc,
                                                     dstT[:, c, a:a + n], Bd[tb]))
                        run_tasks(tasks)
                        tasks = []
                        for t in range(a // 128, (a + n) // 128):
                            mmf = (lambda ps, t=t, hT=hT, a=a: [mm(ps[:, :], hT[:, k, t * 128 - a:(t + 1) * 128 - a], Wt[:, k, 1024:1536],
                                                                 k == 0, k == 7) for k in range(8)])
                            tasks.append(v_task(P, mmf, [Bw, Bh], 512, Vp[:, t, :, 0:64], Bv[t]))
                        run_tasks(tasks)
                    S.barrier()
                with ExitStack() as es2:
                    P = attn_pools(es2, 3)
                    FILL[0], FILL[1] = 0, KT[:, 0, 0:512]
                    ostp = rr(es2, "naost", [64, 8, 512], BF16, 2)
                    for qb in range(8):
                        ost, Bost = ostp.get()
                        for h in range(8):
                            c, hp = h // 2, (h % 2) * 64
                            psO, BpsO = P["acc"].get()
                            TW = 22 * 64
                            if 1 <= qb <= 6:
                                q_rhs = QT[hp:hp + 64, c, qb * 512:(qb + 1) * 512]
                                groups = []
                                for j in range(4 * qb - 2, 4 * qb + 6):
                                    t0 = 10 + 8 * qb - 2 * j
                                    assert 0 <= t0 and t0 + 8 <= 22
                                    cb_ap = Ct2[:, h * TW + t0 * 64:h * TW + (t0 + 8) * 64]

                                    def sfn(o, j=j):
                                        return [mm(o, KT[hp:hp + 64, c, j * 128:(j + 1) * 128], q_rhs, True, True)]
                                    groups.append([(sfn, [Bk[j // 4], Bq[qb]], (0, 128), Vp[:, j, h, :], [Bv[j], Bvo], (cb_ap, [Bc]))])
                                for j in (32, 33):
                                    def sfn(o, j=j):
                                        return [mm(o, KT[hp:hp + 64, c, j * 128:(j + 1) * 128], q_rhs, True, True)]
                                    groups.append([(sfn, [Bk[8], Bq[qb]], (0, 128), Vp[:, j, h, :], [Bv[j], Bvo], None)])
                                attend_groups(P, psO, BpsO, 0, 512, groups, True)
                                normalize(P, psO, BpsO, 512, ost[0:64, h, :], Bost)
                                continue
                            for gg in range(2):
                                g = qb * 2 + gg
                                if 1 <= g <= 14:
                                    qa = g * 256
                                    q_rhs = QT[hp:hp + 64, c, qa:qa + 256]
                                    tiles = []
                                    for j in range(2 * g - 2, 2 * g + 4):
                                        t0 = 10 + 4 * g - 2 * j
                                        assert 0 <= t0 and t0 + 4 <= 22
                                        cb_ap = Ct2[:, h * TW + t0 * 64:h * TW + (t0 + 4) * 64]

                                        def sfn(o, j=j, q_rhs=q_rhs):
                                            return [mm(o, KT[hp:hp + 64, c, j * 128:(j + 1) * 128], q_rhs, True, True)]
                                        tiles.append((sfn, [Bk[j // 4], Bq[qb]], (0, 128), Vp[:, j, h, :], [Bv[j], Bvo], (cb_ap, [Bc])))
                                    for j in (32, 33):
                                        def sfn(o, j=j, q_rhs=q_rhs):
                                            return [mm(o, KT[hp:hp + 64, c, j * 128:(j + 1) * 128], q_rhs, True, True)]
                                        tiles.append((sfn, [Bk[8], Bq[qb]], (0, 128), Vp[:, j, h, :], [Bv[j], Bvo], None))
                                    attend_groups(P, psO, BpsO, gg * 256, 256, [tiles[0:2], tiles[2:4], tiles[4:6], tiles[6:8]], True)
                                    continue
                                for rr_ in range(4):
                                    r = g * 4 + rr_
                                    r0 = min(max(r - 4, 0), 56)
                                    grp = []
                                    qa = r * 64
                                    q_rhs = QT[hp:hp + 64, c, qa:qa + 64]
                                    for j in range(r0 // 2, (r0 + 7) // 2 + 1):
                                        lo_ok = r0 <= 2 * j < r0 + 8
                                        up_ok = r0 <= 2 * j + 1 < r0 + 8
                                        rows = (0 if lo_ok else 64, 128 if up_ok else 64)
                                        s_ = 2 * j - r + 8
                                        assert 0 <= s_ <= 15
                                        cb_ap = Ct[:, (h * 16 + s_) * 64:(h * 16 + s_ + 1) * 64]

                                        def sfn(o, j=j, cb_ap=cb_ap, q_rhs=q_rhs):
                                            return [mm(o, KT[hp:hp + 64, c, j * 128:(j + 1) * 128], q_rhs, True, False),
                                                    mm(o, ident, cb_ap, False, True)]
                                        grp.append((sfn, [Bk[j // 4], Bq[qb], Bc, B_cm], rows,
                                                    Vp[rows[0]:rows[1], j, h, :], [Bv[j], Bvo]))
                                    for j in (32, 33):
                                        def sfn(o, j=j, q_rhs=q_rhs):
                                            return [mm(o, KT[hp:hp + 64, c, j * 128:(j + 1) * 128], q_rhs, True, True)]
                                        grp.append((sfn, [Bk[8], Bq[qb]], (0, 128), Vp[:, j, h, :], [Bv[j], Bvo]))
                                    attend_groups(P, psO, BpsO, gg * 256 + rr_ * 64, 64, [grp], True)
                            normalize(P, psO, BpsO, 512, ost[0:64, h, :], Bost)
                        store_o(l, 0, ost, Bost, qb * 512, 512)
                    if need_ctx:
                        ost, Bost = ostp.get()
                        for h in range(8):
                            c, hp = h // 2, (h % 2) * 64
                            psO, BpsO = P["acc"].get()
                            q_rhs = QT[hp:hp + 64, c, SEQ:T]
                            groups = []
                            for j in (32, 33):
                                def sfn(o, j=j):
                                    return [mm(o, KT[hp:hp + 64, c, j * 128:(j + 1) * 128], q_rhs, True, True)]
                                groups.append([(sfn, [Bk[8], Bq[8]], (0, 128), Vp[:, j, h, :], [Bv[j], Bvo])])
                            attend_groups(P, psO, BpsO, 0, 256, groups, True)
                            normalize(P, psO, BpsO, 256, ost[0:64, h, 0:256], Bost)
                        store_o(l, 0, ost, Bost, SEQ, 256)
                    S.barrier()

        def phase_sw(l, need_ctx):
            with ExitStack() as es:
                Wt = sb(es, "swW", [128, 8, 768], BF16)
                Bw = Buf()
                load_w(Wt[:, :, 0:512], w_in_d[l, :, C_SW:C_SW + 512], Bw)
                load_w(Wt[:, :, 512:768], w_in_d[l, :, C_SW + 512:C_SW + 768], Bw)
                QT = sb(es, "swQ", [128, 4, T], BF16)
                KT = sb(es, "swK", [128, T], BF16)
                Vp = sb(es, "swV", [128, NT, 2, 65], BF16)
                msk = sb(es, "swmsk", [128, 2, 128], BF16)
                sinke = sb(es, "swsink", [128, 8], F32)
                Bcs = Buf()
                S.dma(msk[:], msk_sw_d.rearrange("m p n -> p m n"), writes=[Bcs], q="pool")
                S.op("act", lambda e: e.activation(out=msk[:], in_=msk[:], func=AF.Exp), reads=[Bcs], writes=[Bcs])
                S.dma(sinke[:], bcast_ap(sink_h, l * 8, 8), writes=[Bcs])
                S.op("act", lambda e: e.activation(out=sinke[:], in_=sinke[:], func=AF.Exp), reads=[Bcs], writes=[Bcs])
                Bq = [Buf() for _ in range(9)]
                Bk = [Buf() for _ in range(9)]
                Bv = [Buf() for _ in range(NT)]
                Bvo = Buf()
                S.op("pool", lambda e: e.memset(Vp[:, :, :, 64:65], 1.0), writes=[Bvo])
                with ExitStack() as es2:
                    P = proj_pools(es2)
                    hbp = rr(es2, "swhb", [128, 8, 512], BF16, 2)
                    csp = rr(es2, "swcs", [128, 2, 512], F32, 2)
                    for tb in range(9):
                        a = tb * 512
                        n = 512 if tb < 8 else 256
                        hT, Bh = hload(hbp, a, n)
                        hb = [Bh]
                        rope = None
                        if tb < 8:
                            cs, Bcsb = csp.get()
                            S.dma(cs[:], cs_sw_d[:, :, a:a + n].rearrange("m p n -> p m n"), writes=[Bcsb])
                            rope = (perm_sw, cs[:, 0, 0:n], cs[:, 1, 0:n], Bcsb)
                        tasks = []
                        for c in range(4):
                            if tb == 8 and not need_ctx:
                                continue
                            mmf = (lambda ps, c=c, hT=hT, n=n: [mm(ps[:, 0:n], Wt[:, k, c * 128:(c + 1) * 128], hT[:, k, 0:n], k == 0, k == 7)
                                                                for k in range(8)])
                            tasks.append(qk_task(P, mmf, [Bw] + hb, 128, n, blockones, 1.0 / 64, svq[:, 2:3],
                                                 QT[:, c, a:a + n], Bq[tb], rope))
                        mmf = (lambda ps, hT=hT, n=n: [mm(ps[:, 0:n], Wt[:, k, 512:640], hT[:, k, 0:n], k == 0, k == 7) for k in range(8)])
                        tasks.append(qk_task(P, mmf, [Bw] + hb, 128, n, blockones, 1.0 / 64, svec[:, 3:4], KT[:, a:a + n], Bk[tb], rope))
                        run_tasks(tasks)
                        tasks = []
                        for t in range(a // 128, (a + n) // 128):
                            mmf = (lambda ps, t=t, hT=hT, a=a: [mm(ps[:, 0:128], hT[:, k, t * 128 - a:(t + 1) * 128 - a], Wt[:, k, 640:768],
                                                                 k == 0, k == 7) for k in range(8)])
                            tasks.append(v_task(P, mmf, [Bw, Bh], 128, Vp[:, t, :, 0:64], Bv[t]))
                        run_tasks(tasks)
                    S.barrier()
                with ExitStack() as es2:
                    P = attn_pools(es2, 3)
                    FILL[0], FILL[1] = 0, QT[:, 0, 0:512]
                    ostp = rr(es2, "swost", [64, 8, 512], BF16, 2)
                    for qb in range(8):
                        ost, Bost = ostp.get()
                        for h in range(8):
                            c, hp, kv = h % 4, (h // 4) * 64, h // 4
                            psO, BpsO = P["acc"].get()
                            for nn in range(4):
                                nt = qb * 4 + nn
                                q_rhs = QT[hp:hp + 64, c, nt * 128:(nt + 1) * 128]
                                loc = []
                                for j, mi in ((nt - 1, 0), (nt, None), (nt + 1, 1)):
                                    if j < 0 or j > 31:
                                        continue

                                    def sfn(o, j=j, mi=mi):
                                        return [mm(o, KT[hp:hp + 64, j * 128:(j + 1) * 128], q_rhs, True, True)]
                                    loc.append((sfn, [Bk[j // 4], Bq[qb]], (0, 128), Vp[:, j, kv, :], [Bv[j], Bvo],
                                                None if mi is None else (msk[:, mi, :], [Bcs])))
                                cg = []
                                for j in (32, 33):
                                    def sfn(o, j=j):
                                        return [mm(o, KT[hp:hp + 64, j * 128:(j + 1) * 128], q_rhs, True, True)]
                                    cg.append((sfn, [Bk[8], Bq[qb]], (0, 128), Vp[:, j, kv, :], [Bv[j], Bvo]))
                                attend_groups(P, psO, BpsO, nn * 128, 128, [loc, cg], True)
                            normalize(P, psO, BpsO, 512, ost[0:64, h, :], Bost, sink_ap=sinke[64:65, h:h + 1])
                        store_o(l, 1, ost, Bost, qb * 512, 512)
                    if need_ctx:
                        ost, Bost = ostp.get()
                        for h in range(8):
                            c, hp, kv = h % 4, (h // 4) * 64, h // 4
                            psO, BpsO = P["acc"].get()
                            q_rhs = QT[hp:hp + 64, c, SEQ:T]
                            groups = []
                            for j in (32, 33):
                                def sfn(o, j=j):
                                    return [mm(o, KT[hp:hp + 64, j * 128:(j + 1) * 128], q_rhs, True, True)]
                                groups.append([(sfn, [Bk[8], Bq[8]], (0, 128), Vp[:, j, kv, :], [Bv[j], Bvo])])
                            attend_groups(P, psO, BpsO, 0, 256, groups, True)
                            normalize(P, psO, BpsO, 256, ost[0:64, h, 0:256], Bost, sink_ap=sinke[64:65, h:h + 1])
                        store_o(l, 1, ost, Bost, SEQ, 256)
                    S.barrier()

        def phase_mla(l, need_ctx):
            with ExitStack() as es:
                Wt = sb(es, "mlW", [128, 8, 672], BF16)
                Wq = sb(es, "mlWq", [128, 3, 768], BF16)
                Wkv = sb(es, "mlWkv", [128, 2, 1024], BF16)
                Wkp = sb(es, "mlWkp", [128, 8, 2, 96], BF16)
                Bw = Buf()
                load_w(Wt[:, :, 0:384], w_in_d[l, :, C_ML:C_ML + 384], Bw)
                load_w(Wt[:, :, 384:672], w_in_d[l, :, C_ML + 384:C_ML + 672], Bw)
                load_w(Wq[:], w_uq_d[l], Bw)
                load_w(Wkv[:], w_ukv_d[l], Bw)
                S.op("pool", lambda e: e.memset(Wkp[:], 0.0), writes=[Bw])
                for h in range(8):
                    S.op("pool", lambda e: e.tensor_copy(out=Wkp[:, h, :, 0:64], in_=Wkv[:, :, h * 128:h * 128 + 64]),
                         reads=[Bw], writes=[Bw])
                for hh in range(2):
                    with ExitStack() as esh:
                        QT = sb(esh, "mlQ", [96, 4, T], BF16)
                        KT = sb(esh, "mlK", [96, 4, T], BF16)
                        Vp = sb(esh, "mlV", [128, NT, 4, 65], BF16)
                        Bq = [Buf() for _ in range(9)]
                        Bk = [Buf() for _ in range(9)]
                        Bv = [Buf() for _ in range(NT)]
                        Bvo = Buf()
                        S.op("pool", lambda e: e.memset(Vp[:, :, :, 64:65], 1.0), writes=[Bvo])
                        with ExitStack() as es2:
                            P = proj_pools(es2)
                            rawp = rr(es2, "mlraw", [128, 3, 512], F32, 1)
                            sqp = rr(es2, "mlsq", [128, 3, 512], BF16, 1)
                            rawkp = rr(es2, "mlrawk", [128, 2, 512], F32, 1)
                            sqkp = rr(es2, "mlsqk", [128, 2, 512], BF16, 1)
                            cqp = rr(es2, "mlcq", [128, 3, 512], BF16, 2)
                            ckp = rr(es2, "mlck", [128, 2, 512], BF16, 2)
                            krp = rr(es2, "mlkr", [32, 512], BF16, 2)
                            hbp = rr(es2, "mlhb", [128, 8, 512], BF16, 2)
                            csp = rr(es2, "mlcs", [128, 2, 512], F32, 2)
                            for tb in range(9):
                                a = tb * 512
                                n = 512 if tb < 8 else 256
                                hT, Bh = hload(hbp, a, n)
                                hb = [Bh]
                                do_q = (tb < 8) or need_ctx
                                rope = None
                                if tb < 8:
                                    cs, Bcsb = csp.get()
                                    S.dma(cs[:], cs_ml_d[:, :, a:a + n].rearrange("m p n -> p m n"), writes=[Bcsb])
                                    rope = (perm_ml, cs[0:96, 0, 0:n], cs[0:96, 1, 0:n], Bcsb)

                                def ranknorm_g(ncx, col0, gcol0, dst, Bdst, inv_r, raw, Braw, sq, Bsq, hT=hT, n=n, hb=hb):
                                    for c in range(ncx):
                                        ps, Bps = P["ps1"].get()
                                        cc = col0 + c * 128
                                        S.pe_group([mm(ps[:, 0:n], Wt[:, k, cc:cc + 128], hT[:, k, 0:n], k == 0, k == 7)
                                                    for k in range(8)], reads=[Bw] + hb, writes=[Bps])
                                        S.op("act", lambda e: e.activation(out=raw[:, c, 0:n], in_=ps[:, 0:n], func=AF.Identity),
                                             reads=[Bps], writes=[Braw])
                                        S.op("act", lambda e: e.activation(out=sq[:, c, 0:n], in_=ps[:, 0:n], func=AF.Square),
                                             reads=[Bps], writes=[Bsq])
                                        yield
                                    p2, Bp2 = P["ps2"].get()
                                    S.pe_group([mm(p2[:, 0:n], allones, sq[:, c, 0:n], c == 0, c == ncx - 1) for c in range(ncx)],
                                               reads=[Bsq, B_cm], writes=[Bp2])
                                    yield
                                    rt, Brt = P["rt"].get()
                                    S.op("act", lambda e: e.activation(out=rt[:, 0:n], in_=p2[:, 0:n], func=AF.Ln,
                                                                       scale=inv_r, bias=epst[:, 0:1]),
                                         reads=[Bp2, B_const], writes=[Brt])
                                    yield
                                    S.op("act", lambda e: e.activation(out=rt[:, 0:n], in_=rt[:, 0:n], func=AF.Exp, scale=-0.5),
                                         reads=[Brt], writes=[Brt])
                                    for c in range(ncx):
                                        S.op("dve", lambda e: e.scalar_tensor_tensor(
                                            out=dst[:, c, 0:n], in0=raw[:, c, 0:n], scalar=svec[:, gcol0 + c:gcol0 + c + 1],
                                            in1=rt[:, 0:n], op0=ALU.mult, op1=ALU.mult),
                                            reads=[Braw, Brt, B_sv], writes=[Bdst])

                                def kr_g(kr, Bkr, hT=hT, n=n, hb=hb):
                                    ps, Bps = P["ps1"].get()
                                    S.pe_group([mm(ps[0:32, 0:n], Wt[:, k, 640:672], hT[:, k, 0:n], k == 0, k == 7)
                                                for k in range(8)], reads=[Bw] + hb, writes=[Bps])
                                    S.op("act", lambda e: e.activation(out=kr[0:32, 0:n], in_=ps[0:32, 0:n], func=AF.Identity),
                                         reads=[Bps], writes=[Bkr])
                                    yield

                                tasks = []
                                if do_q:
                                    cq, Bcq = cqp.get()
                                    raw, Braw = rawp.get()
                                    sq, Bsq = sqp.get()
                                    tasks.append(ranknorm_g(3, 0, 4, cq, Bcq, 1.0 / 384, raw, Braw, sq, Bsq))
                                ck, Bck = ckp.get()
                                raw2, Braw2 = rawkp.get()
                                sq2, Bsq2 = sqkp.get()
                                tasks.append(ranknorm_g(2, 384, 7, ck, Bck, 1.0 / 256, raw2, Braw2, sq2, Bsq2))
                                kr, Bkr = krp.get()
                                tasks.append(kr_g(kr, Bkr))
                                run_tasks(tasks)
                                tasks = []
                                for hl in range(4):
                                    h = hh * 4 + hl
                                    if do_q:
                                        mmf = (lambda ps, h=h, cq=cq, n=n: [mm(ps[0:96, 0:n], Wq[:, c, h * 96:(h + 1) * 96], cq[:, c, 0:n],
                                                                               c == 0, c == 2) for c in range(3)])
                                        tasks.append(qk_task(P, mmf, [Bw, Bcq], 96, n, allones, 1.0 / 96, svq[0:96, 9:10],
                                                             QT[0:96, hl, a:a + n], Bq[tb], rope))
                                    mmf = (lambda ps, h=h, ck=ck, kr=kr, n=n: [
                                        mm(ps[0:96, 0:n], Wkp[:, h, 0, :], ck[:, 0, 0:n], True, False),
                                        mm(ps[0:96, 0:n], Wkp[:, h, 1, :], ck[:, 1, 0:n], False, False),
                                        mm(ps[0:96, 0:n], shiftm[0:32, 0:96], kr[0:32, 0:n], False, True)])
                                    tasks.append(qk_task(P, mmf, [Bw, Bck, Bkr, B_cm], 96, n, allones, 1.0 / 96, svec[0:96, 10:11],
                                                         KT[0:96, hl, a:a + n], Bk[tb], rope))
                                run_tasks(tasks)
                                tasks = []
                                for ti, t in enumerate(range(a // 128, (a + n) // 128)):
                                    def mmf(ps, ti=ti, ck=ck):
                                        fns = []
                                        for hl in range(4):
                                            h = hh * 4 + hl
                                            for c in range(2):
                                                fns.append(mm(ps[:, hl * 64:(hl + 1) * 64], ck[:, c, ti * 128:(ti + 1) * 128],
                                                              Wkv[:, c, h * 128 + 64:h * 128 + 128], c == 0, c == 1))
                                        return fns
                                    tasks.append(v_task(P, mmf, [Bw, Bck], 256, Vp[:, t, :, 0:64], Bv[t]))
                                run_tasks(tasks)
                            S.barrier()
                        with ExitStack() as es2:
                            P = attn_pools(es2, 3)
                            FILL[0] = 0
                            ostp = rr(es2, "mlost", [64, 4, 512], BF16, 2)
                            li = l if debug else 0
                            for qb in range(9):
                                if qb == 8 and not need_ctx:
                                    continue
                                a = qb * 512
                                n = 512 if qb < 8 else 256
                                ost, Bost = ostp.get()
                                for hl in range(4):
                                    psO, BpsO = P["acc"].get()
                                    q_rhs = QT[0:96, hl, a:a + n]
                                    groups = []
                                    for j in (range(NT) if qb < 8 else (32, 33)):
                                        def sfn(o, j=j):
                                            return [mm(o, KT[0:96, hl, j * 128:(j + 1) * 128], q_rhs, True, True)]
                                        groups.append([(sfn, [Bk[j // 4], Bq[qb]], (0, 128), Vp[:, j, hl, :], [Bv[j], Bvo])])
                                    attend_groups(P, psO, BpsO, 0, n, groups, True)
                                    normalize(P, psO, BpsO, n, ost[0:64, hl, 0:n], Bost)
                                flush_norm()
                                dst = o_d[li, 2, hh * 256:(hh + 1) * 256, a:a + n].rearrange("(h d) n -> d h n", d=64)
                                S.dma(dst, ost[0:64, :, 0:n], reads=[Bost], writes=B_o[2][a // 128:(a + n) // 128])
                            S.barrier()

        def phase_merge(l, need_ctx):
            li = l if debug else 0
            with ExitStack() as es:
                Wg = sb(es, "mgWg", [128, 8, 3072], BF16)
                Wb = sb(es, "mgWb", [128, 3, 4, D], BF16)
                Wo = sb(es, "mgWo", [128, 8, D], BF16)
                BWg = [Buf() for _ in range(12)]
                BWb = [Buf() for _ in range(3)]
                BWo = Buf()

                def ldg(j):
                    load_w(Wg[:, :, j * 256:(j + 1) * 256], w_in_d[l, :, C_G + j * 256:C_G + (j + 1) * 256], BWg[j])
                for j in (0, 4, 8):
                    ldg(j)
                for nbr in range(3):
                    load_w(Wb[:, nbr, :, :], w_br_d[l, nbr], BWb[nbr])
                for j in (1, 5, 9, 2, 6, 10, 3, 7, 11):
                    ldg(j)
                load_w(Wo[:], w_out_d[l], BWo)
                P = norm_pools(es)
                GA = sb(es, "mgGA", [128, D], F32)
                SHf = sb(es, "mgSHf", [128, D], F32)
                Gf = sb(es, "mgGf", [128, D], F32)
                gt = sb(es, "mgg", [128, D], F32)
                Bg = Buf()
                S.dma(gt[:], bcast_ap(g_ffn_h, l * D, D), writes=[Bg])
                oTp = [rr(es, f"mgo{i}", [128, 4, 512], BF16, 2) for i in range(3)]
                mT = sb(es, "mgmT", [128, 8, 512], BF16)
                BmT = Buf()
                macc = sb(es, "mgacc", [128, 512], F32)
                Bmacc = Buf()
                sgp = rr(es, "mgsg", [128, 512], F32, 2)
                tmp = rr(es, "mgtmp", [128, 512], F32, 2)
                xin = rr(es, "mgx", [128, D], F32, 2)
                x1p = rr(es, "mgx1", [128, D], F32, 2)
                pg = RR(psb[0:2])
                py = RR(psb[2:4])
                po = RR(psb[4:7])
                hbp = rr(es, "mghb", [128, 8, 512], BF16, 2)
                hstp = rr(es, "mghst", [128, 8, 512], BF16, 1)
                nblk = 9 if need_ctx else 8

                def mg_loads(tb):
                    a = tb * 512
                    n = 512 if tb < 8 else 256
                    hT, Bh = hload(hbp, a, n)
                    oTs = []
                    for nbr in range(3):
                        ot, Bot = oTp[nbr].get()
                        S.dma(ot[:, :, 0:n], o_d[li, nbr, :, a:a + n].rearrange("(c p) n -> p c n", p=128),
                              reads=B_o[nbr][a // 128:(a + n) // 128], writes=[Bot])
                        oTs.append((ot, Bot))
                    return hT, Bh, oTs
                nxt = mg_loads(0)
                for tb in range(nblk):
                    a = tb * 512
                    n = 512 if tb < 8 else 256
                    row = 0 if tb < 8 else 1
                    hT, Bh, oTs = nxt
                    if tb + 1 < nblk:
                        nxt = mg_loads(tb + 1)
                    oT = [x[0] for x in oTs]
                    Bo = [x[1] for x in oTs]
                    if tb == 0 or tb == 8:
                        load_mod_bcast(GA, Bg, l, row, 2)
                        load_mod_bcast(SHf, Bg, l, row, 3)
                        load_mod_bcast(Gf, Bg, l, row, 4)
                        S.op("dve", lambda e: e.scalar_tensor_tensor(out=Gf[:], in0=Gf[:], scalar=1.0, in1=gt[:],
                                                                     op0=ALU.add, op1=ALU.mult), reads=[Bg], writes=[Bg])
                    hb = [Bh]
                    hst, Bhst = hstp.get()
                    for f in range(8):
                        for nbr in range(3):
                            psg, Bpg = pg.get()
                            col = nbr * D + f * 128
                            S.pe_group([mm(psg[:, 0:n], Wg[:, k, col:col + 128], hT[:, k, 0:n], k == 0, k == 7)
                                        for k in range(8)], reads=[BWg[col // 256]] + hb, writes=[Bpg])
                            psy, Bpy = py.get()
                            S.pe_group([mm(psy[:, 0:n], Wb[:, nbr, c, f * 128:(f + 1) * 128], oT[nbr][:, c, 0:n], c == 0, c == 3)
                                        for c in range(4)], reads=[BWb[nbr], Bo[nbr]], writes=[Bpy])
                            sg, Bsg = sgp.get()
                            S.op("act", lambda e: e.activation(out=sg[:, 0:n], in_=psg[:, 0:n], func=AF.Sigmoid),
                                 reads=[Bpg], writes=[Bsg])
                            if nbr == 0:
                                S.op("dve", lambda e: e.tensor_tensor(out=macc[:, 0:n], in0=psy[:, 0:n], in1=sg[:, 0:n], op=ALU.mult),
                                     reads=[Bpy, Bsg], writes=[Bmacc])
                            else:
                                tm, Btm = tmp.get()
                                S.op("dve", lambda e: e.tensor_tensor(out=tm[:, 0:n], in0=psy[:, 0:n], in1=sg[:, 0:n], op=ALU.mult),
                                     reads=[Bpy, Bsg], writes=[Btm])
                                if nbr == 1:
                                    S.op("pool", lambda e: e.tensor_tensor(out=macc[:, 0:n], in0=macc[:, 0:n], in1=tm[:, 0:n], op=ALU.add),
                                         reads=[Btm, Bmacc], writes=[Bmacc])
                                else:
                                    S.op("pool", lambda e: e.tensor_tensor(out=mT[:, f, 0:n], in0=macc[:, 0:n], in1=tm[:, 0:n], op=ALU.add),
                                         reads=[Btm, Bmacc], writes=[BmT])
                    for tt in range(n // 128):
                        t = a // 128 + tt
                        xt, Bx = xin.get()
                        if l == 0:
                            src = x_d[t * 128:(t + 1) * 128, :] if t < 32 else ctx_d[(t - 32) * 128:(t - 31) * 128, :]
                            S.dma(xt[:], src, writes=[Bx])
                        else:
                            S.dma(xt[:], xs_d[t * 128:(t + 1) * 128, :], reads=[B_xs[t]], writes=[Bx])
                        x1, Bx1 = x1p.get()
                        for half in range(2):
                            pso, Bpo = po.get()
                            S.pe_group([mm(pso[:, :], mT[:, k, tt * 128:(tt + 1) * 128], Wo[:, k, half * 512:(half + 1) * 512],
                                           k == 0, k == 7) for k in range(8)], reads=[BWo, BmT], writes=[Bpo])
                            tm, Btm = tmp.get()
                            S.op("dve", lambda e: e.tensor_tensor(out=tm[:, :], in0=pso[:, :], in1=GA[:, half * 512:(half + 1) * 512],
                                                                  op=ALU.mult), reads=[Bpo, Bg], writes=[Btm])
                            S.op("pool", lambda e: e.tensor_tensor(out=x1[:, half * 512:(half + 1) * 512], in0=tm[:, :],
                                                                   in1=xt[:, half * 512:(half + 1) * 512], op=ALU.add),
                                 reads=[Btm, Bx], writes=[Bx1])
                        S.dma(xs_d[t * 128:(t + 1) * 128, :], x1[:], reads=[Bx1], writes=[B_xs[t]], q="pool")
                        norm_to_hT(P, x1, Bx1, Gf, SHf, Bg, hst, Bhst, tt)
                    S.dma(hT_d[:, :, a:a + n], hst[:, :, 0:n], reads=[Bhst], writes=hbufs(a, a + n), q="pool")
                S.barrier()

        def phase_ffn(l, need_ctx, last):
            with ExitStack() as es:
                cw = sb(es, "ffcw", [128, 44, 3], F32)
                cb = sb(es, "ffcb", [128, 44], F32)
                GF = sb(es, "ffGF", [128, D], F32)
                Bc = Buf()
                S.dma(cw[:], cw_d[l].rearrange("p (c j) -> p c j", j=3), writes=[Bc])
                S.dma(cb[:], cb_d[l], writes=[Bc])
                Wus = [sb(es, f"ffWu{i}", [128, 8, 2, 1408], BF16) for i in range(2)]
                Wds = [sb(es, f"ffWd{i}", [128, 11, D], BF16) for i in range(2)]
                BWu = [[[Buf() for _ in range(2)] for _ in range(2)] for _ in range(2)]
                BWd = [[Buf() for _ in range(3)] for _ in range(2)]
                for ps_ in range(2):
                    for hf, (j0, j1) in enumerate(((0, 768), (768, 1408))):
                        for gv in range(2):
                            c0 = gv * DFF + ps_ * 1408
                            load_w(Wus[ps_][:, :, gv, j0:j1], w_up_d[l, :, c0 + j0:c0 + j1], BWu[ps_][gv][hf])
                    for ji, j in enumerate(range(0, 11, 4)):
                        je = min(11, j + 4)
                        load_w(Wds[ps_][:, j:je, :], w_dn_d[l, ps_ * 1408 + j * 128:ps_ * 1408 + je * 128, :], BWd[ps_][ji])
                aT = sb(es, "ffaT", [128, 11, 512], BF16)
                BaT = Buf()
                accp = rr(es, "ffacc", [128, 512], F32, 4)
                sgp = rr(es, "ffsg", [128, 512], F32, 2)
                tmp = rr(es, "fftmp", [128, 512], F32, 2)
                xin = rr(es, "ffx", [128, D], F32, 2)
                x1p = rr(es, "ffx1", [128, D], F32, 2)
                pu = RR(psb[0:4])
                po = RR(psb[4:7])
                hbp = rr(es, "ffhb", [128, 8, 512], BF16, 2)
                blocks = [(0, SEQ, i * 510, min(SEQ, (i + 1) * 510)) for i in range(9)]
                if need_ctx:
                    blocks.append((SEQ, T, SEQ, T))
                for ps_ in range(2):
                    Wu, Wd = Wus[ps_], Wds[ps_]
                    cur_row = None
                    def ff_load(bi):
                        s0, s1, a, b = blocks[bi]
                        ua, ub = max(a - 1, s0), min(b + 1, s1)
                        return hload(hbp, ua, ub - ua)
                    nxt = ff_load(0)
                    for bi, (s0, s1, a, b) in enumerate(blocks):
                        row = 0 if s0 == 0 else 1
                        if row != cur_row:
                            load_mod_bcast(GF, Bc, l, row, 5)
                            cur_row = row
                        ua, ub = max(a - 1, s0), min(b + 1, s1)
                        nu = ub - ua
                        n = b - a
                        off = a - ua
                        hT, Bh = nxt
                        if bi + 1 < len(blocks):
                            nxt = ff_load(bi + 1)
                        hb = [Bh]
                        for i in range(11):
                            ci = [ps_ * 11 + i, 22 + ps_ * 11 + i]
                            accs = []
                            for gv in range(2):
                                psu, Bpu = pu.get()
                                S.pe_group([mm(psu[:, 0:nu], Wu[:, k, gv, i * 128:(i + 1) * 128], hT[:, k, 0:nu], k == 0, k == 7)
                                            for k in range(8)], reads=[BWu[ps_][gv][0 if i < 6 else 1]] + hb, writes=[Bpu])
                                acc, Bacc = accp.get()
                                cc = ci[gv]
                                S.op("act", lambda e: e.activation(out=acc[:, 0:n], in_=psu[:, off:off + n], func=AF.Identity,
                                                                   scale=cw[:, cc, 1:2], bias=cb[:, cc:cc + 1]),
                                     reads=[Bpu, Bc], writes=[Bacc])
                                la = max(a, s0 + 1)
                                S.op("dve", lambda e: e.scalar_tensor_tensor(
                                    out=acc[:, la - a:n], in0=psu[:, la - 1 - ua:b - 1 - ua], scalar=cw[:, cc, 0:1],
                                    in1=acc[:, la - a:n], op0=ALU.mult, op1=ALU.add), reads=[Bpu, Bc, Bacc], writes=[Bacc])
                                rb = min(b, s1 - 1)
                                S.op("dve", lambda e: e.scalar_tensor_tensor(
                                    out=acc[:, 0:rb - a], in0=psu[:, a + 1 - ua:rb + 1 - ua], scalar=cw[:, cc, 2:3],
                                    in1=acc[:, 0:rb - a], op0=ALU.mult, op1=ALU.add), reads=[Bpu, Bc, Bacc], writes=[Bacc])
                                accs.append((acc, Bacc))
                            sg, Bsg = sgp.get()
                            S.op("act", lambda e: e.activation(out=sg[:, 0:n], in_=accs[0][0][:, 0:n], func=AF.Silu),
                                 reads=[accs[0][1]], writes=[Bsg])
                            S.op("pool", lambda e: e.tensor_tensor(out=aT[:, i, 0:n], in0=sg[:, 0:n], in1=accs[1][0][:, 0:n],
                                                                   op=ALU.mult), reads=[Bsg, accs[1][1]], writes=[BaT])
                        for m0 in range(0, n, 128):
                            msz = min(128, n - m0)
                            ta = a + m0
                            xt, Bx = xin.get()
                            if l == 0 and ps_ == 0:
                                pass
                            S.dma(xt[0:msz, :], xs_d[ta:ta + msz, :], reads=xbufs(ta, ta + msz), writes=[Bx])
                            x1, Bx1 = x1p.get()
                            for half in range(2):
                                pso, Bpo = po.get()
                                S.pe_group([mm(pso[0:msz, :], aT[:, i, m0:m0 + msz], Wd[:, i, half * 512:(half + 1) * 512],
                                               i == 0, i == 10) for i in range(11)], reads=BWd[ps_] + [BaT], writes=[Bpo])
                                tm, Btm = tmp.get()
                                S.op("dve", lambda e: e.tensor_tensor(out=tm[0:msz, :], in0=pso[0:msz, :],
                                                                      in1=GF[0:msz, half * 512:(half + 1) * 512], op=ALU.mult),
                                     reads=[Bpo, Bc], writes=[Btm])
                                S.op("pool", lambda e: e.tensor_tensor(out=x1[0:msz, half * 512:(half + 1) * 512], in0=tm[0:msz, :],
                                                                       in1=xt[0:msz, half * 512:(half + 1) * 512], op=ALU.add),
                                     reads=[Btm, Bx], writes=[Bx1])
                            if last and ps_ == 1:
                                S.dma(out_d[ta:ta + msz, :], x1[0:msz, :], reads=[Bx1], writes=[Buf()], q="pool")
                            else:
                                S.dma(xs_d[ta:ta + msz, :], x1[0:msz, :], reads=[Bx1], writes=xbufs(ta, ta + msz), q="pool")
                S.barrier()

        def dump_hT(l):
            pass

        for l in range(layers):
            need_ctx = l < DEPTH - 1
            phase_ada(l)
            phase_n1(l)
            dump_hT(l)
            if PH["na"]:
                phase_na(l, need_ctx)
            if PH["sw"]:
                phase_sw(l, need_ctx)
            if PH["mla"]:
                phase_mla(l, need_ctx)
            if PH["merge"]:
                phase_merge(l, need_ctx)
            if PH["ffn"]:
                phase_ffn(l, need_ctx, l == DEPTH - 1)
        S.barrier()
    nc._sched_stats = (S.n_ops, S.n_waits, S.nsem)
    return nc


PH = {"na": True, "sw": True, "mla": True, "merge": True, "ffn": True}


def _consts():
    ident = np.eye(128, dtype=np.float32)
    blockones = np.zeros((128, 128), np.float32)
    blockones[0:64, 0:64] = 1
    blockones[64:128, 64:128] = 1
    allones = np.ones((128, 128), np.float32)

    def partner64(d):
        return d + 16 if (d % 32) < 16 else d - 16
    perm_sw = np.zeros((128, 128), np.float32)
    for i in range(128):
        base = (i // 64) * 64
        perm_sw[base + partner64(i % 64), i] = 1
    perm_ml = np.zeros((128, 128), np.float32)
    for i in range(64, 96):
        dd = i - 64
        p = dd + 8 if (dd % 16) < 8 else dd - 8
        perm_ml[64 + p, i] = 1
    shiftm = np.zeros((128, 128), np.float32)
    for k in range(32):
        shiftm[k, 64 + k] = 1
    cmats = np.stack([ident, blockones, allones, perm_sw, perm_ml, shiftm])
    t = np.arange(SEQ)
    row = (t // GRID).astype(np.float32)
    col = (t % GRID).astype(np.float32)
    cs_sw = np.zeros((2, 128, SEQ), np.float32)
    inv16 = (10000.0 ** (-np.arange(16, dtype=np.float32) / 16)).astype(np.float32)
    for p in range(128):
        d = p % 64
        pos = row if d < 32 else col
        i = d % 16
        ang = (pos * inv16[i]).astype(np.float32)
        cs_sw[0, p] = np.cos(ang)
        cs_sw[1, p] = -np.sin(ang) if (d % 32) < 16 else np.sin(ang)
    cs_ml = np.zeros((2, 128, SEQ), np.float32)
    cs_ml[0, :, :] = 1.0
    inv8 = (10000.0 ** (-np.arange(8, dtype=np.float32) / 8)).astype(np.float32)
    for p in range(64, 96):
        dd = p - 64
        pos = row if dd < 16 else col
        i = dd % 8
        ang = (pos * inv8[i]).astype(np.float32)
        cs_ml[0, p] = np.cos(ang)
        cs_ml[1, p] = -np.sin(ang) if (dd % 16) < 8 else np.sin(ang)
    j = np.arange(128)[:, None]
    i = np.arange(128)[None, :]
    msk = np.stack([np.where(j >= i, 0.0, NEG), np.where(j <= i, 0.0, NEG)]).astype(np.float32)
    qc = np.arange(64)[None, :]
    kc = np.arange(64)[:, None]
    c0 = np.clip(qc - 8, 0, 48)
    inwin = (kc >= c0) & (kc < c0 + 16)
    negm = np.where(inwin, 0.0, NEG).astype(np.float32)
    negm2 = np.full((128, 22, 64), NEG, np.float32)
    for t in range(22):
        for half, dr in ((0, 17 - t), (1, 18 - t)):
            if 3 <= dr <= 10:
                negm2[half * 64:(half + 1) * 64, t, :] = negm
    negm2 = negm2.reshape(128, 22 * 64)
    negm = np.concatenate([negm, negm], axis=0)
    return dict(cmats=cmats, cs_sw=cs_sw, cs_ml=cs_ml, msk_sw=msk, negm=negm, negm2=negm2)


def _layouts(inp):
    w_in = np.ascontiguousarray(inp["w_in"]).copy()
    perm = [0, 4, 1, 5, 2, 6, 3, 7]
    swq = w_in[:, :, C_SW:C_SW + 512].reshape(DEPTH, D, 8, 64)[:, :, perm, :].reshape(DEPTH, D, 512)
    w_in[:, :, C_SW:C_SW + 512] = swq
    rpb = inp["na_rpb"]
    kc = np.arange(64)[:, None]
    qc = np.arange(64)[None, :]
    dc = np.clip(kc - qc, -15, 15) + 15
    rpbg = np.zeros((DEPTH, 128, 8, 16, 64), np.float32)
    for s in range(16):
        dr_lo, dr_up = s - 1, s
        if 0 <= dr_lo <= 14:
            rpbg[:, 0:64, :, s, :] = np.transpose(rpb[:, :, dr_lo, :][:, :, dc], (0, 2, 1, 3))
        if 0 <= dr_up <= 14:
            rpbg[:, 64:128, :, s, :] = np.transpose(rpb[:, :, dr_up, :][:, :, dc], (0, 2, 1, 3))
    rpbg = rpbg.reshape(DEPTH, 128, 8 * 16 * 64)
    rpbg2 = np.zeros((DEPTH, 128, 8, 22, 64), np.float32)
    for t in range(22):
        dr_lo, dr_up = 17 - t, 18 - t
        if 0 <= dr_lo <= 14:
            rpbg2[:, 0:64, :, t, :] = np.transpose(rpb[:, :, dr_lo, :][:, :, dc], (0, 2, 1, 3))
        if 0 <= dr_up <= 14:
            rpbg2[:, 64:128, :, t, :] = np.transpose(rpb[:, :, dr_up, :][:, :, dc], (0, 2, 1, 3))
    rpbg2 = rpbg2.reshape(DEPTH, 128, 8 * 22 * 64)
    svec = np.zeros((DEPTH, 128, 16), np.float32)
    svec[:, :, 0] = np.tile(inp["na_q_norm"], (1, 2))
    svec[:, :, 1] = np.tile(inp["na_k_norm"], (1, 2))
    svec[:, :, 2] = np.tile(inp["sw_q_norm"], (1, 2))
    svec[:, :, 3] = np.tile(inp["sw_k_norm"], (1, 2))
    svec[:, :, 4:7] = inp["mla_q_rank_norm"].reshape(DEPTH, 3, 128).transpose(0, 2, 1)
    svec[:, :, 7:9] = inp["mla_kv_rank_norm"].reshape(DEPTH, 2, 128).transpose(0, 2, 1)
    svec[:, 0:96, 9] = inp["mla_q_norm"]
    svec[:, 0:96, 10] = inp["mla_k_norm"]
    cw = inp["conv_w"].reshape(DEPTH, 3, 44, 128).transpose(0, 3, 2, 1).reshape(DEPTH, 128, 44 * 3)
    cb = inp["conv_b"].reshape(DEPTH, 44, 128).transpose(0, 2, 1)
    f = lambda a: np.ascontiguousarray(a, dtype=np.float32)
    return dict(w_in=f(w_in), rpbg=f(rpbg), rpbg2=f(rpbg2), svec=f(svec), cw=f(cw), cb=f(cb))


_CACHE = {}


def _in_maps(inp):
    consts = _consts()
    lay = _layouts(inp)
    f = lambda a: np.ascontiguousarray(a, dtype=np.float32)
    shared = dict(w_ada=f(inp["w_ada"]), b_ada=f(inp["b_ada"]), g_mix=f(inp["g_mix"]), g_ffn=f(inp["g_ffn"]),
                  sw_sink=f(inp["sw_sink"]), w_uq=f(inp["w_uq"]), w_ukv=f(inp["w_ukv"]), w_branch=f(inp["w_branch"]),
                  w_out=f(inp["w_out"]), w_up=f(inp["w_up"]), w_down=f(inp["w_down"]))
    shared.update(lay)
    shared.update(consts)
    maps = []
    for b in range(8):
        m = dict(shared)
        m["x"] = f(inp["x"][b])
        m["ctx"] = f(inp["ctx"][b])
        cc = np.stack([inp["c"][b], inp["c_ctx"]], axis=-1)
        m["cT"] = f(cc.reshape(8, 128, 2).transpose(1, 0, 2))
        maps.append(m)
    return maps


def kernel(**inputs):
    inp = {k: np.asarray(v) for k, v in inputs.items()}
    if "nc" not in _CACHE:
        _CACHE["nc"] = build_program(DEPTH, False)
    nc = _CACHE["nc"]
    maps = _in_maps(inp)
    res = run_bass_kernel_spmd(nc, maps, core_ids=list(range(8)))
    out = np.stack([np.asarray(r["out"], dtype=np.float32) for r in res.results], axis=0)
    return out
```

```python
import numpy as np
from contextlib import ExitStack
import concourse.bass as bass
import concourse.mybir as mybir
from concourse.bass_utils import run_bass_kernel_spmd

F32 = mybir.dt.float32
BF16 = mybir.dt.bfloat16
AF = mybir.ActivationFunctionType
ALU = mybir.AluOpType

D = 1024
SEQ = 4096
CTX = 256
T = SEQ + CTX
NT = T // 128
DEPTH = 2
D_IN = 6048
DFF = 2816
GRID = 64
NEG = -30000.0
EPS = 1e-6
C_NA = 0
C_SW = 1536
C_ML = 2304
C_G = 2976
SEM_EPOCH = 8000

DEBUG = False
LAYERS = DEPTH


class Buf:
    __slots__ = ("name", "w", "r")

    def __init__(self, name=""):
        self.name = name
        self.w = None
        self.r = {}


class Sched:
    def __init__(self, nc, n_dma_sems=24):
        self.nc = nc
        self.eng = {"pe": nc.tensor, "act": nc.scalar, "dve": nc.vector,
                    "pool": nc.gpsimd, "sp": nc.sync}
        self.cur = {}
        self.cnt = {}
        self.nsem = 0
        self.known = {e: {} for e in self.eng}
        self.allsems = []
        for e in self.eng:
            self._new_sem(e)
        self.dma_sems = []
        for i in range(n_dma_sems):
            s = nc.alloc_semaphore(f"dq{i}")
            self.dma_sems.append([s, 0])
        self.dma_i = 0
        self.bar_sem = nc.alloc_semaphore("barsem")
        self.bar_n = 0
        self.n_ops = 0
        self.n_waits = 0

    def _new_sem(self, e):
        self.cur[e] = self.nc.alloc_semaphore(f"s_{e}_{self.nsem}")
        self.nsem += 1
        self.cnt[e] = 0

    def _wait(self, e, tk):
        sem, val = tk
        k = id(sem)
        kn = self.known[e]
        if kn.get(k, 0) >= val:
            return
        self.eng[e].wait_ge(sem, val)
        kn[k] = val
        self.n_waits += 1

    def _deps(self, e, reads, writes, skip_same_engine=False):
        deps = {}
        for b in reads:
            tk = b.w
            if tk is not None:
                k = id(tk[0])
                if k not in deps or deps[k][1] < tk[1]:
                    deps[k] = tk
        for b in writes:
            tk = b.w
            if tk is not None:
                k = id(tk[0])
                if k not in deps or deps[k][1] < tk[1]:
                    deps[k] = tk
            for k, tk in b.r.items():
                if k not in deps or deps[k][1] < tk[1]:
                    deps[k] = tk
        for tk in deps.values():
            if skip_same_engine and tk[0] is self.cur[e]:
                continue
            self._wait(e, tk)

    def _commit(self, tk, reads, writes):
        k = id(tk[0])
        for b in reads:
            b.r[k] = tk
        for b in writes:
            b.w = tk
            b.r = {}

    def _ticket(self, e, ins):
        if self.cnt[e] >= SEM_EPOCH:
            self._new_sem(e)
        self.cnt[e] += 1
        tk = (self.cur[e], self.cnt[e])
        ins.then_inc(tk[0], 1)
        return tk

    def op(self, e, fn, reads=(), writes=()):
        self._deps(e, reads, writes, skip_same_engine=(e == "pe"))
        ins = fn(self.eng[e])
        self.n_ops += 1
        tk = self._ticket(e, ins)
        self._commit(tk, reads, writes)
        return tk

    def pe_group(self, fns, reads=(), writes=()):
        self._deps("pe", reads, writes, skip_same_engine=True)
        ins = None
        for fn in fns:
            ins = fn(self.eng["pe"])
        self.n_ops += len(fns)
        tk = self._ticket("pe", ins)
        self._commit(tk, reads, writes)
        return tk

    def dma(self, out, in_, reads=(), writes=(), q="sp"):
        slot = self.dma_sems[self.dma_i % len(self.dma_sems)]
        self.dma_i += 1
        sem, v = slot
        if v > 0:
            self._wait(q, (sem, v))
        self._deps(q, reads, writes)
        ins = self.eng[q].dma_start(out=out, in_=in_)
        slot[1] = v + 16
        tk = (sem, v + 16)
        ins.then_inc(sem, 16)
        self._commit(tk, reads, writes)
        self.n_ops += 1
        return tk

    def barrier(self):
        for sem, v in self.dma_sems:
            if v > 0:
                self._wait("sp", (sem, v))
        for e in self.eng:
            if e != "sp" and self.cnt[e] > 0:
                self._wait("sp", (self.cur[e], self.cnt[e]))
        self.bar_n += 1
        self.eng["sp"].sem_inc(self.bar_sem, 1)
        for e in self.eng:
            if e == "sp":
                continue
            self.eng[e].wait_ge(self.bar_sem, self.bar_n)
            kn = self.known[e]
            for sem, v in self.dma_sems:
                kn[id(sem)] = v
            for f in self.eng:
                kn[id(self.cur[f])] = self.cnt[f]


class RR:
    def __init__(self, tiles):
        self.t = tiles
        self.b = [Buf() for _ in tiles]
        self.i = 0

    def get(self):
        j = self.i % len(self.t)
        self.i += 1
        return self.t[j], self.b[j]


def mm(out, lhsT, rhs, start, stop):
    return lambda e: e.matmul(out, lhsT=lhsT, rhs=rhs, start=start, stop=stop)


def build_program(layers=DEPTH, debug=False):
    nc = bass.Bass("TRN2", target_bir_lowering=False)
    S = Sched(nc)
    okind = "ExternalOutput" if debug else "Internal"

    def din(name, shape, dt=F32):
        return nc.dram_tensor(name, list(shape), dt, kind="ExternalInput")

    x_d = din("x", [SEQ, D]).ap()
    ctx_d = din("ctx", [CTX, D]).ap()
    cT_d = din("cT", [128, 8, 2]).ap()
    w_ada_d = din("w_ada", [DEPTH, D, 6 * D]).ap()
    b_ada_h = din("b_ada", [DEPTH, 6 * D])
    g_mix_h = din("g_mix", [DEPTH, D])
    g_ffn_h = din("g_ffn", [DEPTH, D])
    w_in_d = din("w_in", [DEPTH, D, D_IN]).ap()
    rpbg_d = din("rpbg", [DEPTH, 128, 8 * 16 * 64]).ap()
    rpbg2_d = din("rpbg2", [DEPTH, 128, 8 * 22 * 64]).ap()
    negm2_d = din("negm2", [128, 22 * 64]).ap()
    svec_d = din("svec", [DEPTH, 128, 16]).ap()
    sink_h = din("sw_sink", [DEPTH, 8])
    w_uq_d = din("w_uq", [DEPTH, 384, 768]).ap()
    w_ukv_d = din("w_ukv", [DEPTH, 256, 1024]).ap()
    w_br_d = din("w_branch", [DEPTH, 3, 512, D]).ap()
    w_out_d = din("w_out", [DEPTH, D, D]).ap()
    w_up_d = din("w_up", [DEPTH, D, 2 * DFF]).ap()
    cw_d = din("cw", [DEPTH, 128, 44 * 3]).ap()
    cb_d = din("cb", [DEPTH, 128, 44]).ap()
    w_dn_d = din("w_down", [DEPTH, DFF, D]).ap()
    cm_d = din("cmats", [6, 128, 128]).ap()
    cs_sw_d = din("cs_sw", [2, 128, SEQ]).ap()
    cs_ml_d = din("cs_ml", [2, 128, SEQ]).ap()
    msk_sw_d = din("msk_sw", [2, 128, 128]).ap()
    negm_d = din("negm", [128, 64]).ap()

    out_d = nc.dram_tensor("out", [SEQ, D], F32, kind="ExternalOutput").ap()
    mod_h = nc.dram_tensor("mod_s", [DEPTH, 2, 6 * D], F32, kind=okind)
    mod_d = mod_h.ap()
    xs_d = nc.dram_tensor("xs_s", [T, D], F32, kind=okind).ap()
    nod = DEPTH if debug else 1
    o_d = nc.dram_tensor("o_s", [nod, 3, 512, T], BF16, kind=okind).ap()
    hT_d = nc.dram_tensor("hT_s", [128, 8, T], BF16, kind=okind).ap()
    NRS = 6
    rs_h = nc.dram_tensor("rs_s", [NRS, 512], F32, kind="Internal")
    rs_d = rs_h.ap()
    B_rsd = [Buf() for _ in range(NRS)]
    rs_i = [0]

    B_mod = [Buf() for _ in range(DEPTH)]
    B_xs = [Buf() for _ in range(NT)]
    B_o = [[Buf() for _ in range(NT)] for _ in range(3)]
    B_out = Buf()

    _uid = [0]

    def sb(es, name, shape, dt):
        _uid[0] += 1
        return es.enter_context(nc.sbuf_tensor(f"sb_{name}_{_uid[0]}", list(shape), dt))

    def rr(es, name, shape, dt, n):
        return RR([sb(es, f"{name}{i}", shape, dt) for i in range(n)])

    def bcast_ap(handle, offset, n):
        return bass.AP(tensor=handle, offset=offset, ap=[[0, 128], [1, n]])

    with ExitStack() as top:
        psb = [top.enter_context(nc.psum_tensor(f"psb{i}", [128, 512], F32)) for i in range(7)]
        psT = top.enter_context(nc.psum_tensor("psT", [128, 1024], BF16))
        B_psT = Buf()
        B_hT = [Buf() for _ in range(NT)]
        cm = sb(top, "cm", [128, 6, 128], BF16)
        B_cm = Buf()
        S.dma(cm[:], cm_d.rearrange("m p n -> p m n"), writes=[B_cm], q="pool")
        ident = cm[:, 0, :]
        blockones = cm[:, 1, :]
        allones = cm[:, 2, :]
        perm_sw = cm[:, 3, :]
        perm_ml = cm[:, 4, :]
        shiftm = cm[:, 5, :]
        onesf = sb(top, "onesf", [128, 64], F32)
        epst = sb(top, "epst", [128, 1], F32)
        B_const = Buf()
        S.op("dve", lambda e: e.memset(onesf[:], 1.0), writes=[B_const])
        S.op("dve", lambda e: e.memset(epst[:], EPS), writes=[B_const])
        svec = sb(top, "svec", [128, 16], F32)
        svq = sb(top, "svq", [128, 16], F32)
        B_sv = Buf()

        def hbufs(a, b):
            return B_hT[a // 128:(b + 127) // 128]

        def xbufs(a, b):
            return B_xs[a // 128:(b + 127) // 128]

        def hload(pool, a, n):
            ht, Bht = pool.get()
            S.dma(ht[:, :, 0:n], hT_d[:, :, a:a + n], reads=hbufs(a, a + n), writes=[Bht])
            return ht, Bht

        def hstore(hst, Bhst, a, n):
            S.dma(hT_d[:, :, a:a + n], hst[:, :, 0:n], reads=[Bhst], writes=hbufs(a, a + n))

        def norm_to_hT(es_pools, xt, Bx, G, SH, Bg, hst, Bhst, slot):
            for _ in norm_gen(es_pools, xt, Bx, G, SH, Bg, hst, Bhst, slot):
                pass

        def norm_gen(es_pools, xt, Bx, G, SH, Bg, hst, Bhst, slot, after=None):
            P = es_pools
            st, Bst = P["stat"].get()
            jk, Bjk = P["junk"].get()
            S.op("act", lambda e: e.activation(out=jk[:], in_=xt[:], func=AF.Square, accum_out=st[:, 0:1]),
                 reads=[Bx], writes=[Bjk, Bst])
            S.op("act", lambda e: e.activation(out=st[:, 1:2], in_=st[:, 0:1], func=AF.Sqrt,
                                               scale=1.0 / D, bias=epst[:, 0:1]),
                 reads=[Bst, B_const], writes=[Bst])
            yield
            S.op("dve", lambda e: e.reciprocal(out=st[:, 2:3], in_=st[:, 1:2]), reads=[Bst], writes=[Bst])
            tm, Btm = P["tmpf"].get()
            S.op("dve", lambda e: e.scalar_tensor_tensor(out=tm[:], in0=xt[:], scalar=st[:, 2:3], in1=G[:],
                                                         op0=ALU.mult, op1=ALU.mult),
                 reads=[Bx, Bst, Bg], writes=[Btm])
            hb, Bhb = P["hb"].get()
            S.op("pool", lambda e: e.tensor_tensor(out=hb[:], in0=tm[:], in1=SH[:], op=ALU.add),
                 reads=[Btm, Bg], writes=[Bhb])
            yield
            S.pe_group([(lambda e, k=k: e.transpose(psT[:, k * 128:(k + 1) * 128], hb[:, k * 128:(k + 1) * 128], ident))
                        for k in range(8)], reads=[Bhb, B_cm], writes=[B_psT])
            S.op("act", lambda e: e.activation(out=hst[:, :, slot * 128:(slot + 1) * 128],
                                               in_=psT[:, :].rearrange("p (k n) -> p k n", k=8), func=AF.Identity),
                 reads=[B_psT], writes=[Bhst])
            if after is not None:
                after()

        def load_mod_bcast(tile, Bt, l, row, which):
            S.dma(tile[:], bcast_ap(mod_h, (l * 2 + row) * 6 * D + which * D, D), reads=[B_mod[l]], writes=[Bt])

        def make_G(es, l, row, which_scale, ghandle, name):
            Gt = sb(es, name, [128, D], F32)
            Bg = Buf()
            gt = sb(es, name + "g", [128, D], F32)
            Bgt = Buf()
            load_mod_bcast(Gt, Bg, l, row, which_scale)
            S.dma(gt[:], bcast_ap(ghandle, l * D, D), writes=[Bgt])
            S.op("dve", lambda e: e.scalar_tensor_tensor(out=Gt[:], in0=Gt[:], scalar=1.0, in1=gt[:],
                                                         op0=ALU.add, op1=ALU.mult),
                 reads=[Bg, Bgt], writes=[Bg])
            return Gt, Bg

        def norm_pools(es, depth=2):
            return {"stat": rr(es, "nstat", [128, 4], F32, depth + 1),
                    "junk": rr(es, "njunk", [128, D], BF16, depth),
                    "tmpf": rr(es, "ntmpf", [128, D], F32, depth),
                    "hb": rr(es, "nhb", [128, D], BF16, depth)}

        def load_w(dst, src_rows, Bw):
            S.dma(dst, src_rows.rearrange("(kc p) n -> p kc n", p=128), writes=[Bw], q="pool")

        def phase_ada(l):
            with ExitStack() as es:
                cT = sb(es, "cT", [128, 8, 2], F32)
                Bc = Buf()
                S.dma(cT[:], cT_d, writes=[Bc])
                S.op("act", lambda e: e.activation(out=cT[:], in_=cT[:], func=AF.Silu), reads=[Bc], writes=[Bc])
                bada = sb(es, "bada", [2, 6 * D], F32)
                Bb = Buf()
                S.dma(bada[:], bass.AP(tensor=b_ada_h, offset=l * 6 * D, ap=[[0, 2], [1, 6 * D]]), writes=[Bb])
                modsb = sb(es, "modsb", [2, 6 * D], F32)
                Bm = Buf()
                wa = rr(es, "wa", [128, 8, 512], F32, 4)
                pp = RR(psb[0:2])
                for j in range(12):
                    wt, Bw = wa.get()
                    S.dma(wt[:], w_ada_d[l, :, j * 512:(j + 1) * 512].rearrange("(kc p) n -> p kc n", p=128),
                          writes=[Bw])
                    ps, Bp = pp.get()
                    S.pe_group([mm(ps[0:2, :], cT[:, k, :], wt[:, k, :], k == 0, k == 7) for k in range(8)],
                               reads=[Bc, Bw], writes=[Bp])
                    S.op("dve", lambda e: e.tensor_tensor(out=modsb[:, j * 512:(j + 1) * 512], in0=ps[0:2, :],
                                                          in1=bada[:, j * 512:(j + 1) * 512], op=ALU.add),
                         reads=[Bp, Bb], writes=[Bm])
                S.dma(mod_d[l], modsb[:], reads=[Bm], writes=[B_mod[l]])
                S.dma(svec[:], svec_d[l], writes=[B_sv])
                S.op("dve", lambda e: e.tensor_scalar(out=svq[:, 0:4], in0=svec[:, 0:4], scalar1=0.125, scalar2=None,
                                                      op0=ALU.mult), reads=[B_sv], writes=[B_sv])
                S.op("dve", lambda e: e.tensor_scalar(out=svq[:, 9:10], in0=svec[:, 9:10], scalar1=96.0 ** -0.5,
                                                      scalar2=None, op0=ALU.mult), reads=[B_sv], writes=[B_sv])
                S.barrier()

        def phase_n1(l):
            with ExitStack() as es:
                P = norm_pools(es, 4)
                xin = rr(es, "n1x", [128, D], F32, 6)
                Gl, Bgl = make_G(es, l, 0, 1, g_mix_h, "n1Gl")
                Gc, Bgc = make_G(es, l, 1, 1, g_mix_h, "n1Gc")
                SHl = sb(es, "n1SHl", [128, D], F32)
                SHc = sb(es, "n1SHc", [128, D], F32)
                load_mod_bcast(SHl, Bgl, l, 0, 0)
                load_mod_bcast(SHc, Bgc, l, 1, 0)
                hstp = rr(es, "n1hst", [128, 8, 512], BF16, 3)

                def n1_task(t, hst, Bhst):
                    xt, Bx = xin.get()
                    if l == 0:
                        src = x_d[t * 128:(t + 1) * 128, :] if t < 32 else ctx_d[(t - 32) * 128:(t - 31) * 128, :]
                        S.dma(xt[:], src, writes=[Bx])
                    else:
                        S.dma(xt[:], xs_d[t * 128:(t + 1) * 128, :], reads=[B_xs[t]], writes=[Bx])
                    after = None
                    if t % 4 == 3 or t == NT - 1:
                        a0 = (t // 4) * 512
                        after = (lambda: hstore(hst, Bhst, a0, (t + 1) * 128 - a0))
                    if t < 32:
                        yield from norm_gen(P, xt, Bx, Gl, SHl, Bgl, hst, Bhst, t % 4, after)
                    else:
                        yield from norm_gen(P, xt, Bx, Gc, SHc, Bgc, hst, Bhst, t % 4, after)
                tasks = []
                for t in range(NT):
                    if t % 4 == 0:
                        hst, Bhst = hstp.get()
                    tasks.append(n1_task(t, hst, Bhst))
                run_tasks(tasks)
                S.barrier()

        def run_tasks(gens, depth=3):
            active = []
            it = iter(gens)
            while True:
                while len(active) < depth:
                    g = next(it, None)
                    if g is None:
                        break
                    active.append(g)
                if not active:
                    break
                for g in list(active):
                    try:
                        next(g)
                    except StopIteration:
                        active.remove(g)

        def qk_task(P, mmf, mrd, npart, n, onesm, inv_d, gcol, dst, Bdst, rope=None):
            ps, Bps = P["ps1"].get()
            S.pe_group(mmf(ps), reads=mrd, writes=[Bps])
            sq, Bsq = P["sq"].get()
            S.op("act", lambda e: e.activation(out=sq[0:npart, 0:n], in_=ps[0:npart, 0:n], func=AF.Square),
                 reads=[Bps], writes=[Bsq])
            yield
            p2, Bp2 = P["ps2"].get()
            S.pe_group([mm(p2[0:npart, 0:n], onesm[0:npart, 0:npart], sq[0:npart, 0:n], True, True)],
                       reads=[Bsq, B_cm], writes=[Bp2])
            yield
            rt, Brt = P["rt"].get()
            S.op("act", lambda e: e.activation(out=rt[0:npart, 0:n], in_=p2[0:npart, 0:n], func=AF.Ln,
                                               scale=inv_d, bias=epst[0:npart, 0:1]),
                 reads=[Bp2, B_const], writes=[Brt])
            yield
            S.op("act", lambda e: e.activation(out=rt[0:npart, 0:n], in_=rt[0:npart, 0:n], func=AF.Exp, scale=-0.5),
                 reads=[Brt], writes=[Brt])
            if rope is None:
                S.op("dve", lambda e: e.scalar_tensor_tensor(out=dst, in0=ps[0:npart, 0:n], scalar=gcol,
                                                             in1=rt[0:npart, 0:n], op0=ALU.mult, op1=ALU.mult),
                     reads=[Bps, Brt, B_sv], writes=[Bdst])
                return
            perm, cos_ap, sin_ap, Bcs = rope
            qn, Bqn = P["qn"].get()
            S.op("dve", lambda e: e.scalar_tensor_tensor(out=qn[0:npart, 0:n], in0=ps[0:npart, 0:n], scalar=gcol,
                                                         in1=rt[0:npart, 0:n], op0=ALU.mult, op1=ALU.mult),
                 reads=[Bps, Brt, B_sv], writes=[Bqn])
            yield
            p3, Bp3 = P["ps2"].get()
            S.pe_group([mm(p3[0:npart, 0:n], perm[0:npart, 0:npart], qn[0:npart, 0:n], True, True)],
                       reads=[Bqn, B_cm], writes=[Bp3])
            t1, Bt1 = P["rt"].get()
            S.op("pool", lambda e: e.tensor_tensor(out=t1[0:npart, 0:n], in0=qn[0:npart, 0:n], in1=cos_ap, op=ALU.mult),
                 reads=[Bqn, Bcs], writes=[Bt1])
            yield
            t2, Bt2 = P["rt"].get()
            S.op("dve", lambda e: e.tensor_tensor(out=t2[0:npart, 0:n], in0=p3[0:npart, 0:n], in1=sin_ap, op=ALU.mult),
                 reads=[Bp3, Bcs], writes=[Bt2])
            S.op("pool", lambda e: e.tensor_tensor(out=dst, in0=t1[0:npart, 0:n], in1=t2[0:npart, 0:n], op=ALU.add),
                 reads=[Bt1, Bt2], writes=[Bdst])

        def v_task(P, mmf, mrd, ncol, dst, Bdst):
            ps, Bps = P["ps1"].get()
            S.pe_group(mmf(ps), reads=mrd, writes=[Bps])
            S.op("act", lambda e: e.activation(out=dst, in_=ps[:, 0:ncol].rearrange("p (h d) -> p h d", d=64),
                                               func=AF.Identity), reads=[Bps], writes=[Bdst])
            yield

        def proj_pools(es):
            return {"sq": rr(es, "psq", [128, 512], BF16, 4),
                    "rt": rr(es, "prt", [128, 512], F32, 8),
                    "qn": rr(es, "pqn", [128, 512], BF16, 4),
                    "ps1": RR(psb[0:3]),
                    "ps2": RR(psb[3:7])}

        def attn_pools(es, nsc):
            return {"sc": RR(psb[0:nsc]), "acc": RR(psb[4:7]),
                    "pT": rr(es, "apT", [128, 512], BF16, nsc + 1),
                    "rs": rr(es, "ars", [128, 512], F32, 3),
                    "bcs": rr(es, "abcs", [64, 512], F32, 3)}

        pend_pv = []
        LOOK = 2
        FILL = [0, None]

        def flush_pv(keep=0):
            while len(pend_pv) > keep:
                fns, rds, BpsO_ = pend_pv.pop(0)
                S.pe_group(fns, reads=rds, writes=[BpsO_])

        def attend_groups(P, psO, BpsO, col0, n, groups, first):
            started = not first
            ng = len(groups)
            for gi, grp in enumerate(groups):
                sc, Bsc = P["sc"].get()
                fns = []
                rds = []
                for ti, tl in enumerate(grp):
                    fns += tl[0](sc[:, ti * n:(ti + 1) * n])
                    rds += tl[1]
                S.pe_group(fns, reads=rds, writes=[Bsc])
                pT, BpT = P["pT"].get()
                w = len(grp) * n
                S.op("act", lambda e: e.activation(out=pT[:, 0:w], in_=sc[:, 0:w], func=AF.Exp),
                     reads=[Bsc], writes=[BpT])
                for ti, tl in enumerate(grp):
                    if len(tl) > 5 and tl[5] is not None:
                        b_ap, b_rd = tl[5]
                        S.op("dve", lambda e: e.tensor_tensor(out=pT[:, ti * n:(ti + 1) * n], in0=pT[:, ti * n:(ti + 1) * n],
                                                              in1=b_ap, op=ALU.mult), reads=[BpT] + b_rd, writes=[BpT])
                fns = []
                rds = [BpT]
                for ti, tl in enumerate(grp):
                    (p0, p1), vl, vrd = tl[2], tl[3], tl[4]
                    last = (gi == ng - 1) and (ti == len(grp) - 1)
                    fns.append(mm(psO[0:65, col0:col0 + n], vl, pT[p0:p1, ti * n:(ti + 1) * n], not started, last))
                    started = True
                    rds += vrd
                pend_pv.append((fns, rds, BpsO))
                flush_pv(LOOK)
                for _ in range(FILL[0]):
                    nc.tensor.matmul(psb[3][:, :], lhsT=ident, rhs=FILL[1], start=True, stop=True)

        pend_norm = []

        def flush_norm():
            while pend_norm:
                pend_norm.pop(0)()

        def normalize(P, psO, BpsO, n, dst, Bdst, sink_ap=None):
            flush_pv()
            rs, Brs = P["rs"].get()
            if sink_ap is not None:
                S.op("act", lambda e: e.activation(out=rs[64:65, 0:n], in_=psO[64:65, 0:n], func=AF.Ln, bias=sink_ap),
                     reads=[BpsO, B_sv], writes=[Brs])
            else:
                S.op("act", lambda e: e.activation(out=rs[64:65, 0:n], in_=psO[64:65, 0:n], func=AF.Ln),
                     reads=[BpsO], writes=[Brs])
            S.op("act", lambda e: e.activation(out=rs[64:65, 0:n], in_=rs[64:65, 0:n], func=AF.Exp, scale=-1.0),
                 reads=[Brs], writes=[Brs])
            slot = rs_i[0] % NRS
            rs_i[0] += 1
            S.dma(rs_d[slot:slot + 1, 0:n], rs[64:65, 0:n], reads=[Brs], writes=[B_rsd[slot]])
            bc, Bbc = P["bcs"].get()
            S.dma(bc[0:64, 0:n], bass.AP(tensor=rs_h, offset=slot * 512, ap=[[0, 64], [1, n]]),
                  reads=[B_rsd[slot]], writes=[Bbc])
            flush_norm()
            pend_norm.append(lambda: S.op("dve", lambda e: e.tensor_tensor(out=dst, in0=psO[0:64, 0:n], in1=bc[0:64, 0:n],
                                                                           op=ALU.mult),
                                          reads=[BpsO, Bbc], writes=[Bdst]))

        def close_acc(psO, BpsO, col0, n, vl_dummy=None):
            pass

        def store_o(l, br, ost, Bost, a, n):
            li = l if debug else 0
            flush_norm()
            dst = o_d[li, br, :, a:a + n].rearrange("(h d) n -> d h n", d=64)
            S.dma(dst, ost[0:64, :, 0:n], reads=[Bost], writes=B_o[br][a // 128:(a + n + 127) // 128])

        def phase_na(l, need_ctx):
            with ExitStack() as es:
                Wt = sb(es, "naW", [128, 8, 1536], BF16)
                Bw = Buf()
                for c3 in range(3):
                    load_w(Wt[:, :, c3 * 512:(c3 + 1) * 512], w_in_d[l, :, C_NA + c3 * 512:C_NA + (c3 + 1) * 512], Bw)
                QT = sb(es, "naQ", [128, 4, T], BF16)
                KT = sb(es, "naK", [128, 4, T], BF16)
                Vp = sb(es, "naV", [128, NT, 8, 65], BF16)
                Bq = [Buf() for _ in range(9)]
                Bk = [Buf() for _ in range(9)]
                Bv = [Buf() for _ in range(NT)]
                Bvo = Buf()
                S.op("pool", lambda e: e.memset(Vp[:, :, :, 64:65], 1.0), writes=[Bvo])
                Ct = sb(es, "naC", [128, 8 * 16 * 64], BF16)
                Ct2 = sb(es, "naC2", [128, 8 * 22 * 64], BF16)
                Bc = Buf()
                with ExitStack() as es2:
                    Cg = sb(es2, "naCg", [128, 8 * 16, 64], F32)
                    ng = sb(es2, "naNg", [128, 64], F32)
                    Bcg = Buf()
                    S.dma(Cg[:], rpbg_d[l].rearrange("p (a q) -> p a q", q=64), writes=[Bcg])
                    S.dma(ng[:], negm_d, writes=[Bcg])
                    ngb = bass.AP(tensor=ng, offset=0, ap=[[64, 128], [0, 128], [1, 64]])
                    S.op("dve", lambda e: e.tensor_tensor(out=Ct[:].rearrange("p (a q) -> p a q", q=64), in0=Cg[:],
                                                          in1=ngb, op=ALU.add),
                         reads=[Bcg], writes=[Bc])
                    ng2 = sb(es2, "naNg2", [128, 22 * 64], F32)
                    S.dma(ng2[:], negm2_d, writes=[Bcg])
                    TW = 22 * 64
                    for hh2 in range(2):
                        S.dma(Cg[:, 0:88, :], rpbg2_d[l, :, hh2 * 4 * TW:(hh2 + 1) * 4 * TW].rearrange("p (a q) -> p a q", q=64),
                              reads=[Bcg], writes=[Bcg])
                        ngb2 = bass.AP(tensor=ng2, offset=0, ap=[[TW, 128], [0, 4], [1, TW]])
                        S.op("dve", lambda e: e.tensor_tensor(
                            out=Ct2[:, hh2 * 4 * TW:(hh2 + 1) * 4 * TW].rearrange("p (h q) -> p h q", q=TW),
                            in0=Cg[:, 0:88, :].rearrange("p (h u) q -> p h (u q)", h=4), in1=ngb2, op=ALU.add),
                            reads=[Bcg], writes=[Bc])
                    for hh2 in range(4):
                        S.op("act", lambda e: e.activation(out=Ct2[:, hh2 * 2 * TW:(hh2 + 1) * 2 * TW],
                                                           in_=Ct2[:, hh2 * 2 * TW:(hh2 + 1) * 2 * TW], func=AF.Exp),
                             reads=[Bc], writes=[Bc])
                    S.barrier()
                with ExitStack() as es2:
                    P = proj_pools(es2)
                    hbp = rr(es2, "nahb", [128, 8, 512], BF16, 2)
                    for tb in range(9):
                        a = tb * 512
                        n = 512 if tb < 8 else 256
                        hT, Bh = hload(hbp, a, n)
                        hb = [Bh]
                        tasks = []
                        for which, dstT, Bd, gc in ((0, QT, Bq, svq[:, 0:1]), (1, KT, Bk, svec[:, 1:2])):
                            if which == 0 and tb == 8 and not need_ctx:
                                continue
                            for c in range(4):
                                col = which * 512 + c * 128
                                mmf = (lambda ps, col=col, hT=hT, n=n: [mm(ps[:, 0:n], Wt[:, k, col:col + 128], hT[:, k, 0:n], k == 0, k == 7)
                                                                       for k in range(8)])
                                tasks.append(qk_task(P, mmf, [Bw] + hb, 128, n, blockones, 1.0 / 64, gc,
                                                     dstT[:, c, a:a + n], Bd[tb]))
                        run_tasks(tasks)
                        tasks = []
                        for t in range(a // 128, (a + n) // 128):
                            mmf = (lambda ps, t=t, hT=hT, a=a: [mm(ps[:, :], hT[:, k, t * 128 - a:(t + 1) * 128 - a], Wt[:, k, 1024:1536],
                                                                 k == 0, k == 7) for k in range(8)])
                            tasks.append(v_task(P, mmf, [Bw, Bh], 512, Vp[:, t, :, 0:64], Bv[t]))
                        run_tasks(tasks)
                    S.barrier()
                with ExitStack() as es2:
                    P = attn_pools(es2, 3)
                    FILL[0], FILL[1] = 0, KT[:, 0, 0:512]
                    ostp = rr(es2, "naost", [64, 8, 512], BF16, 2)
                    for qb in range(8):
                        ost, Bost = ostp.get()
                        for h in range(8):
                            c, hp = h // 2, (h % 2) * 64
                            psO, BpsO = P["acc"].get()
                            TW = 22 * 64
                            if 1 <= qb <= 6:
                                q_rhs = QT[hp:hp + 64, c, qb * 512:(qb + 1) * 512]
                                groups = []
                                for j in range(4 * qb - 2, 4 * qb + 6):
                                    t0 = 10 + 8 * qb - 2 * j
                                    assert 0 <= t0 and t0 + 8 <= 22
                                    cb_ap = Ct2[:, h * TW + t0 * 64:h * TW + (t0 + 8) * 64]

                                    def sfn(o, j=j):
                                        return [mm(o, KT[hp:hp + 64, c, j * 128:(j + 1) * 128], q_rhs, True, True)]
                                    groups.append([(sfn, [Bk[j // 4], Bq[qb]], (0, 128), Vp[:, j, h, :], [Bv[j], Bvo], (cb_ap, [Bc]))])
                                for j in (32, 33):
                                    def sfn(o, j=j):
                                        return [mm(o, KT[hp:hp + 64, c, j * 128:(j + 1) * 128], q_rhs, True, True)]
                                    groups.append([(sfn, [Bk[8], Bq[qb]], (0, 128), Vp[:, j, h, :], [Bv[j], Bvo], None)])
                                attend_groups(P, psO, BpsO, 0, 512, groups, True)
                                normalize(P, psO, BpsO, 512, ost[0:64, h, :], Bost)
                                continue
                            for gg in range(2):
                                g = qb * 2 + gg
                                if 1 <= g <= 14:
                                    qa = g * 256
                                    q_rhs = QT[hp:hp + 64, c, qa:qa + 256]
                                    tiles = []
                                    for j in range(2 * g - 2, 2 * g + 4):
                                        t0 = 10 + 4 * g - 2 * j
                                        assert 0 <= t0 and t0 + 4 <= 22
                                        cb_ap = Ct2[:, h * TW + t0 * 64:h * TW + (t0 + 4) * 64]

                                        def sfn(o, j=j, q_rhs=q_rhs):
                                            return [mm(o, KT[hp:hp + 64, c, j * 128:(j + 1) * 128], q_rhs, True, True)]
                                        tiles.append((sfn, [Bk[j // 4], Bq[qb]], (0, 128), Vp[:, j, h, :], [Bv[j], Bvo], (cb_ap, [Bc])))
                                    for j in (32, 33):
                                        def sfn(o, j=j, q_rhs=q_rhs):
                                            return [mm(o, KT[hp:hp + 64, c, j * 128:(j + 1) * 128], q_rhs, True, True)]
                                        tiles.append((sfn, [Bk[8], Bq[qb]], (0, 128), Vp[:, j, h, :], [Bv[j], Bvo], None))
                                    attend_groups(P, psO, BpsO, gg * 256, 256, [tiles[0:2], tiles[2:4], tiles[4:6], tiles[6:8]], True)
                                    continue
                                for rr_ in range(4):
                                    r = g * 4 + rr_
                                    r0 = min(max(r - 4, 0), 56)
                                    grp = []
                                    qa = r * 64
                                    q_rhs = QT[hp:hp + 64, c, qa:qa + 64]
                                    for j in range(r0 // 2, (r0 + 7) // 2 + 1):
                                        lo_ok = r0 <= 2 * j < r0 + 8
                                        up_ok = r0 <= 2 * j + 1 < r0 + 8
                                        rows = (0 if lo_ok else 64, 128 if up_ok else 64)
                                        s_ = 2 * j - r + 8
                                        assert 0 <= s_ <= 15
                                        cb_ap = Ct[:, (h * 16 + s_) * 64:(h * 16 + s_ + 1) * 64]

                                        def sfn(o, j=j, cb_ap=cb_ap, q_rhs=q_rhs):
                                            return [mm(o, KT[hp:hp + 64, c, j * 128:(j + 1) * 128], q_rhs, True, False),
                                                    mm(o, ident, cb_ap, False, True)]
                                        grp.append((sfn, [Bk[j // 4], Bq[qb], Bc, B_cm], rows,
                                                    Vp[rows[0]:rows[1], j, h, :], [Bv[j], Bvo]))
                                    for j in (32, 33):
                                        def sfn(o, j=j, q_rhs=q_rhs):
                                            return [mm(o, KT[hp:hp + 64, c, j * 128:(j + 1) * 128], q_rhs, True, True)]
                                        grp.append((sfn, [Bk[8], Bq[qb]], (0, 128), Vp[:, j, h, :], [Bv[j], Bvo]))
                                    attend_groups(P, psO, BpsO, gg * 256 + rr_ * 64, 64, [grp], True)
                            normalize(P, psO, BpsO, 512, ost[0:64, h, :], Bost)
                        store_o(l, 0, ost, Bost, qb * 512, 512)
                    if need_ctx:
                        ost, Bost = ostp.get()
                        for h in range(8):
                            c, hp = h // 2, (h % 2) * 64
                            psO, BpsO = P["acc"].get()
                            q_rhs = QT[hp:hp + 64, c, SEQ:T]
                            groups = []
                            for j in (32, 33):
                                def sfn(o, j=j):
                                    return [mm(o, KT[hp:hp + 64, c, j * 128:(j + 1) * 128], q_rhs, True, True)]
                                groups.append([(sfn, [Bk[8], Bq[8]], (0, 128), Vp[:, j, h, :], [Bv[j], Bvo])])
                            attend_groups(P, psO, BpsO, 0, 256, groups, True)
                            normalize(P, psO, BpsO, 256, ost[0:64, h, 0:256], Bost)
                        store_o(l, 0, ost, Bost, SEQ, 256)
                    S.barrier()

        def phase_sw(l, need_ctx):
            with ExitStack() as es:
                Wt = sb(es, "swW", [128, 8, 768], BF16)
                Bw = Buf()
                load_w(Wt[:, :, 0:512], w_in_d[l, :, C_SW:C_SW + 512], Bw)
                load_w(Wt[:, :, 512:768], w_in_d[l, :, C_SW + 512:C_SW + 768], Bw)
                QT = sb(es, "swQ", [128, 4, T], BF16)
                KT = sb(es, "swK", [128, T], BF16)
                Vp = sb(es, "swV", [128, NT, 2, 65], BF16)
                msk = sb(es, "swmsk", [128, 2, 128], BF16)
                sinke = sb(es, "swsink", [128, 8], F32)
                Bcs = Buf()
                S.dma(msk[:], msk_sw_d.rearrange("m p n -> p m n"), writes=[Bcs], q="pool")
                S.op("act", lambda e: e.activation(out=msk[:], in_=msk[:], func=AF.Exp), reads=[Bcs], writes=[Bcs])
                S.dma(sinke[:], bcast_ap(sink_h, l * 8, 8), writes=[Bcs])
                S.op("act", lambda e: e.activation(out=sinke[:], in_=sinke[:], func=AF.Exp), reads=[Bcs], writes=[Bcs])
                Bq = [Buf() for _ in range(9)]
                Bk = [Buf() for _ in range(9)]
                Bv = [Buf() for _ in range(NT)]
                Bvo = Buf()
                S.op("pool", lambda e: e.memset(Vp[:, :, :, 64:65], 1.0), writes=[Bvo])
                with ExitStack() as es2:
                    P = proj_pools(es2)
                    hbp = rr(es2, "swhb", [128, 8, 512], BF16, 2)
                    csp = rr(es2, "swcs", [128, 2, 512], F32, 2)
                    for tb in range(9):
                        a = tb * 512
                        n = 512 if tb < 8 else 256
                        hT, Bh = hload(hbp, a, n)
                        hb = [Bh]
                        rope = None
                        if tb < 8:
                            cs, Bcsb = csp.get()
                            S.dma(cs[:], cs_sw_d[:, :, a:a + n].rearrange("m p n -> p m n"), writes=[Bcsb])
                            rope = (perm_sw, cs[:, 0, 0:n], cs[:, 1, 0:n], Bcsb)
                        tasks = []
                        for c in range(4):
                            if tb == 8 and not need_ctx:
                                continue
                            mmf = (lambda ps, c=c, hT=hT, n=n: [mm(ps[:, 0:n], Wt[:, k, c * 128:(c + 1) * 128], hT[:, k, 0:n], k == 0, k == 7)
                                                                for k in range(8)])
                            tasks.append(qk_task(P, mmf, [Bw] + hb, 128, n, blockones, 1.0 / 64, svq[:, 2:3],
                                                 QT[:, c, a:a + n], Bq[tb], rope))
                        mmf = (lambda ps, hT=hT, n=n: [mm(ps[:, 0:n], Wt[:, k, 512:640], hT[:, k, 0:n], k == 0, k == 7) for k in range(8)])
                        tasks.append(qk_task(P, mmf, [Bw] + hb, 128, n, blockones, 1.0 / 64, svec[:, 3:4], KT[:, a:a + n], Bk[tb], rope))
                        run_tasks(tasks)
                        tasks = []
                        for t in range(a // 128, (a + n) // 128):
                            mmf = (lambda ps, t=t, hT=hT, a=a: [mm(ps[:, 0:128], hT[:, k, t * 128 - a:(t + 1) * 128 - a], Wt[:, k, 640:768],
                                                                 k == 0, k == 7) for k in range(8)])
                            tasks.append(v_task(P, mmf, [Bw, Bh], 128, Vp[:, t, :, 0:64], Bv[t]))
                        run_tasks(tasks)
                    S.barrier()
                with ExitStack() as es2:
                    P = attn_pools(es2, 3)
                    FILL[0], FILL[1] = 0, QT[:, 0, 0:512]
                    ostp = rr(es2, "swost", [64, 8, 512], BF16, 2)
                    for qb in range(8):
                        ost, Bost = ostp.get()
                        for h in range(8):
                            c, hp, kv = h % 4, (h // 4) * 64, h // 4
                            psO, BpsO = P["acc"].get()
                            for nn in range(4):
                                nt = qb * 4 + nn
                                q_rhs = QT[hp:hp + 64, c, nt * 128:(nt + 1) * 128]
                                loc = []
                                for j, mi in ((nt - 1, 0), (nt, None), (nt + 1, 1)):
                                    if j < 0 or j > 31:
                                        continue

                                    def sfn(o, j=j, mi=mi):
                                        return [mm(o, KT[hp:hp + 64, j * 128:(j + 1) * 128], q_rhs, True, True)]
                                    loc.append((sfn, [Bk[j // 4], Bq[qb]], (0, 128), Vp[:, j, kv, :], [Bv[j], Bvo],
                                                None if mi is None else (msk[:, mi, :], [Bcs])))
                                cg = []
                                for j in (32, 33):
                                    def sfn(o, j=j):
                                        return [mm(o, KT[hp:hp + 64, j * 128:(j + 1) * 128], q_rhs, True, True)]
                                    cg.append((sfn, [Bk[8], Bq[qb]], (0, 128), Vp[:, j, kv, :], [Bv[j], Bvo]))
                                attend_groups(P, psO, BpsO, nn * 128, 128, [loc, cg], True)
                            normalize(P, psO, BpsO, 512, ost[0:64, h, :], Bost, sink_ap=sinke[64:65, h:h + 1])
                        store_o(l, 1, ost, Bost, qb * 512, 512)
                    if need_ctx:
                        ost, Bost = ostp.get()
                        for h in range(8):
                            c, hp, kv = h % 4, (h // 4) * 64, h // 4
                            psO, BpsO = P["acc"].get()
                            q_rhs = QT[hp:hp + 64, c, SEQ:T]
                            groups = []
                            for j in (32, 33):
                                def sfn(o, j=j):
                                    return [mm(o, KT[hp:hp + 64, j * 128:(j + 1) * 128], q_rhs, True, True)]
                                groups.append([(sfn, [Bk[8], Bq[8]], (0, 128), Vp[:, j, kv, :], [Bv[j], Bvo])])
                            attend_groups(P, psO, BpsO, 0, 256, groups, True)
                            normalize(P, psO, BpsO, 256, ost[0:64, h, 0:256], Bost, sink_ap=sinke[64:65, h:h + 1])
                        store_o(l, 1, ost, Bost, SEQ, 256)
                    S.barrier()

        def phase_mla(l, need_ctx):
            with ExitStack() as es:
                Wt = sb(es, "mlW", [128, 8, 672], BF16)
                Wq = sb(es, "mlWq", [128, 3, 768], BF16)
                Wkv = sb(es, "mlWkv", [128, 2, 1024], BF16)
                Wkp = sb(es, "mlWkp", [128, 8, 2, 96], BF16)
                Bw = Buf()
                load_w(Wt[:, :, 0:384], w_in_d[l, :, C_ML:C_ML + 384], Bw)
                load_w(Wt[:, :, 384:672], w_in_d[l, :, C_ML + 384:C_ML + 672], Bw)
                load_w(Wq[:], w_uq_d[l], Bw)
                load_w(Wkv[:], w_ukv_d[l], Bw)
                S.op("pool", lambda e: e.memset(Wkp[:], 0.0), writes=[Bw])
                for h in range(8):
                    S.op("pool", lambda e: e.tensor_copy(out=Wkp[:, h, :, 0:64], in_=Wkv[:, :, h * 128:h * 128 + 64]),
                         reads=[Bw], writes=[Bw])
                for hh in range(2):
                    with ExitStack() as esh:
                        QT = sb(esh, "mlQ", [96, 4, T], BF16)
                        KT = sb(esh, "mlK", [96, 4, T], BF16)
                        Vp = sb(esh, "mlV", [128, NT, 4, 65], BF16)
                        Bq = [Buf() for _ in range(9)]
                        Bk = [Buf() for _ in range(9)]
                        Bv = [Buf() for _ in range(NT)]
                        Bvo = Buf()
                        S.op("pool", lambda e: e.memset(Vp[:, :, :, 64:65], 1.0), writes=[Bvo])
                        with ExitStack() as es2:
                            P = proj_pools(es2)
                            rawp = rr(es2, "mlraw", [128, 3, 512], F32, 1)
                            sqp = rr(es2, "mlsq", [128, 3, 512], BF16, 1)
                            rawkp = rr(es2, "mlrawk", [128, 2, 512], F32, 1)
                            sqkp = rr(es2, "mlsqk", [128, 2, 512], BF16, 1)
                            cqp = rr(es2, "mlcq", [128, 3, 512], BF16, 2)
                            ckp = rr(es2, "mlck", [128, 2, 512], BF16, 2)
                            krp = rr(es2, "mlkr", [32, 512], BF16, 2)
                            hbp = rr(es2, "mlhb", [128, 8, 512], BF16, 2)
                            csp = rr(es2, "mlcs", [128, 2, 512], F32, 2)
                            for tb in range(9):
                                a = tb * 512
                                n = 512 if tb < 8 else 256
                                hT, Bh = hload(hbp, a, n)
                                hb = [Bh]
                                do_q = (tb < 8) or need_ctx
                                rope = None
                                if tb < 8:
                                    cs, Bcsb = csp.get()
                                    S.dma(cs[:], cs_ml_d[:, :, a:a + n].rearrange("m p n -> p m n"), writes=[Bcsb])
                                    rope = (perm_ml, cs[0:96, 0, 0:n], cs[0:96, 1, 0:n], Bcsb)

                                def ranknorm_g(ncx, col0, gcol0, dst, Bdst, inv_r, raw, Braw, sq, Bsq, hT=hT, n=n, hb=hb):
                                    for c in range(ncx):
                                        ps, Bps = P["ps1"].get()
                                        cc = col0 + c * 128
                                        S.pe_group([mm(ps[:, 0:n], Wt[:, k, cc:cc + 128], hT[:, k, 0:n], k == 0, k == 7)
                                                    for k in range(8)], reads=[Bw] + hb, writes=[Bps])
                                        S.op("act", lambda e: e.activation(out=raw[:, c, 0:n], in_=ps[:, 0:n], func=AF.Identity),
                                             reads=[Bps], writes=[Braw])
                                        S.op("act", lambda e: e.activation(out=sq[:, c, 0:n], in_=ps[:, 0:n], func=AF.Square),
                                             reads=[Bps], writes=[Bsq])
                                        yield
                                    p2, Bp2 = P["ps2"].get()
                                    S.pe_group([mm(p2[:, 0:n], allones, sq[:, c, 0:n], c == 0, c == ncx - 1) for c in range(ncx)],
                                               reads=[Bsq, B_cm], writes=[Bp2])
                                    yield
                                    rt, Brt = P["rt"].get()
                                    S.op("act", lambda e: e.activation(out=rt[:, 0:n], in_=p2[:, 0:n], func=AF.Ln,
                                                                       scale=inv_r, bias=epst[:, 0:1]),
                                         reads=[Bp2, B_const], writes=[Brt])
                                    yield
                                    S.op("act", lambda e: e.activation(out=rt[:, 0:n], in_=rt[:, 0:n], func=AF.Exp, scale=-0.5),
                                         reads=[Brt], writes=[Brt])
                                    for c in range(ncx):
                                        S.op("dve", lambda e: e.scalar_tensor_tensor(
                                            out=dst[:, c, 0:n], in0=raw[:, c, 0:n], scalar=svec[:, gcol0 + c:gcol0 + c + 1],
                                            in1=rt[:, 0:n], op0=ALU.mult, op1=ALU.mult),
                                            reads=[Braw, Brt, B_sv], writes=[Bdst])

                                def kr_g(kr, Bkr, hT=hT, n=n, hb=hb):
                                    ps, Bps = P["ps1"].get()
                                    S.pe_group([mm(ps[0:32, 0:n], Wt[:, k, 640:672], hT[:, k, 0:n], k == 0, k == 7)
                                                for k in range(8)], reads=[Bw] + hb, writes=[Bps])
                                    S.op("act", lambda e: e.activation(out=kr[0:32, 0:n], in_=ps[0:32, 0:n], func=AF.Identity),
                                         reads=[Bps], writes=[Bkr])
                                    yield

                                tasks = []
                                if do_q:
                                    cq, Bcq = cqp.get()
                                    raw, Braw = rawp.get()
                                    sq, Bsq = sqp.get()
                                    tasks.append(ranknorm_g(3, 0, 4, cq, Bcq, 1.0 / 384, raw, Braw, sq, Bsq))
                                ck, Bck = ckp.get()
                                raw2, Braw2 = rawkp.get()
                                sq2, Bsq2 = sqkp.get()
                                tasks.append(ranknorm_g(2, 384, 7, ck, Bck, 1.0 / 256, raw2, Braw2, sq2, Bsq2))
                                kr, Bkr = krp.get()
                                tasks.append(kr_g(kr, Bkr))
                                run_tasks(tasks)
                                tasks = []
                                for hl in range(4):
                                    h = hh * 4 + hl
                                    if do_q:
                                        mmf = (lambda ps, h=h, cq=cq, n=n: [mm(ps[0:96, 0:n], Wq[:, c, h * 96:(h + 1) * 96], cq[:, c, 0:n],
                                                                               c == 0, c == 2) for c in range(3)])
                                        tasks.append(qk_task(P, mmf, [Bw, Bcq], 96, n, allones, 1.0 / 96, svq[0:96, 9:10],
                                                             QT[0:96, hl, a:a + n], Bq[tb], rope))
                                    mmf = (lambda ps, h=h, ck=ck, kr=kr, n=n: [
                                        mm(ps[0:96, 0:n], Wkp[:, h, 0, :], ck[:, 0, 0:n], True, False),
                                        mm(ps[0:96, 0:n], Wkp[:, h, 1, :], ck[:, 1, 0:n], False, False),
                                        mm(ps[0:96, 0:n], shiftm[0:32, 0:96], kr[0:32, 0:n], False, True)])
                                    tasks.append(qk_task(P, mmf, [Bw, Bck, Bkr, B_cm], 96, n, allones, 1.0 / 96, svec[0:96, 10:11],
                                                         KT[0:96, hl, a:a + n], Bk[tb], rope))
                                run_tasks(tasks)
                                tasks = []
                                for ti, t in enumerate(range(a // 128, (a + n) // 128)):
                                    def mmf(ps, ti=ti, ck=ck):
                                        fns = []
                                        for hl in range(4):
                                            h = hh * 4 + hl
                                            for c in range(2):
                                                fns.append(mm(ps[:, hl * 64:(hl + 1) * 64], ck[:, c, ti * 128:(ti + 1) * 128],
                                                              Wkv[:, c, h * 128 + 64:h * 128 + 128], c == 0, c == 1))
                                        return fns
                                    tasks.append(v_task(P, mmf, [Bw, Bck], 256, Vp[:, t, :, 0:64], Bv[t]))
                                run_tasks(tasks)
                            S.barrier()
                        with ExitStack() as es2:
                            P = attn_pools(es2, 3)
                            FILL[0] = 0
                            ostp = rr(es2, "mlost", [64, 4, 512], BF16, 2)
                            li = l if debug else 0
                            for qb in range(9):
                                if qb == 8 and not need_ctx:
                                    continue
                                a = qb * 512
                                n = 512 if qb < 8 else 256
                                ost, Bost = ostp.get()
                                for hl in range(4):
                                    psO, BpsO = P["acc"].get()
                                    q_rhs = QT[0:96, hl, a:a + n]
                                    groups = []
                                    for j in (range(NT) if qb < 8 else (32, 33)):
                                        def sfn(o, j=j):
                                            return [mm(o, KT[0:96, hl, j * 128:(j + 1) * 128], q_rhs, True, True)]
                                        groups.append([(sfn, [Bk[j // 4], Bq[qb]], (0, 128), Vp[:, j, hl, :], [Bv[j], Bvo])])
                                    attend_groups(P, psO, BpsO, 0, n, groups, True)
                                    normalize(P, psO, BpsO, n, ost[0:64, hl, 0:n], Bost)
                                flush_norm()
                                dst = o_d[li, 2, hh * 256:(hh + 1) * 256, a:a + n].rearrange("(h d) n -> d h n", d=64)
                                S.dma(dst, ost[0:64, :, 0:n], reads=[Bost], writes=B_o[2][a // 128:(a + n) // 128])
                            S.barrier()

        def phase_merge(l, need_ctx):
            li = l if debug else 0
            with ExitStack() as es:
                Wg = sb(es, "mgWg", [128, 8, 3072], BF16)
                Wb = sb(es, "mgWb", [128, 3, 4, D], BF16)
                Wo = sb(es, "mgWo", [128, 8, D], BF16)
                BWg = [Buf() for _ in range(12)]
                BWb = [Buf() for _ in range(3)]
                BWo = Buf()

                def ldg(j):
                    load_w(Wg[:, :, j * 256:(j + 1) * 256], w_in_d[l, :, C_G + j * 256:C_G + (j + 1) * 256], BWg[j])
                for j in (0, 4, 8):
                    ldg(j)
                for nbr in range(3):
                    load_w(Wb[:, nbr, :, :], w_br_d[l, nbr], BWb[nbr])
                for j in (1, 5, 9, 2, 6, 10, 3, 7, 11):
                    ldg(j)
                load_w(Wo[:], w_out_d[l], BWo)
                P = norm_pools(es)
                GA = sb(es, "mgGA", [128, D], F32)
                SHf = sb(es, "mgSHf", [128, D], F32)
                Gf = sb(es, "mgGf", [128, D], F32)
                gt = sb(es, "mgg", [128, D], F32)
                Bg = Buf()
                S.dma(gt[:], bcast_ap(g_ffn_h, l * D, D), writes=[Bg])
                oTp = [rr(es, f"mgo{i}", [128, 4, 512], BF16, 2) for i in range(3)]
                mT = sb(es, "mgmT", [128, 8, 512], BF16)
                BmT = Buf()
                macc = sb(es, "mgacc", [128, 512], F32)
                Bmacc = Buf()
                sgp = rr(es, "mgsg", [128, 512], F32, 2)
                tmp = rr(es, "mgtmp", [128, 512], F32, 2)
                xin = rr(es, "mgx", [128, D], F32, 2)
                x1p = rr(es, "mgx1", [128, D], F32, 2)
                pg = RR(psb[0:2])
                py = RR(psb[2:4])
                po = RR(psb[4:7])
                hbp = rr(es, "mghb", [128, 8, 512], BF16, 2)
                hstp = rr(es, "mghst", [128, 8, 512], BF16, 1)
                nblk = 9 if need_ctx else 8

                def mg_loads(tb):
                    a = tb * 512
                    n = 512 if tb < 8 else 256
                    hT, Bh = hload(hbp, a, n)
                    oTs = []
                    for nbr in range(3):
                        ot, Bot = oTp[nbr].get()
                        S.dma(ot[:, :, 0:n], o_d[li, nbr, :, a:a + n].rearrange("(c p) n -> p c n", p=128),
                              reads=B_o[nbr][a // 128:(a + n) // 128], writes=[Bot])
                        oTs.append((ot, Bot))
                    return hT, Bh, oTs
                nxt = mg_loads(0)
                for tb in range(nblk):
                    a = tb * 512
                    n = 512 if tb < 8 else 256
                    row = 0 if tb < 8 else 1
                    hT, Bh, oTs = nxt
                    if tb + 1 < nblk:
                        nxt = mg_loads(tb + 1)
                    oT = [x[0] for x in oTs]
                    Bo = [x[1] for x in oTs]
                    if tb == 0 or tb == 8:
                        load_mod_bcast(GA, Bg, l, row, 2)
                        load_mod_bcast(SHf, Bg, l, row, 3)
                        load_mod_bcast(Gf, Bg, l, row, 4)
                        S.op("dve", lambda e: e.scalar_tensor_tensor(out=Gf[:], in0=Gf[:], scalar=1.0, in1=gt[:],
                                                                     op0=ALU.add, op1=ALU.mult), reads=[Bg], writes=[Bg])
                    hb = [Bh]
                    hst, Bhst = hstp.get()
                    for f in range(8):
                        for nbr in range(3):
                            psg, Bpg = pg.get()
                            col = nbr * D + f * 128
                            S.pe_group([mm(psg[:, 0:n], Wg[:, k, col:col + 128], hT[:, k, 0:n], k == 0, k == 7)
                                        for k in range(8)], reads=[BWg[col // 256]] + hb, writes=[Bpg])
                            psy, Bpy = py.get()
                            S.pe_group([mm(psy[:, 0:n], Wb[:, nbr, c, f * 128:(f + 1) * 128], oT[nbr][:, c, 0:n], c == 0, c == 3)
                                        for c in range(4)], reads=[BWb[nbr], Bo[nbr]], writes=[Bpy])
                            sg, Bsg = sgp.get()
                            S.op("act", lambda e: e.activation(out=sg[:, 0:n], in_=psg[:, 0:n], func=AF.Sigmoid),
                                 reads=[Bpg], writes=[Bsg])
                            if nbr == 0:
                                S.op("dve", lambda e: e.tensor_tensor(out=macc[:, 0:n], in0=psy[:, 0:n], in1=sg[:, 0:n], op=ALU.mult),
                                     reads=[Bpy, Bsg], writes=[Bmacc])
                            else:
                                tm, Btm = tmp.get()
                                S.op("dve", lambda e: e.tensor_tensor(out=tm[:, 0:n], in0=psy[:, 0:n], in1=sg[:, 0:n], op=ALU.mult),
                                     reads=[Bpy, Bsg], writes=[Btm])
                                if nbr == 1:
                                    S.op("pool", lambda e: e.tensor_tensor(out=macc[:, 0:n], in0=macc[:, 0:n], in1=tm[:, 0:n], op=ALU.add),
                                         reads=[Btm, Bmacc], writes=[Bmacc])
                                else:
                                    S.op("pool", lambda e: e.tensor_tensor(out=mT[:, f, 0:n], in0=macc[:, 0:n], in1=tm[:, 0:n], op=ALU.add),
                                         reads=[Btm, Bmacc], writes=[BmT])
                    for tt in range(n // 128):
                        t = a // 128 + tt
                        xt, Bx = xin.get()
                        if l == 0:
                            src = x_d[t * 128:(t + 1) * 128, :] if t < 32 else ctx_d[(t - 32) * 128:(t - 31) * 128, :]
                            S.dma(xt[:], src, writes=[Bx])
                        else:
                            S.dma(xt[:], xs_d[t * 128:(t + 1) * 128, :], reads=[B_xs[t]], writes=[Bx])
                        x1, Bx1 = x1p.get()
                        for half in range(2):
                            pso, Bpo = po.get()
                            S.pe_group([mm(pso[:, :], mT[:, k, tt * 128:(tt + 1) * 128], Wo[:, k, half * 512:(half + 1) * 512],
                                           k == 0, k == 7) for k in range(8)], reads=[BWo, BmT], writes=[Bpo])
                            tm, Btm = tmp.get()
                            S.op("dve", lambda e: e.tensor_tensor(out=tm[:, :], in0=pso[:, :], in1=GA[:, half * 512:(half + 1) * 512],
                                                                  op=ALU.mult), reads=[Bpo, Bg], writes=[Btm])
                            S.op("pool", lambda e: e.tensor_tensor(out=x1[:, half * 512:(half + 1) * 512], in0=tm[:, :],
                                                                   in1=xt[:, half * 512:(half + 1) * 512], op=ALU.add),
                                 reads=[Btm, Bx], writes=[Bx1])
                        S.dma(xs_d[t * 128:(t + 1) * 128, :], x1[:], reads=[Bx1], writes=[B_xs[t]], q="pool")
                        norm_to_hT(P, x1, Bx1, Gf, SHf, Bg, hst, Bhst, tt)
                    S.dma(hT_d[:, :, a:a + n], hst[:, :, 0:n], reads=[Bhst], writes=hbufs(a, a + n), q="pool")
                S.barrier()

        def phase_ffn(l, need_ctx, last):
            with ExitStack() as es:
                cw = sb(es, "ffcw", [128, 44, 3], F32)
                cb = sb(es, "ffcb", [128, 44], F32)
                GF = sb(es, "ffGF", [128, D], F32)
                Bc = Buf()
                S.dma(cw[:], cw_d[l].rearrange("p (c j) -> p c j", j=3), writes=[Bc])
                S.dma(cb[:], cb_d[l], writes=[Bc])
                Wus = [sb(es, f"ffWu{i}", [128, 8, 2, 1408], BF16) for i in range(2)]
                Wds = [sb(es, f"ffWd{i}", [128, 11, D], BF16) for i in range(2)]
                BWu = [[[Buf() for _ in range(2)] for _ in range(2)] for _ in range(2)]
                BWd = [[Buf() for _ in range(3)] for _ in range(2)]
                for ps_ in range(2):
                    for hf, (j0, j1) in enumerate(((0, 768), (768, 1408))):
                        for gv in range(2):
                            c0 = gv * DFF + ps_ * 1408
                            load_w(Wus[ps_][:, :, gv, j0:j1], w_up_d[l, :, c0 + j0:c0 + j1], BWu[ps_][gv][hf])
                    for ji, j in enumerate(range(0, 11, 4)):
                        je = min(11, j + 4)
                        load_w(Wds[ps_][:, j:je, :], w_dn_d[l, ps_ * 1408 + j * 128:ps_ * 1408 + je * 128, :], BWd[ps_][ji])
                aT = sb(es, "ffaT", [128, 11, 512], BF16)
                BaT = Buf()
                accp = rr(es, "ffacc", [128, 512], F32, 4)
                sgp = rr(es, "ffsg", [128, 512], F32, 2)
                tmp = rr(es, "fftmp", [128, 512], F32, 2)
                xin = rr(es, "ffx", [128, D], F32, 2)
                x1p = rr(es, "ffx1", [128, D], F32, 2)
                pu = RR(psb[0:4])
                po = RR(psb[4:7])
                hbp = rr(es, "ffhb", [128, 8, 512], BF16, 2)
                blocks = [(0, SEQ, i * 510, min(SEQ, (i + 1) * 510)) for i in range(9)]
                if need_ctx:
                    blocks.append((SEQ, T, SEQ, T))
                for ps_ in range(2):
                    Wu, Wd = Wus[ps_], Wds[ps_]
                    cur_row = None
                    def ff_load(bi):
                        s0, s1, a, b = blocks[bi]
                        ua, ub = max(a - 1, s0), min(b + 1, s1)
                        return hload(hbp, ua, ub - ua)
                    nxt = ff_load(0)
                    for bi, (s0, s1, a, b) in enumerate(blocks):
                        row = 0 if s0 == 0 else 1
                        if row != cur_row:
                            load_mod_bcast(GF, Bc, l, row, 5)
                            cur_row = row
                        ua, ub = max(a - 1, s0), min(b + 1, s1)
                        nu = ub - ua
                        n = b - a
                        off = a - ua
                        hT, Bh = nxt
                        if bi + 1 < len(blocks):
                            nxt = ff_load(bi + 1)
                        hb = [Bh]
                        for i in range(11):
                            ci = [ps_ * 11 + i, 22 + ps_ * 11 + i]
                            accs = []
                            for gv in range(2):
                                psu, Bpu = pu.get()
                                S.pe_group([mm(psu[:, 0:nu], Wu[:, k, gv, i * 128:(i + 1) * 128], hT[:, k, 0:nu], k == 0, k == 7)
                                            for k in range(8)], reads=[BWu[ps_][gv][0 if i < 6 else 1]] + hb, writes=[Bpu])
                                acc, Bacc = accp.get()
                                cc = ci[gv]
                                S.op("act", lambda e: e.activation(out=acc[:, 0:n], in_=psu[:, off:off + n], func=AF.Identity,
                                                                   scale=cw[:, cc, 1:2], bias=cb[:, cc:cc + 1]),
                                     reads=[Bpu, Bc], writes=[Bacc])
                                la = max(a, s0 + 1)
                                S.op("dve", lambda e: e.scalar_tensor_tensor(
                                    out=acc[:, la - a:n], in0=psu[:, la - 1 - ua:b - 1 - ua], scalar=cw[:, cc, 0:1],
                                    in1=acc[:, la - a:n], op0=ALU.mult, op1=ALU.add), reads=[Bpu, Bc, Bacc], writes=[Bacc])
                                rb = min(b, s1 - 1)
                                S.op("dve", lambda e: e.scalar_tensor_tensor(
                                    out=acc[:, 0:rb - a], in0=psu[:, a + 1 - ua:rb + 1 - ua], scalar=cw[:, cc, 2:3],
                                    in1=acc[:, 0:rb - a], op0=ALU.mult, op1=ALU.add), reads=[Bpu, Bc, Bacc], writes=[Bacc])
                                accs.append((acc, Bacc))
                            sg, Bsg = sgp.get()
                            S.op("act", lambda e: e.activation(out=sg[:, 0:n], in_=accs[0][0][:, 0:n], func=AF.Silu),
                                 reads=[accs[0][1]], writes=[Bsg])
                            S.op("pool", lambda e: e.tensor_tensor(out=aT[:, i, 0:n], in0=sg[:, 0:n], in1=accs[1][0][:, 0:n],
                                                                   op=ALU.mult), reads=[Bsg, accs[1][1]], writes=[BaT])
                        for m0 in range(0, n, 128):
                            msz = min(128, n - m0)
                            ta = a + m0
                            xt, Bx = xin.get()
                            if l == 0 and ps_ == 0:
                                pass
                            S.dma(xt[0:msz, :], xs_d[ta:ta + msz, :], reads=xbufs(ta, ta + msz), writes=[Bx])
                            x1, Bx1 = x1p.get()
                            for half in range(2):
                                pso, Bpo = po.get()
                                S.pe_group([mm(pso[0:msz, :], aT[:, i, m0:m0 + msz], Wd[:, i, half * 512:(half + 1) * 512],
                                               i == 0, i == 10) for i in range(11)], reads=BWd[ps_] + [BaT], writes=[Bpo])
                                tm, Btm = tmp.get()
                                S.op("dve", lambda e: e.tensor_tensor(out=tm[0:msz, :], in0=pso[0:msz, :],
                                                                      in1=GF[0:msz, half * 512:(half + 1) * 512], op=ALU.mult),
                                     reads=[Bpo, Bc], writes=[Btm])
                                S.op("pool", lambda e: e.tensor_tensor(out=x1[0:msz, half * 512:(half + 1) * 512], in0=tm[0:msz, :],
                                                                       in1=xt[0:msz, half * 512:(half + 1) * 512], op=ALU.add),
                                     reads=[Btm, Bx], writes=[Bx1])
                            if last and ps_ == 1:
                                S.dma(out_d[ta:ta + msz, :], x1[0:msz, :], reads=[Bx1], writes=[Buf()], q="pool")
                            else:
                                S.dma(xs_d[ta:ta + msz, :], x1[0:msz, :], reads=[Bx1], writes=xbufs(ta, ta + msz), q="pool")
                S.barrier()

        def dump_hT(l):
            pass

        for l in range(layers):
            need_ctx = l < DEPTH - 1
            phase_ada(l)
            phase_n1(l)
            dump_hT(l)
            if PH["na"]:
                phase_na(l, need_ctx)
            if PH["sw"]:
                phase_sw(l, need_ctx)
            if PH["mla"]:
                phase_mla(l, need_ctx)
            if PH["merge"]:
                phase_merge(l, need_ctx)
            if PH["ffn"]:
                phase_ffn(l, need_ctx, l == DEPTH - 1)
        S.barrier()
    nc._sched_stats = (S.n_ops, S.n_waits, S.nsem)
    return nc


PH = {"na": True, "sw": True, "mla": True, "merge": True, "ffn": True}


def _consts():
    ident = np.eye(128, dtype=np.float32)
    blockones = np.zeros((128, 128), np.float32)
    blockones[0:64, 0:64] = 1
    blockones[64:128, 64:128] = 1
    allones = np.ones((128, 128), np.float32)

    def partner64(d):
        return d + 16 if (d % 32) < 16 else d - 16
    perm_sw = np.zeros((128, 128), np.float32)
    for i in range(128):
        base = (i // 64) * 64
        perm_sw[base + partner64(i % 64), i] = 1
    perm_ml = np.zeros((128, 128), np.float32)
    for i in range(64, 96):
        dd = i - 64
        p = dd + 8 if (dd % 16) < 8 else dd - 8
        perm_ml[64 + p, i] = 1
    shiftm = np.zeros((128, 128), np.float32)
    for k in range(32):
        shiftm[k, 64 + k] = 1
    cmats = np.stack([ident, blockones, allones, perm_sw, perm_ml, shiftm])
    t = np.arange(SEQ)
    row = (t // GRID).astype(np.float32)
    col = (t % GRID).astype(np.float32)
    cs_sw = np.zeros((2, 128, SEQ), np.float32)
    inv16 = (10000.0 ** (-np.arange(16, dtype=np.float32) / 16)).astype(np.float32)
    for p in range(128):
        d = p % 64
        pos = row if d < 32 else col
        i = d % 16
        ang = (pos * inv16[i]).astype(np.float32)
        cs_sw[0, p] = np.cos(ang)
        cs_sw[1, p] = -np.sin(ang) if (d % 32) < 16 else np.sin(ang)
    cs_ml = np.zeros((2, 128, SEQ), np.float32)
    cs_ml[0, :, :] = 1.0
    inv8 = (10000.0 ** (-np.arange(8, dtype=np.float32) / 8)).astype(np.float32)
    for p in range(64, 96):
        dd = p - 64
        pos = row if dd < 16 else col
        i = dd % 8
        ang = (pos * inv8[i]).astype(np.float32)
        cs_ml[0, p] = np.cos(ang)
        cs_ml[1, p] = -np.sin(ang) if (dd % 16) < 8 else np.sin(ang)
    j = np.arange(128)[:, None]
    i = np.arange(128)[None, :]
    msk = np.stack([np.where(j >= i, 0.0, NEG), np.where(j <= i, 0.0, NEG)]).astype(np.float32)
    qc = np.arange(64)[None, :]
    kc = np.arange(64)[:, None]
    c0 = np.clip(qc - 8, 0, 48)
    inwin = (kc >= c0) & (kc < c0 + 16)
    negm = np.where(inwin, 0.0, NEG).astype(np.float32)
    negm2 = np.full((128, 22, 64), NEG, np.float32)
    for t in range(22):
        for half, dr in ((0, 17 - t), (1, 18 - t)):
            if 3 <= dr <= 10:
                negm2[half * 64:(half + 1) * 64, t, :] = negm
    negm2 = negm2.reshape(128, 22 * 64)
    negm = np.concatenate([negm, negm], axis=0)
    return dict(cmats=cmats, cs_sw=cs_sw, cs_ml=cs_ml, msk_sw=msk, negm=negm, negm2=negm2)


def _layouts(inp):
    w_in = np.ascontiguousarray(inp["w_in"]).copy()
    perm = [0, 4, 1, 5, 2, 6, 3, 7]
    swq = w_in[:, :, C_SW:C_SW + 512].reshape(DEPTH, D, 8, 64)[:, :, perm, :].reshape(DEPTH, D, 512)
    w_in[:, :, C_SW:C_SW + 512] = swq
    rpb = inp["na_rpb"]
    kc = np.arange(64)[:, None]
    qc = np.arange(64)[None, :]
    dc = np.clip(kc - qc, -15, 15) + 15
    rpbg = np.zeros((DEPTH, 128, 8, 16, 64), np.float32)
    for s in range(16):
        dr_lo, dr_up = s - 1, s
        if 0 <= dr_lo <= 14:
            rpbg[:, 0:64, :, s, :] = np.transpose(rpb[:, :, dr_lo, :][:, :, dc], (0, 2, 1, 3))
        if 0 <= dr_up <= 14:
            rpbg[:, 64:128, :, s, :] = np.transpose(rpb[:, :, dr_up, :][:, :, dc], (0, 2, 1, 3))
    rpbg = rpbg.reshape(DEPTH, 128, 8 * 16 * 64)
    rpbg2 = np.zeros((DEPTH, 128, 8, 22, 64), np.float32)
    for t in range(22):
        dr_lo, dr_up = 17 - t, 18 - t
        if 0 <= dr_lo <= 14:
            rpbg2[:, 0:64, :, t, :] = np.transpose(rpb[:, :, dr_lo, :][:, :, dc], (0, 2, 1, 3))
        if 0 <= dr_up <= 14:
            rpbg2[:, 64:128, :, t, :] = np.transpose(rpb[:, :, dr_up, :][:, :, dc], (0, 2, 1, 3))
    rpbg2 = rpbg2.reshape(DEPTH, 128, 8 * 22 * 64)
    svec = np.zeros((DEPTH, 128, 16), np.float32)
    svec[:, :, 0] = np.tile(inp["na_q_norm"], (1, 2))
    svec[:, :, 1] = np.tile(inp["na_k_norm"], (1, 2))
    svec[:, :, 2] = np.tile(inp["sw_q_norm"], (1, 2))
    svec[:, :, 3] = np.tile(inp["sw_k_norm"], (1, 2))
    svec[:, :, 4:7] = inp["mla_q_rank_norm"].reshape(DEPTH, 3, 128).transpose(0, 2, 1)
    svec[:, :, 7:9] = inp["mla_kv_rank_norm"].reshape(DEPTH, 2, 128).transpose(0, 2, 1)
    svec[:, 0:96, 9] = inp["mla_q_norm"]
    svec[:, 0:96, 10] = inp["mla_k_norm"]
    cw = inp["conv_w"].reshape(DEPTH, 3, 44, 128).transpose(0, 3, 2, 1).reshape(DEPTH, 128, 44 * 3)
    cb = inp["conv_b"].reshape(DEPTH, 44, 128).transpose(0, 2, 1)
    f = lambda a: np.ascontiguousarray(a, dtype=np.float32)
    return dict(w_in=f(w_in), rpbg=f(rpbg), rpbg2=f(rpbg2), svec=f(svec), cw=f(cw), cb=f(cb))


_CACHE = {}


def _in_maps(inp):
    consts = _consts()
    lay = _layouts(inp)
    f = lambda a: np.ascontiguousarray(a, dtype=np.float32)
    shared = dict(w_ada=f(inp["w_ada"]), b_ada=f(inp["b_ada"]), g_mix=f(inp["g_mix"]), g_ffn=f(inp["g_ffn"]),
                  sw_sink=f(inp["sw_sink"]), w_uq=f(inp["w_uq"]), w_ukv=f(inp["w_ukv"]), w_branch=f(inp["w_branch"]),
                  w_out=f(inp["w_out"]), w_up=f(inp["w_up"]), w_down=f(inp["w_down"]))
    shared.update(lay)
    shared.update(consts)
    maps = []
    for b in range(8):
        m = dict(shared)
        m["x"] = f(inp["x"][b])
        m["ctx"] = f(inp["ctx"][b])
        cc = np.stack([inp["c"][b], inp["c_ctx"]], axis=-1)
        m["cT"] = f(cc.reshape(8, 128, 2).transpose(1, 0, 2))
        maps.append(m)
    return maps


def kernel(**inputs):
    inp = {k: np.asarray(v) for k, v in inputs.items()}
    if "nc" not in _CACHE:
        _CACHE["nc"] = build_program(DEPTH, False)
    nc = _CACHE["nc"]
    maps = _in_maps(inp)
    res = run_bass_kernel_spmd(nc, maps, core_ids=list(range(8)))
    out = np.stack([np.asarray(r["out"], dtype=np.float32) for r in res.results], axis=0)
    return out
```

```python
import numpy as np
from contextlib import ExitStack
import concourse.bass as bass
import concourse.mybir as mybir
from concourse.bass_utils import run_bass_kernel_spmd

F32 = mybir.dt.float32
BF16 = mybir.dt.bfloat16
AF = mybir.ActivationFunctionType
ALU = mybir.AluOpType

D = 1024
SEQ = 4096
CTX = 256
T = SEQ + CTX
NT = T // 128
DEPTH = 2
D_IN = 6048
DFF = 2816
GRID = 64
NEG = -30000.0
EPS = 1e-6
C_NA = 0
C_SW = 1536
C_ML = 2304
C_G = 2976
SEM_EPOCH = 8000

DEBUG = False
LAYERS = DEPTH


class Buf:
    __slots__ = ("name", "w", "r")

    def __init__(self, name=""):
        self.name = name
        self.w = None
        self.r = {}


class Sched:
    def __init__(self, nc, n_dma_sems=24):
        self.nc = nc
        self.eng = {"pe": nc.tensor, "act": nc.scalar, "dve": nc.vector,
                    "pool": nc.gpsimd, "sp": nc.sync}
        self.cur = {}
        self.cnt = {}
        self.nsem = 0
        self.known = {e: {} for e in self.eng}
        self.allsems = []
        for e in self.eng:
            self._new_sem(e)
        self.dma_sems = []
        for i in range(n_dma_sems):
            s = nc.alloc_semaphore(f"dq{i}")
            self.dma_sems.append([s, 0])
        self.dma_i = 0
        self.bar_sem = nc.alloc_semaphore("barsem")
        self.bar_n = 0
        self.n_ops = 0
        self.n_waits = 0

    def _new_sem(self, e):
        self.cur[e] = self.nc.alloc_semaphore(f"s_{e}_{self.nsem}")
        self.nsem += 1
        self.cnt[e] = 0

    def _wait(self, e, tk):
        sem, val = tk
        k = id(sem)
        kn = self.known[e]
        if kn.get(k, 0) >= val:
            return
        self.eng[e].wait_ge(sem, val)
        kn[k] = val
        self.n_waits += 1

    def _deps(self, e, reads, writes, skip_same_engine=False):
        deps = {}
        for b in reads:
            tk = b.w
            if tk is not None:
                k = id(tk[0])
                if k not in deps or deps[k][1] < tk[1]:
                    deps[k] = tk
        for b in writes:
            tk = b.w
            if tk is not None:
                k = id(tk[0])
                if k not in deps or deps[k][1] < tk[1]:
                    deps[k] = tk
            for k, tk in b.r.items():
                if k not in deps or deps[k][1] < tk[1]:
                    deps[k] = tk
        for tk in deps.values():
            if skip_same_engine and tk[0] is self.cur[e]:
                continue
            self._wait(e, tk)

    def _commit(self, tk, reads, writes):
        k = id(tk[0])
        for b in reads:
            b.r[k] = tk
        for b in writes:
            b.w = tk
            b.r = {}

    def _ticket(self, e, ins):
        if self.cnt[e] >= SEM_EPOCH:
            self._new_sem(e)
        self.cnt[e] += 1
        tk = (self.cur[e], self.cnt[e])
        ins.then_inc(tk[0], 1)
        return tk

    def op(self, e, fn, reads=(), writes=()):
        self._deps(e, reads, writes, skip_same_engine=(e == "pe"))
        ins = fn(self.eng[e])
        self.n_ops += 1
        tk = self._ticket(e, ins)
        self._commit(tk, reads, writes)
        return tk

    def pe_group(self, fns, reads=(), writes=()):
        self._deps("pe", reads, writes, skip_same_engine=True)
        ins = None
        for fn in fns:
            ins = fn(self.eng["pe"])
        self.n_ops += len(fns)
        tk = self._ticket("pe", ins)
        self._commit(tk, reads, writes)
        return tk

    def dma(self, out, in_, reads=(), writes=(), q="sp"):
        slot = self.dma_sems[self.dma_i % len(self.dma_sems)]
        self.dma_i += 1
        sem, v = slot
        if v > 0:
            self._wait(q, (sem, v))
        self._deps(q, reads, writes)
        ins = self.eng[q].dma_start(out=out, in_=in_)
        slot[1] = v + 16
        tk = (sem, v + 16)
        ins.then_inc(sem, 16)
        self._commit(tk, reads, writes)
        self.n_ops += 1
        return tk

    def barrier(self):
        for sem, v in self.dma_sems:
            if v > 0:
                self._wait("sp", (sem, v))
        for e in self.eng:
            if e != "sp" and self.cnt[e] > 0:
                self._wait("sp", (self.cur[e], self.cnt[e]))
        self.bar_n += 1
        self.eng["sp"].sem_inc(self.bar_sem, 1)
        for e in self.eng:
            if e == "sp":
                continue
            self.eng[e].wait_ge(self.bar_sem, self.bar_n)
            kn = self.known[e]
            for sem, v in self.dma_sems:
                kn[id(sem)] = v
            for f in self.eng:
                kn[id(self.cur[f])] = self.cnt[f]


class RR:
    def __init__(self, tiles):
        self.t = tiles
        self.b = [Buf() for _ in tiles]
        self.i = 0

    def get(self):
        j = self.i % len(self.t)
        self.i += 1
        return self.t[j], self.b[j]


def mm(out, lhsT, rhs, start, stop):
    return lambda e: e.matmul(out, lhsT=lhsT, rhs=rhs, start=start, stop=stop)


def build_program(layers=DEPTH, debug=False):
    nc = bass.Bass("TRN2", target_bir_lowering=False)
    S = Sched(nc)
    okind = "ExternalOutput" if debug else "Internal"

    def din(name, shape, dt=F32):
        return nc.dram_tensor(name, list(shape), dt, kind="ExternalInput")

    x_d = din("x", [SEQ, D]).ap()
    ctx_d = din("ctx", [CTX, D]).ap()
    cT_d = din("cT", [128, 8, 2]).ap()
    w_ada_d = din("w_ada", [DEPTH, D, 6 * D]).ap()
    b_ada_h = din("b_ada", [DEPTH, 6 * D])
    g_mix_h = din("g_mix", [DEPTH, D])
    g_ffn_h = din("g_ffn", [DEPTH, D])
    w_in_d = din("w_in", [DEPTH, D, D_IN]).ap()
    rpbg_d = din("rpbg", [DEPTH, 128, 8 * 16 * 64]).ap()
    rpbg2_d = din("rpbg2", [DEPTH, 128, 8 * 22 * 64]).ap()
    negm2_d = din("negm2", [128, 22 * 64]).ap()
    svec_d = din("svec", [DEPTH, 128, 16]).ap()
    sink_h = din("sw_sink", [DEPTH, 8])
    w_uq_d = din("w_uq", [DEPTH, 384, 768]).ap()
    w_ukv_d = din("w_ukv", [DEPTH, 256, 1024]).ap()
    w_br_d = din("w_branch", [DEPTH, 3, 512, D]).ap()
    w_out_d = din("w_out", [DEPTH, D, D]).ap()
    w_up_d = din("w_up", [DEPTH, D, 2 * DFF]).ap()
    cw_d = din("cw", [DEPTH, 128, 44 * 3]).ap()
    cb_d = din("cb", [DEPTH, 128, 44]).ap()
    w_dn_d = din("w_down", [DEPTH, DFF, D]).ap()
    cm_d = din("cmats", [6, 128, 128]).ap()
    cs_sw_d = din("cs_sw", [2, 128, SEQ]).ap()
    cs_ml_d = din("cs_ml", [2, 128, SEQ]).ap()
    msk_sw_d = din("msk_sw", [2, 128, 128]).ap()
    negm_d = din("negm", [128, 64]).ap()

    out_d = nc.dram_tensor("out", [SEQ, D], F32, kind="ExternalOutput").ap()
    mod_h = nc.dram_tensor("mod_s", [DEPTH, 2, 6 * D], F32, kind=okind)
    mod_d = mod_h.ap()
    xs_d = nc.dram_tensor("xs_s", [T, D], F32, kind=okind).ap()
    nod = DEPTH if debug else 1
    o_d = nc.dram_tensor("o_s", [nod, 3, 512, T], BF16, kind=okind).ap()
    hT_d = nc.dram_tensor("hT_s", [128, 8, T], BF16, kind=okind).ap()
    NRS = 6
    rs_h = nc.dram_tensor("rs_s", [NRS, 512], F32, kind="Internal")
    rs_d = rs_h.ap()
    B_rsd = [Buf() for _ in range(NRS)]
    rs_i = [0]

    B_mod = [Buf() for _ in range(DEPTH)]
    B_xs = [Buf() for _ in range(NT)]
    B_o = [[Buf() for _ in range(NT)] for _ in range(3)]
    B_out = Buf()

    _uid = [0]

    def sb(es, name, shape, dt):
        _uid[0] += 1
        return es.enter_context(nc.sbuf_tensor(f"sb_{name}_{_uid[0]}", list(shape), dt))

    def rr(es, name, shape, dt, n):
        return RR([sb(es, f"{name}{i}", shape, dt) for i in range(n)])

    def bcast_ap(handle, offset, n):
        return bass.AP(tensor=handle, offset=offset, ap=[[0, 128], [1, n]])

    with ExitStack() as top:
        psb = [top.enter_context(nc.psum_tensor(f"psb{i}", [128, 512], F32)) for i in range(7)]
        psT = top.enter_context(nc.psum_tensor("psT", [128, 1024], BF16))
        B_psT = Buf()
        B_hT = [Buf() for _ in range(NT)]
        cm = sb(top, "cm", [128, 6, 128], BF16)
        B_cm = Buf()
        S.dma(cm[:], cm_d.rearrange("m p n -> p m n"), writes=[B_cm], q="pool")
        ident = cm[:, 0, :]
        blockones = cm[:, 1, :]
        allones = cm[:, 2, :]
        perm_sw = cm[:, 3, :]
        perm_ml = cm[:, 4, :]
        shiftm = cm[:, 5, :]
        onesf = sb(top, "onesf", [128, 64], F32)
        epst = sb(top, "epst", [128, 1], F32)
        B_const = Buf()
        S.op("dve", lambda e: e.memset(onesf[:], 1.0), writes=[B_const])
        S.op("dve", lambda e: e.memset(epst[:], EPS), writes=[B_const])
        svec = sb(top, "svec", [128, 16], F32)
        svq = sb(top, "svq", [128, 16], F32)
        B_sv = Buf()

        def hbufs(a, b):
            return B_hT[a // 128:(b + 127) // 128]

        def xbufs(a, b):
            return B_xs[a // 128:(b + 127) // 128]

        def hload(pool, a, n):
            ht, Bht = pool.get()
            S.dma(ht[:, :, 0:n], hT_d[:, :, a:a + n], reads=hbufs(a, a + n), writes=[Bht])
            return ht, Bht

        def hstore(hst, Bhst, a, n):
            S.dma(hT_d[:, :, a:a + n], hst[:, :, 0:n], reads=[Bhst], writes=hbufs(a, a + n))

        def norm_to_hT(es_pools, xt, Bx, G, SH, Bg, hst, Bhst, slot):
            for _ in norm_gen(es_pools, xt, Bx, G, SH, Bg, hst, Bhst, slot):
                pass

        def norm_gen(es_pools, xt, Bx, G, SH, Bg, hst, Bhst, slot, after=None):
            P = es_pools
            st, Bst = P["stat"].get()
            jk, Bjk = P["junk"].get()
            S.op("act", lambda e: e.activation(out=jk[:], in_=xt[:], func=AF.Square, accum_out=st[:, 0:1]),
                 reads=[Bx], writes=[Bjk, Bst])
            S.op("act", lambda e: e.activation(out=st[:, 1:2], in_=st[:, 0:1], func=AF.Sqrt,
                                               scale=1.0 / D, bias=epst[:, 0:1]),
                 reads=[Bst, B_const], writes=[Bst])
            yield
            S.op("dve", lambda e: e.reciprocal(out=st[:, 2:3], in_=st[:, 1:2]), reads=[Bst], writes=[Bst])
            tm, Btm = P["tmpf"].get()
            S.op("dve", lambda e: e.scalar_tensor_tensor(out=tm[:], in0=xt[:], scalar=st[:, 2:3], in1=G[:],
                                                         op0=ALU.mult, op1=ALU.mult),
                 reads=[Bx, Bst, Bg], writes=[Btm])
            hb, Bhb = P["hb"].get()
            S.op("pool", lambda e: e.tensor_tensor(out=hb[:], in0=tm[:], in1=SH[:], op=ALU.add),
                 reads=[Btm, Bg], writes=[Bhb])
            yield
            S.pe_group([(lambda e, k=k: e.transpose(psT[:, k * 128:(k + 1) * 128], hb[:, k * 128:(k + 1) * 128], ident))
                        for k in range(8)], reads=[Bhb, B_cm], writes=[B_psT])
            S.op("act", lambda e: e.activation(out=hst[:, :, slot * 128:(slot + 1) * 128],
                                               in_=psT[:, :].rearrange("p (k n) -> p k n", k=8), func=AF.Identity),
                 reads=[B_psT], writes=[Bhst])
            if after is not None:
                after()

        def load_mod_bcast(tile, Bt, l, row, which):
            S.dma(tile[:], bcast_ap(mod_h, (l * 2 + row) * 6 * D + which * D, D), reads=[B_mod[l]], writes=[Bt])

        def make_G(es, l, row, which_scale, ghandle, name):
            Gt = sb(es, name, [128, D], F32)
            Bg = Buf()
            gt = sb(es, name + "g", [128, D], F32)
            Bgt = Buf()
            load_mod_bcast(Gt, Bg, l, row, which_scale)
            S.dma(gt[:], bcast_ap(ghandle, l * D, D), writes=[Bgt])
            S.op("dve", lambda e: e.scalar_tensor_tensor(out=Gt[:], in0=Gt[:], scalar=1.0, in1=gt[:],
                                                         op0=ALU.add, op1=ALU.mult),
                 reads=[Bg, Bgt], writes=[Bg])
            return Gt, Bg

        def norm_pools(es, depth=2):
            return {"stat": rr(es, "nstat", [128, 4], F32, depth + 1),
                    "junk": rr(es, "njunk", [128, D], BF16, depth),
                    "tmpf": rr(es, "ntmpf", [128, D], F32, depth),
                    "hb": rr(es, "nhb", [128, D], BF16, depth)}

        def load_w(dst, src_rows, Bw):
            S.dma(dst, src_rows.rearrange("(kc p) n -> p kc n", p=128), writes=[Bw], q="pool")

        def phase_ada(l):
            with ExitStack() as es:
                cT = sb(es, "cT", [128, 8, 2], F32)
                Bc = Buf()
                S.dma(cT[:], cT_d, writes=[Bc])
                S.op("act", lambda e: e.activation(out=cT[:], in_=cT[:], func=AF.Silu), reads=[Bc], writes=[Bc])
                bada = sb(es, "bada", [2, 6 * D], F32)
                Bb = Buf()
                S.dma(bada[:], bass.AP(tensor=b_ada_h, offset=l * 6 * D, ap=[[0, 2], [1, 6 * D]]), writes=[Bb])
                modsb = sb(es, "modsb", [2, 6 * D], F32)
                Bm = Buf()
                wa = rr(es, "wa", [128, 8, 512], F32, 4)
                pp = RR(psb[0:2])
                for j in range(12):
                    wt, Bw = wa.get()
                    S.dma(wt[:], w_ada_d[l, :, j * 512:(j + 1) * 512].rearrange("(kc p) n -> p kc n", p=128),
                          writes=[Bw])
                    ps, Bp = pp.get()
                    S.pe_group([mm(ps[0:2, :], cT[:, k, :], wt[:, k, :], k == 0, k == 7) for k in range(8)],
                               reads=[Bc, Bw], writes=[Bp])
                    S.op("dve", lambda e: e.tensor_tensor(out=modsb[:, j * 512:(j + 1) * 512], in0=ps[0:2, :],
                                                          in1=bada[:, j * 512:(j + 1) * 512], op=ALU.add),
                         reads=[Bp, Bb], writes=[Bm])
                S.dma(mod_d[l], modsb[:], reads=[Bm], writes=[B_mod[l]])
                S.dma(svec[:], svec_d[l], writes=[B_sv])
                S.op("dve", lambda e: e.tensor_scalar(out=svq[:, 0:4], in0=svec[:, 0:4], scalar1=0.125, scalar2=None,
                                                      op0=ALU.mult), reads=[B_sv], writes=[B_sv])
                S.op("dve", lambda e: e.tensor_scalar(out=svq[:, 9:10], in0=svec[:, 9:10], scalar1=96.0 ** -0.5,
                                                      scalar2=None, op0=ALU.mult), reads=[B_sv], writes=[B_sv])
                S.barrier()

        def phase_n1(l):
            with ExitStack() as es:
                P = norm_pools(es, 4)
                xin = rr(es, "n1x", [128, D], F32, 6)
                Gl, Bgl = make_G(es, l, 0, 1, g_mix_h, "n1Gl")
                Gc, Bgc = make_G(es, l, 1, 1, g_mix_h, "n1Gc")
                SHl = sb(es, "n1SHl", [128, D], F32)
                SHc = sb(es, "n1SHc", [128, D], F32)
                load_mod_bcast(SHl, Bgl, l, 0, 0)
                load_mod_bcast(SHc, Bgc, l, 1, 0)
                hstp = rr(es, "n1hst", [128, 8, 512], BF16, 3)

                def n1_task(t, hst, Bhst):
                    xt, Bx = xin.get()
                    if l == 0:
                        src = x_d[t * 128:(t + 1) * 128, :] if t < 32 else ctx_d[(t - 32) * 128:(t - 31) * 128, :]
                        S.dma(xt[:], src, writes=[Bx])
                    else:
                        S.dma(xt[:], xs_d[t * 128:(t + 1) * 128, :], reads=[B_xs[t]], writes=[Bx])
                    after = None
                    if t % 4 == 3 or t == NT - 1:
                        a0 = (t // 4) * 512
                        after = (lambda: hstore(hst, Bhst, a0, (t + 1) * 128 - a0))
                    if t < 32:
                        yield from norm_gen(P, xt, Bx, Gl, SHl, Bgl, hst, Bhst, t % 4, after)
                    else:
                        yield from norm_gen(P, xt, Bx, Gc, SHc, Bgc, hst, Bhst, t % 4, after)
                tasks = []
                for t in range(NT):
                    if t % 4 == 0:
                        hst, Bhst = hstp.get()
                    tasks.append(n1_task(t, hst, Bhst))
                run_tasks(tasks)
                S.barrier()

        def run_tasks(gens, depth=3):
            active = []
            it = iter(gens)
            while True:
                while len(active) < depth:
                    g = next(it, None)
                    if g is None:
                        break
                    active.append(g)
                if not active:
                    break
                for g in list(active):
                    try:
                        next(g)
                    except StopIteration:
                        active.remove(g)

        def qk_task(P, mmf, mrd, npart, n, onesm, inv_d, gcol, dst, Bdst, rope=None):
            ps, Bps = P["ps1"].get()
            S.pe_group(mmf(ps), reads=mrd, writes=[Bps])
            sq, Bsq = P["sq"].get()
            S.op("act", lambda e: e.activation(out=sq[0:npart, 0:n], in_=ps[0:npart, 0:n], func=AF.Square),
                 reads=[Bps], writes=[Bsq])
            yield
            p2, Bp2 = P["ps2"].get()
            S.pe_group([mm(p2[0:npart, 0:n], onesm[0:npart, 0:npart], sq[0:npart, 0:n], True, True)],
                       reads=[Bsq, B_cm], writes=[Bp2])
            yield
            rt, Brt = P["rt"].get()
            S.op("act", lambda e: e.activation(out=rt[0:npart, 0:n], in_=p2[0:npart, 0:n], func=AF.Ln,
                                               scale=inv_d, bias=epst[0:npart, 0:1]),
                 reads=[Bp2, B_const], writes=[Brt])
            yield
            S.op("act", lambda e: e.activation(out=rt[0:npart, 0:n], in_=rt[0:npart, 0:n], func=AF.Exp, scale=-0.5),
                 reads=[Brt], writes=[Brt])
            if rope is None:
                S.op("dve", lambda e: e.scalar_tensor_tensor(out=dst, in0=ps[0:npart, 0:n], scalar=gcol,
                                                             in1=rt[0:npart, 0:n], op0=ALU.mult, op1=ALU.mult),
                     reads=[Bps, Brt, B_sv], writes=[Bdst])
                return
            perm, cos_ap, sin_ap, Bcs = rope
            qn, Bqn = P["qn"].get()
            S.op("dve", lambda e: e.scalar_tensor_tensor(out=qn[0:npart, 0:n], in0=ps[0:npart, 0:n], scalar=gcol,
                                                         in1=rt[0:npart, 0:n], op0=ALU.mult, op1=ALU.mult),
                 reads=[Bps, Brt, B_sv], writes=[Bqn])
            yield
            p3, Bp3 = P["ps2"].get()
            S.pe_group([mm(p3[0:npart, 0:n], perm[0:npart, 0:npart], qn[0:npart, 0:n], True, True)],
                       reads=[Bqn, B_cm], writes=[Bp3])
            t1, Bt1 = P["rt"].get()
            S.op("pool", lambda e: e.tensor_tensor(out=t1[0:npart, 0:n], in0=qn[0:npart, 0:n], in1=cos_ap, op=ALU.mult),
                 reads=[Bqn, Bcs], writes=[Bt1])
            yield
            t2, Bt2 = P["rt"].get()
            S.op("dve", lambda e: e.tensor_tensor(out=t2[0:npart, 0:n], in0=p3[0:npart, 0:n], in1=sin_ap, op=ALU.mult),
                 reads=[Bp3, Bcs], writes=[Bt2])
            S.op("pool", lambda e: e.tensor_tensor(out=dst, in0=t1[0:npart, 0:n], in1=t2[0:npart, 0:n], op=ALU.add),
                 reads=[Bt1, Bt2], writes=[Bdst])

        def v_task(P, mmf, mrd, ncol, dst, Bdst):
            ps, Bps = P["ps1"].get()
            S.pe_group(mmf(ps), reads=mrd, writes=[Bps])
            S.op("act", lambda e: e.activation(out=dst, in_=ps[:, 0:ncol].rearrange("p (h d) -> p h d", d=64),
                                               func=AF.Identity), reads=[Bps], writes=[Bdst])
            yield

        def proj_pools(es):
            return {"sq": rr(es, "psq", [128, 512], BF16, 4),
                    "rt": rr(es, "prt", [128, 512], F32, 8),
                    "qn": rr(es, "pqn", [128, 512], BF16, 4),
                    "ps1": RR(psb[0:3]),
                    "ps2": RR(psb[3:7])}

        def attn_pools(es, nsc):
            return {"sc": RR(psb[0:nsc]), "acc": RR(psb[4:7]),
                    "pT": rr(es, "apT", [128, 512], BF16, nsc + 1),
                    "rs": rr(es, "ars", [128, 512], F32, 3),
                    "bcs": rr(es, "abcs", [64, 512], F32, 3)}

        pend_pv = []
        LOOK = 2
        FILL = [0, None]

        def flush_pv(keep=0):
            while len(pend_pv) > keep:
                fns, rds, BpsO_ = pend_pv.pop(0)
                S.pe_group(fns, reads=rds, writes=[BpsO_])

        def attend_groups(P, psO, BpsO, col0, n, groups, first):
            started = not first
            ng = len(groups)
            for gi, grp in enumerate(groups):
                sc, Bsc = P["sc"].get()
                fns = []
                rds = []
                for ti, tl in enumerate(grp):
                    fns += tl[0](sc[:, ti * n:(ti + 1) * n])
                    rds += tl[1]
                S.pe_group(fns, reads=rds, writes=[Bsc])
                pT, BpT = P["pT"].get()
                w = len(grp) * n
                S.op("act", lambda e: e.activation(out=pT[:, 0:w], in_=sc[:, 0:w], func=AF.Exp),
                     reads=[Bsc], writes=[BpT])
                for ti, tl in enumerate(grp):
                    if len(tl) > 5 and tl[5] is not None:
                        b_ap, b_rd = tl[5]
                        S.op("dve", lambda e: e.tensor_tensor(out=pT[:, ti * n:(ti + 1) * n], in0=pT[:, ti * n:(ti + 1) * n],
                                                              in1=b_ap, op=ALU.mult), reads=[BpT] + b_rd, writes=[BpT])
                fns = []
                rds = [BpT]
                for ti, tl in enumerate(grp):
                    (p0, p1), vl, vrd = tl[2], tl[3], tl[4]
                    last = (gi == ng - 1) and (ti == len(grp) - 1)
                    fns.append(mm(psO[0:65, col0:col0 + n], vl, pT[p0:p1, ti * n:(ti + 1) * n], not started, last))
                    started = True
                    rds += vrd
                pend_pv.append((fns, rds, BpsO))
                flush_pv(LOOK)
                for _ in range(FILL[0]):
                    nc.tensor.matmul(psb[3][:, :], lhsT=ident, rhs=FILL[1], start=True, stop=True)

        pend_norm = []

        def flush_norm():
            while pend_norm:
                pend_norm.pop(0)()

        def normalize(P, psO, BpsO, n, dst, Bdst, sink_ap=None):
            flush_pv()
            rs, Brs = P["rs"].get()
            if sink_ap is not None:
                S.op("act", lambda e: e.activation(out=rs[64:65, 0:n], in_=psO[64:65, 0:n], func=AF.Ln, bias=sink_ap),
                     reads=[BpsO, B_sv], writes=[Brs])
            else:
                S.op("act", lambda e: e.activation(out=rs[64:65, 0:n], in_=psO[64:65, 0:n], func=AF.Ln),
                     reads=[BpsO], writes=[Brs])
            S.op("act", lambda e: e.activation(out=rs[64:65, 0:n], in_=rs[64:65, 0:n], func=AF.Exp, scale=-1.0),
                 reads=[Brs], writes=[Brs])
            slot = rs_i[0] % NRS
            rs_i[0] += 1
            S.dma(rs_d[slot:slot + 1, 0:n], rs[64:65, 0:n], reads=[Brs], writes=[B_rsd[slot]])
            bc, Bbc = P["bcs"].get()
            S.dma(bc[0:64, 0:n], bass.AP(tensor=rs_h, offset=slot * 512, ap=[[0, 64], [1, n]]),
                  reads=[B_rsd[slot]], writes=[Bbc])
            flush_norm()
            pend_norm.append(lambda: S.op("dve", lambda e: e.tensor_tensor(out=dst, in0=psO[0:64, 0:n], in1=bc[0:64, 0:n],
                                                                           op=ALU.mult),
                                          reads=[BpsO, Bbc], writes=[Bdst]))

        def close_acc(psO, BpsO, col0, n, vl_dummy=None):
            pass

        def store_o(l, br, ost, Bost, a, n):
            li = l if debug else 0
            flush_norm()
            dst = o_d[li, br, :, a:a + n].rearrange("(h d) n -> d h n", d=64)
            S.dma(dst, ost[0:64, :, 0:n], reads=[Bost], writes=B_o[br][a // 128:(a + n + 127) // 128])

        def phase_na(l, need_ctx):
            with ExitStack() as es:
                Wt = sb(es, "naW", [128, 8, 1536], BF16)
                Bw = Buf()
                for c3 in range(3):
                    load_w(Wt[:, :, c3 * 512:(c3 + 1) * 512], w_in_d[l, :, C_NA + c3 * 512:C_NA + (c3 + 1) * 512], Bw)
                QT = sb(es, "naQ", [128, 4, T], BF16)
                KT = sb(es, "naK", [128, 4, T], BF16)
                Vp = sb(es, "naV", [128, NT, 8, 65], BF16)
                Bq = [Buf() for _ in range(9)]
                Bk = [Buf() for _ in range(9)]
                Bv = [Buf() for _ in range(NT)]
                Bvo = Buf()
                S.op("pool", lambda e: e.memset(Vp[:, :, :, 64:65], 1.0), writes=[Bvo])
                Ct = sb(es, "naC", [128, 8 * 16 * 64], BF16)
                Ct2 = sb(es, "naC2", [128, 8 * 22 * 64], BF16)
                Bc = Buf()
                with ExitStack() as es2:
                    Cg = sb(es2, "naCg", [128, 8 * 16, 64], F32)
                    ng = sb(es2, "naNg", [128, 64], F32)
                    Bcg = Buf()
                    S.dma(Cg[:], rpbg_d[l].rearrange("p (a q) -> p a q", q=64), writes=[Bcg])
                    S.dma(ng[:], negm_d, writes=[Bcg])
                    ngb = bass.AP(tensor=ng, offset=0, ap=[[64, 128], [0, 128], [1, 64]])
                    S.op("dve", lambda e: e.tensor_tensor(out=Ct[:].rearrange("p (a q) -> p a q", q=64), in0=Cg[:],
                                                          in1=ngb, op=ALU.add),
                         reads=[Bcg], writes=[Bc])
                    ng2 = sb(es2, "naNg2", [128, 22 * 64], F32)
                    S.dma(ng2[:], negm2_d, writes=[Bcg])
                    TW = 22 * 64
                    for hh2 in range(2):
                        S.dma(Cg[:, 0:88, :], rpbg2_d[l, :, hh2 * 4 * TW:(hh2 + 1) * 4 * TW].rearrange("p (a q) -> p a q", q=64),
                              reads=[Bcg], writes=[Bcg])
                        ngb2 = bass.AP(tensor=ng2, offset=0, ap=[[TW, 128], [0, 4], [1, TW]])
                        S.op("dve", lambda e: e.tensor_tensor(
                            out=Ct2[:, hh2 * 4 * TW:(hh2 + 1) * 4 * TW].rearrange("p (h q) -> p h q", q=TW),
                            in0=Cg[:, 0:88, :].rearrange("p (h u) q -> p h (u q)", h=4), in1=ngb2, op=ALU.add),
                            reads=[Bcg], writes=[Bc])
                    for hh2 in range(4):
                        S.op("act", lambda e: e.activation(out=Ct2[:, hh2 * 2 * TW:(hh2 + 1) * 2 * TW],
                                                           in_=Ct2[:, hh2 * 2 * TW:(hh2 + 1) * 2 * TW], func=AF.Exp),
                             reads=[Bc], writes=[Bc])
                    S.barrier()
                with ExitStack() as es2:
                    P = proj_pools(es2)
                    hbp = rr(es2, "nahb", [128, 8, 512], BF16, 2)
                    for tb in range(9):
                        a = tb * 512
                        n = 512 if tb < 8 else 256
                        hT, Bh = hload(hbp, a, n)
                        hb = [Bh]
                        tasks = []
                        for which, dstT, Bd, gc in ((0, QT, Bq, svq[:, 0:1]), (1, KT, Bk, svec[:, 1:2])):
                            if which == 0 and tb == 8 and not need_ctx:
                                continue
                            for c in range(4):
                                col = which * 512 + c * 128
                                mmf = (lambda ps, col=col, hT=hT, n=n: [mm(ps[:, 0:n], Wt[:, k, col:col + 128], hT[:, k, 0:n], k == 0, k == 7)
                                                                       for k in range(8)])
                                tasks.append(qk_task(P, mmf, [Bw] + hb, 128, n, blockones, 1.0 / 64, gc,
                                                     dstT[:, c, a:a + n], Bd[tb]))
                        run_tasks(tasks)
                        tasks = []
                        for t in range(a // 128, (a + n) // 128):
                            mmf = (lambda ps, t=t, hT=hT, a=a: [mm(ps[:, :], hT[:, k, t * 128 - a:(t + 1) * 128 - a], Wt[:, k, 1024:1536],
                                                                 k == 0, k == 7) for k in range(8)])
                            tasks.append(v_task(P, mmf, [Bw, Bh], 512, Vp[:, t, :, 0:64], Bv[t]))
                        run_tasks(tasks)
                    S.barrier()
                with ExitStack() as es2:
                    P = attn_pools(es2, 3)
                    FILL[0], FILL[1] = 0, KT[:, 0, 0:512]
                    ostp = rr(es2, "naost", [64, 8, 512], BF16, 2)
                    for qb in range(8):
                        ost, Bost = ostp.get()
                        for h in range(8):
                            c, hp = h // 2, (h % 2) * 64
                            psO, BpsO = P["acc"].get()
                            TW = 22 * 64
                            if 1 <= qb <= 6:
                                q_rhs = QT[hp:hp + 64, c, qb * 512:(qb + 1) * 512]
                                groups = []
                                for j in range(4 * qb - 2, 4 * qb + 6):
                                    t0 = 10 + 8 * qb - 2 * j
                                    assert 0 <= t0 and t0 + 8 <= 22
                                    cb_ap = Ct2[:, h * TW + t0 * 64:h * TW + (t0 + 8) * 64]

                                    def sfn(o, j=j):
                                        return [mm(o, KT[hp:hp + 64, c, j * 128:(j + 1) * 128], q_rhs, True, True)]
                                    groups.append([(sfn, [Bk[j // 4], Bq[qb]], (0, 128), Vp[:, j, h, :], [Bv[j], Bvo], (cb_ap, [Bc]))])
                                for j in (32, 33):
                                    def sfn(o, j=j):
                                        return [mm(o, KT[hp:hp + 64, c, j * 128:(j + 1) * 128], q_rhs, True, True)]
                                    groups.append([(sfn, [Bk[8], Bq[qb]], (0, 128), Vp[:, j, h, :], [Bv[j], Bvo], None)])
                                attend_groups(P, psO, BpsO, 0, 512, groups, True)
                                normalize(P, psO, BpsO, 512, ost[0:64, h, :], Bost)
                                continue
                            for gg in range(2):
                                g = qb * 2 + gg
                                if 1 <= g <= 14:
                                    qa = g * 256
                                    q_rhs = QT[hp:hp + 64, c, qa:qa + 256]
                                    tiles = []
                                    for j in range(2 * g - 2, 2 * g + 4):
                                        t0 = 10 + 4 * g - 2 * j
                                        assert 0 <= t0 and t0 + 4 <= 22
                                        cb_ap = Ct2[:, h * TW + t0 * 64:h * TW + (t0 + 4) * 64]

                                        def sfn(o, j=j, q_rhs=q_rhs):
                                            return [mm(o, KT[hp:hp + 64, c, j * 128:(j + 1) * 128], q_rhs, True, True)]
                                        tiles.append((sfn, [Bk[j // 4], Bq[qb]], (0, 128), Vp[:, j, h, :], [Bv[j], Bvo], (cb_ap, [Bc])))
                                    for j in (32, 33):
                                        def sfn(o, j=j, q_rhs=q_rhs):
                                            return [mm(o, KT[hp:hp + 64, c, j * 128:(j + 1) * 128], q_rhs, True, True)]
                                        tiles.append((sfn, [Bk[8], Bq[qb]], (0, 128), Vp[:, j, h, :], [Bv[j], Bvo], None))
                                    attend_groups(P, psO, BpsO, gg * 256, 256, [tiles[0:2], tiles[2:4], tiles[4:6], tiles[6:8]], True)
                                    continue
                                for rr_ in range(4):
                                    r = g * 4 + rr_
                                    r0 = min(max(r - 4, 0), 56)
                                    grp = []
                                    qa = r * 64
                                    q_rhs = QT[hp:hp + 64, c, qa:qa + 64]
                                    for j in range(r0 // 2, (r0 + 7) // 2 + 1):
                                        lo_ok = r0 <= 2 * j < r0 + 8
                                        up_ok = r0 <= 2 * j + 1 < r0 + 8
                                        rows = (0 if lo_ok else 64, 128 if up_ok else 64)
                                        s_ = 2 * j - r + 8
                                        assert 0 <= s_ <= 15
                                        cb_ap = Ct[:, (h * 16 + s_) * 64:(h * 16 + s_ + 1) * 64]

                                        def sfn(o, j=j, cb_ap=cb_ap, q_rhs=q_rhs):
                                            return [mm(o, KT[hp:hp + 64, c, j * 128:(j + 1) * 128], q_rhs, True, False),
                                                    mm(o, ident, cb_ap, False, True)]
                                        grp.append((sfn, [Bk[j // 4], Bq[qb], Bc, B_cm], rows,
                                                    Vp[rows[0]:rows[1], j, h, :], [Bv[j], Bvo]))
                                    for j in (32, 33):
                                        def sfn(o, j=j, q_rhs=q_rhs):
                                            return [mm(o, KT[hp:hp + 64, c, j * 128:(j + 1) * 128], q_rhs, True, True)]
                                        grp.append((sfn, [Bk[8], Bq[qb]], (0, 128), Vp[:, j, h, :], [Bv[j], Bvo]))
                                    attend_groups(P, psO, BpsO, gg * 256 + rr_ * 64, 64, [grp], True)
                            normalize(P, psO, BpsO, 512, ost[0:64, h, :], Bost)
                        store_o(l, 0, ost, Bost, qb * 512, 512)
                    if need_ctx:
                        ost, Bost = ostp.get()
                        for h in range(8):
                            c, hp = h // 2, (h % 2) * 64
                            psO, BpsO = P["acc"].get()
                            q_rhs = QT[hp:hp + 64, c, SEQ:T]
                            groups = []
                            for j in (32, 33):
                                def sfn(o, j=j):
                                    return [mm(o, KT[hp:hp + 64, c, j * 128:(j + 1) * 128], q_rhs, True, True)]
                                groups.append([(sfn, [Bk[8], Bq[8]], (0, 128), Vp[:, j, h, :], [Bv[j], Bvo])])
                            attend_groups(P, psO, BpsO, 0, 256, groups, True)
                            normalize(P, psO, BpsO, 256, ost[0:64, h, 0:256], Bost)
                        store_o(l, 0, ost, Bost, SEQ, 256)
                    S.barrier()

        def phase_sw(l, need_ctx):
            with ExitStack() as es:
                Wt = sb(es, "swW", [128, 8, 768], BF16)
                Bw = Buf()
                load_w(Wt[:, :, 0:512], w_in_d[l, :, C_SW:C_SW + 512], Bw)
                load_w(Wt[:, :, 512:768], w_in_d[l, :, C_SW + 512:C_SW + 768], Bw)
                QT = sb(es, "swQ", [128, 4, T], BF16)
                KT = sb(es, "swK", [128, T], BF16)
                Vp = sb(es, "swV", [128, NT, 2, 65], BF16)
                msk = sb(es, "swmsk", [128, 2, 128], BF16)
                sinke = sb(es, "swsink", [128, 8], F32)
                Bcs = Buf()
                S.dma(msk[:], msk_sw_d.rearrange("m p n -> p m n"), writes=[Bcs], q="pool")
                S.op("act", lambda e: e.activation(out=msk[:], in_=msk[:], func=AF.Exp), reads=[Bcs], writes=[Bcs])
                S.dma(sinke[:], bcast_ap(sink_h, l * 8, 8), writes=[Bcs])
                S.op("act", lambda e: e.activation(out=sinke[:], in_=sinke[:], func=AF.Exp), reads=[Bcs], writes=[Bcs])
                Bq = [Buf() for _ in range(9)]
                Bk = [Buf() for _ in range(9)]
                Bv = [Buf() for _ in range(NT)]
                Bvo = Buf()
                S.op("pool", lambda e: e.memset(Vp[:, :, :, 64:65], 1.0), writes=[Bvo])
                with ExitStack() as es2:
                    P = proj_pools(es2)
                    hbp = rr(es2, "swhb", [128, 8, 512], BF16, 2)
                    csp = rr(es2, "swcs", [128, 2, 512], F32, 2)
                    for tb in range(9):
                        a = tb * 512
                        n = 512 if tb < 8 else 256
                        hT, Bh = hload(hbp, a, n)
                        hb = [Bh]
                        rope = None
                        if tb < 8:
                            cs, Bcsb = csp.get()
                            S.dma(cs[:], cs_sw_d[:, :, a:a + n].rearrange("m p n -> p m n"), writes=[Bcsb])
                            rope = (perm_sw, cs[:, 0, 0:n], cs[:, 1, 0:n], Bcsb)
                        tasks = []
                        for c in range(4):
                            if tb == 8 and not need_ctx:
                                continue
                            mmf = (lambda ps, c=c, hT=hT, n=n: [mm(ps[:, 0:n], Wt[:, k, c * 128:(c + 1) * 128], hT[:, k, 0:n], k == 0, k == 7)
                                                                for k in range(8)])
                            tasks.append(qk_task(P, mmf, [Bw] + hb, 128, n, blockones, 1.0 / 64, svq[:, 2:3],
                                                 QT[:, c, a:a + n], Bq[tb], rope))
                        mmf = (lambda ps, hT=hT, n=n: [mm(ps[:, 0:n], Wt[:, k, 512:640], hT[:, k, 0:n], k == 0, k == 7) for k in range(8)])
                        tasks.append(qk_task(P, mmf, [Bw] + hb, 128, n, blockones, 1.0 / 64, svec[:, 3:4], KT[:, a:a + n], Bk[tb], rope))
                        run_tasks(tasks)
                        tasks = []
                        for t in range(a // 128, (a + n) // 128):
                            mmf = (lambda ps, t=t, hT=hT, a=a: [mm(ps[:, 0:128], hT[:, k, t * 128 - a:(t + 1) * 128 - a], Wt[:, k, 640:768],
                                                                 k == 0, k == 7) for k in range(8)])
                            tasks.append(v_task(P, mmf, [Bw, Bh], 128, Vp[:, t, :, 0:64], Bv[t]))
                        run_tasks(tasks)
                    S.barrier()
                with ExitStack() as es2:
                    P = attn_pools(es2, 3)
                    FILL[0], FILL[1] = 0, QT[:, 0, 0:512]
                    ostp = rr(es2, "swost", [64, 8, 512], BF16, 2)
                    for qb in range(8):
                        ost, Bost = ostp.get()
                        for h in range(8):
                            c, hp, kv = h % 4, (h // 4) * 64, h // 4
                            psO, BpsO = P["acc"].get()
                            for nn in range(4):
                                nt = qb * 4 + nn
                                q_rhs = QT[hp:hp + 64, c, nt * 128:(nt + 1) * 128]
                                loc = []
                                for j, mi in ((nt - 1, 0), (nt, None), (nt + 1, 1)):
                                    if j < 0 or j > 31:
                                        continue

                                    def sfn(o, j=j, mi=mi):
                                        return [mm(o, KT[hp:hp + 64, j * 128:(j + 1) * 128], q_rhs, True, True)]
                                    loc.append((sfn, [Bk[j // 4], Bq[qb]], (0, 128), Vp[:, j, kv, :], [Bv[j], Bvo],
                                                None if mi is None else (msk[:, mi, :], [Bcs])))
                                cg = []
                                for j in (32, 33):
                                    def sfn(o, j=j):
                                        return [mm(o, KT[hp:hp + 64, j * 128:(j + 1) * 128], q_rhs, True, True)]
                                    cg.append((sfn, [Bk[8], Bq[qb]], (0, 128), Vp[:, j, kv, :], [Bv[j], Bvo]))
                                attend_groups(P, psO, BpsO, nn * 128, 128, [loc, cg], True)
                            normalize(P, psO, BpsO, 512, ost[0:64, h, :], Bost, sink_ap=sinke[64:65, h:h + 1])
                        store_o(l, 1, ost, Bost, qb * 512, 512)
                    if need_ctx:
                        ost, Bost = ostp.get()
                        for h in range(8):
                            c, hp, kv = h % 4, (h // 4) * 64, h // 4
                            psO, BpsO = P["acc"].get()
                            q_rhs = QT[hp:hp + 64, c, SEQ:T]
                            groups = []
                            for j in (32, 33):
                                def sfn(o, j=j):
                                    return [mm(o, KT[hp:hp + 64, j * 128:(j + 1) * 128], q_rhs, True, True)]
                                groups.append([(sfn, [Bk[8], Bq[8]], (0, 128), Vp[:, j, kv, :], [Bv[j], Bvo])])
                            attend_groups(P, psO, BpsO, 0, 256, groups, True)
                            normalize(P, psO, BpsO, 256, ost[0:64, h, 0:256], Bost, sink_ap=sinke[64:65, h:h + 1])
                        store_o(l, 1, ost, Bost, SEQ, 256)
                    S.barrier()

        def phase_mla(l, need_ctx):
            with ExitStack() as es:
                Wt = sb(es, "mlW", [128, 8, 672], BF16)
                Wq = sb(es, "mlWq", [128, 3, 768], BF16)
                Wkv = sb(es, "mlWkv", [128, 2, 1024], BF16)
                Wkp = sb(es, "mlWkp", [128, 8, 2, 96], BF16)
                Bw = Buf()
                load_w(Wt[:, :, 0:384], w_in_d[l, :, C_ML:C_ML + 384], Bw)
                load_w(Wt[:, :, 384:672], w_in_d[l, :, C_ML + 384:C_ML + 672], Bw)
                load_w(Wq[:], w_uq_d[l], Bw)
                load_w(Wkv[:], w_ukv_d[l], Bw)
                S.op("pool", lambda e: e.memset(Wkp[:], 0.0), writes=[Bw])
                for h in range(8):
                    S.op("pool", lambda e: e.tensor_copy(out=Wkp[:, h, :, 0:64], in_=Wkv[:, :, h * 128:h * 128 + 64]),
                         reads=[Bw], writes=[Bw])
                for hh in range(2):
                    with ExitStack() as esh:
                        QT = sb(esh, "mlQ", [96, 4, T], BF16)
                        KT = sb(esh, "mlK", [96, 4, T], BF16)
                        Vp = sb(esh, "mlV", [128, NT, 4, 65], BF16)
                        Bq = [Buf() for _ in range(9)]
                        Bk = [Buf() for _ in range(9)]
                        Bv = [Buf() for _ in range(NT)]
                        Bvo = Buf()
                        S.op("pool", lambda e: e.memset(Vp[:, :, :, 64:65], 1.0), writes=[Bvo])
                        with ExitStack() as es2:
                            P = proj_pools(es2)
                            rawp = rr(es2, "mlraw", [128, 3, 512], F32, 1)
                            sqp = rr(es2, "mlsq", [128, 3, 512], BF16, 1)
                            rawkp = rr(es2, "mlrawk", [128, 2, 512], F32, 1)
                            sqkp = rr(es2, "mlsqk", [128, 2, 512], BF16, 1)
                            cqp = rr(es2, "mlcq", [128, 3, 512], BF16, 2)
                            ckp = rr(es2, "mlck", [128, 2, 512], BF16, 2)
                            krp = rr(es2, "mlkr", [32, 512], BF16, 2)
                            hbp = rr(es2, "mlhb", [128, 8, 512], BF16, 2)
                            csp = rr(es2, "mlcs", [128, 2, 512], F32, 2)
                            for tb in range(9):
                                a = tb * 512
                                n = 512 if tb < 8 else 256
                                hT, Bh = hload(hbp, a, n)
                                hb = [Bh]
                                do_q = (tb < 8) or need_ctx
                                rope = None
                                if tb < 8:
                                    cs, Bcsb = csp.get()
                                    S.dma(cs[:], cs_ml_d[:, :, a:a + n].rearrange("m p n -> p m n"), writes=[Bcsb])
                                    rope = (perm_ml, cs[0:96, 0, 0:n], cs[0:96, 1, 0:n], Bcsb)

                                def ranknorm_g(ncx, col0, gcol0, dst, Bdst, inv_r, raw, Braw, sq, Bsq, hT=hT, n=n, hb=hb):
                                    for c in range(ncx):
                                        ps, Bps = P["ps1"].get()
                                        cc = col0 + c * 128
                                        S.pe_group([mm(ps[:, 0:n], Wt[:, k, cc:cc + 128], hT[:, k, 0:n], k == 0, k == 7)
                                                    for k in range(8)], reads=[Bw] + hb, writes=[Bps])
                                        S.op("act", lambda e: e.activation(out=raw[:, c, 0:n], in_=ps[:, 0:n], func=AF.Identity),
                                             reads=[Bps], writes=[Braw])
                                        S.op("act", lambda e: e.activation(out=sq[:, c, 0:n], in_=ps[:, 0:n], func=AF.Square),
                                             reads=[Bps], writes=[Bsq])
                                        yield
                                    p2, Bp2 = P["ps2"].get()
                                    S.pe_group([mm(p2[:, 0:n], allones, sq[:, c, 0:n], c == 0, c == ncx - 1) for c in range(ncx)],
                                               reads=[Bsq, B_cm], writes=[Bp2])
                                    yield
                                    rt, Brt = P["rt"].get()
                                    S.op("act", lambda e: e.activation(out=rt[:, 0:n], in_=p2[:, 0:n], func=AF.Ln,
                                                                       scale=inv_r, bias=epst[:, 0:1]),
                                         reads=[Bp2, B_const], writes=[Brt])
                                    yield
                                    S.op("act", lambda e: e.activation(out=rt[:, 0:n], in_=rt[:, 0:n], func=AF.Exp, scale=-0.5),
                                         reads=[Brt], writes=[Brt])
                                    for c in range(ncx):
                                        S.op("dve", lambda e: e.scalar_tensor_tensor(
                                            out=dst[:, c, 0:n], in0=raw[:, c, 0:n], scalar=svec[:, gcol0 + c:gcol0 + c + 1],
                                            in1=rt[:, 0:n], op0=ALU.mult, op1=ALU.mult),
                                            reads=[Braw, Brt, B_sv], writes=[Bdst])

                                def kr_g(kr, Bkr, hT=hT, n=n, hb=hb):
                                    ps, Bps = P["ps1"].get()
                                    S.pe_group([mm(ps[0:32, 0:n], Wt[:, k, 640:672], hT[:, k, 0:n], k == 0, k == 7)
                                                for k in range(8)], reads=[Bw] + hb, writes=[Bps])
                                    S.op("act", lambda e: e.activation(out=kr[0:32, 0:n], in_=ps[0:32, 0:n], func=AF.Identity),
                                         reads=[Bps], writes=[Bkr])
                                    yield

                                tasks = []
                                if do_q:
                                    cq, Bcq = cqp.get()
                                    raw, Braw = rawp.get()
                                    sq, Bsq = sqp.get()
                                    tasks.append(ranknorm_g(3, 0, 4, cq, Bcq, 1.0 / 384, raw, Braw, sq, Bsq))
                                ck, Bck = ckp.get()
                                raw2, Braw2 = rawkp.get()
                                sq2, Bsq2 = sqkp.get()
                                tasks.append(ranknorm_g(2, 384, 7, ck, Bck, 1.0 / 256, raw2, Braw2, sq2, Bsq2))
                                kr, Bkr = krp.get()
                                tasks.append(kr_g(kr, Bkr))
                                run_tasks(tasks)
                                tasks = []
                                for hl in range(4):
                                    h = hh * 4 + hl
                                    if do_q:
                                        mmf = (lambda ps, h=h, cq=cq, n=n: [mm(ps[0:96, 0:n], Wq[:, c, h * 96:(h + 1) * 96], cq[:, c, 0:n],
                                                                               c == 0, c == 2) for c in range(3)])
                                        tasks.append(qk_task(P, mmf, [Bw, Bcq], 96, n, allones, 1.0 / 96, svq[0:96, 9:10],
                                                             QT[0:96, hl, a:a + n], Bq[tb], rope))
                                    mmf = (lambda ps, h=h, ck=ck, kr=kr, n=n: [
                                        mm(ps[0:96, 0:n], Wkp[:, h, 0, :], ck[:, 0, 0:n], True, False),
                                        mm(ps[0:96, 0:n], Wkp[:, h, 1, :], ck[:, 1, 0:n], False, False),
                                        mm(ps[0:96, 0:n], shiftm[0:32, 0:96], kr[0:32, 0:n], False, True)])
                                    tasks.append(qk_task(P, mmf, [Bw, Bck, Bkr, B_cm], 96, n, allones, 1.0 / 96, svec[0:96, 10:11],
                                                         KT[0:96, hl, a:a + n], Bk[tb], rope))
                                run_tasks(tasks)
                                tasks = []
                                for ti, t in enumerate(range(a // 128, (a + n) // 128)):
                                    def mmf(ps, ti=ti, ck=ck):
                                        fns = []
                                        for hl in range(4):
                                            h = hh * 4 + hl
                                            for c in range(2):
                                                fns.append(mm(ps[:, hl * 64:(hl + 1) * 64], ck[:, c, ti * 128:(ti + 1) * 128],
                                                              Wkv[:, c, h * 128 + 64:h * 128 + 128], c == 0, c == 1))
                                        return fns
                                    tasks.append(v_task(P, mmf, [Bw, Bck], 256, Vp[:, t, :, 0:64], Bv[t]))
                                run_tasks(tasks)
                            S.barrier()
                        with ExitStack() as es2:
                            P = attn_pools(es2, 3)
                            FILL[0] = 0
                            ostp = rr(es2, "mlost", [64, 4, 512], BF16, 2)
                            li = l if debug else 0
                            for qb in range(9):
                                if qb == 8 and not need_ctx:
                                    continue
                                a = qb * 512
                                n = 512 if qb < 8 else 256
                                ost, Bost = ostp.get()
                                for hl in range(4):
                                    psO, BpsO = P["acc"].get()
                                    q_rhs = QT[0:96, hl, a:a + n]
                                    groups = []
                                    for j in (range(NT) if qb < 8 else (32, 33)):
                                        def sfn(o, j=j):
                                            return [mm(o, KT[0:96, hl, j * 128:(j + 1) * 128], q_rhs, True, True)]
                                        groups.append([(sfn, [Bk[j // 4], Bq[qb]], (0, 128), Vp[:, j, hl, :], [Bv[j], Bvo])])
                                    attend_groups(P, psO, BpsO, 0, n, groups, True)
                                    normalize(P, psO, BpsO, n, ost[0:64, hl, 0:n], Bost)
                                flush_norm()
                                dst = o_d[li, 2, hh * 256:(hh + 1) * 256, a:a + n].rearrange("(h d) n -> d h n", d=64)
                                S.dma(dst, ost[0:64, :, 0:n], reads=[Bost], writes=B_o[2][a // 128:(a + n) // 128])
                            S.barrier()

        def phase_merge(l, need_ctx):
            li = l if debug else 0
            with ExitStack() as es:
                Wg = sb(es, "mgWg", [128, 8, 3072], BF16)
                Wb = sb(es, "mgWb", [128, 3, 4, D], BF16)
                Wo = sb(es, "mgWo", [128, 8, D], BF16)
                BWg = [Buf() for _ in range(12)]
                BWb = [Buf() for _ in range(3)]
                BWo = Buf()

                def ldg(j):
                    load_w(Wg[:, :, j * 256:(j + 1) * 256], w_in_d[l, :, C_G + j * 256:C_G + (j + 1) * 256], BWg[j])
                for j in (0, 4, 8):
                    ldg(j)
                for nbr in range(3):
                    load_w(Wb[:, nbr, :, :], w_br_d[l, nbr], BWb[nbr])
                for j in (1, 5, 9, 2, 6, 10, 3, 7, 11):
                    ldg(j)
                load_w(Wo[:], w_out_d[l], BWo)
                P = norm_pools(es)
                GA = sb(es, "mgGA", [128, D], F32)
                SHf = sb(es, "mgSHf", [128, D], F32)
                Gf = sb(es, "mgGf", [128, D], F32)
                gt = sb(es, "mgg", [128, D], F32)
                Bg = Buf()
                S.dma(gt[:], bcast_ap(g_ffn_h, l * D, D), writes=[Bg])
                oTp = [rr(es, f"mgo{i}", [128, 4, 512], BF16, 2) for i in range(3)]
                mT = sb(es, "mgmT", [128, 8, 512], BF16)
                BmT = Buf()
                macc = sb(es, "mgacc", [128, 512], F32)
                Bmacc = Buf()
                sgp = rr(es, "mgsg", [128, 512], F32, 2)
                tmp = rr(es, "mgtmp", [128, 512], F32, 2)
                xin = rr(es, "mgx", [128, D], F32, 2)
                x1p = rr(es, "mgx1", [128, D], F32, 2)
                pg = RR(psb[0:2])
                py = RR(psb[2:4])
                po = RR(psb[4:7])
                hbp = rr(es, "mghb", [128, 8, 512], BF16, 2)
                hstp = rr(es, "mghst", [128, 8, 512], BF16, 1)
                nblk = 9 if need_ctx else 8

                def mg_loads(tb):
                    a = tb * 512
                    n = 512 if tb < 8 else 256
                    hT, Bh = hload(hbp, a, n)
                    oTs = []
                    for nbr in range(3):
                        ot, Bot = oTp[nbr].get()
                        S.dma(ot[:, :, 0:n], o_d[li, nbr, :, a:a + n].rearrange("(c p) n -> p c n", p=128),
                              reads=B_o[nbr][a // 128:(a + n) // 128], writes=[Bot])
                        oTs.append((ot, Bot))
                    return hT, Bh, oTs
                nxt = mg_loads(0)
                pend_n2 = []
                for tb in range(nblk):
                    a = tb * 512
                    n = 512 if tb < 8 else 256
                    row = 0 if tb < 8 else 1
                    hT, Bh, oTs = nxt
                    if tb + 1 < nblk:
                        nxt = mg_loads(tb + 1)
                    oT = [x[0] for x in oTs]
                    Bo = [x[1] for x in oTs]
                    if tb == 0 or tb == 8:
                        load_mod_bcast(GA, Bg, l, row, 2)
                        load_mod_bcast(SHf, Bg, l, row, 3)
                        load_mod_bcast(Gf, Bg, l, row, 4)
                        S.op("dve", lambda e: e.scalar_tensor_tensor(out=Gf[:], in0=Gf[:], scalar=1.0, in1=gt[:],
                                                                     op0=ALU.add, op1=ALU.mult), reads=[Bg], writes=[Bg])
                    hb = [Bh]
                    hst, Bhst = hstp.get()
                    for f in range(8):
                        for nbr in range(3):
                            psg, Bpg = pg.get()
                            col = nbr * D + f * 128
                            S.pe_group([mm(psg[:, 0:n], Wg[:, k, col:col + 128], hT[:, k, 0:n], k == 0, k == 7)
                                        for k in range(8)], reads=[BWg[col // 256]] + hb, writes=[Bpg])
                            psy, Bpy = py.get()
                            S.pe_group([mm(psy[:, 0:n], Wb[:, nbr, c, f * 128:(f + 1) * 128], oT[nbr][:, c, 0:n], c == 0, c == 3)
                                        for c in range(4)], reads=[BWb[nbr], Bo[nbr]], writes=[Bpy])
                            sg, Bsg = sgp.get()
                            S.op("act", lambda e: e.activation(out=sg[:, 0:n], in_=psg[:, 0:n], func=AF.Sigmoid),
                                 reads=[Bpg], writes=[Bsg])
                            if nbr == 0:
                                S.op("dve", lambda e: e.tensor_tensor(out=macc[:, 0:n], in0=psy[:, 0:n], in1=sg[:, 0:n], op=ALU.mult),
                                     reads=[Bpy, Bsg], writes=[Bmacc])
                            else:
                                tm, Btm = tmp.get()
                                S.op("dve", lambda e: e.tensor_tensor(out=tm[:, 0:n], in0=psy[:, 0:n], in1=sg[:, 0:n], op=ALU.mult),
                                     reads=[Bpy, Bsg], writes=[Btm])
                                if nbr == 1:
                                    S.op("pool", lambda e: e.tensor_tensor(out=macc[:, 0:n], in0=macc[:, 0:n], in1=tm[:, 0:n], op=ALU.add),
                                         reads=[Btm, Bmacc], writes=[Bmacc])
                                else:
                                    S.op("pool", lambda e: e.tensor_tensor(out=mT[:, f, 0:n], in0=macc[:, 0:n], in1=tm[:, 0:n], op=ALU.add),
                                         reads=[Btm, Bmacc], writes=[BmT])
                    for tt in range(n // 128):
                        t = a // 128 + tt
                        xt, Bx = xin.get()
                        if l == 0:
                            src = x_d[t * 128:(t + 1) * 128, :] if t < 32 else ctx_d[(t - 32) * 128:(t - 31) * 128, :]
                            S.dma(xt[:], src, writes=[Bx])
                        else:
                            S.dma(xt[:], xs_d[t * 128:(t + 1) * 128, :], reads=[B_xs[t]], writes=[Bx])
                        x1, Bx1 = x1p.get()
                        for half in range(2):
                            pso, Bpo = po.get()
                            S.pe_group([mm(pso[:, :], mT[:, k, tt * 128:(tt + 1) * 128], Wo[:, k, half * 512:(half + 1) * 512],
                                           k == 0, k == 7) for k in range(8)], reads=[BWo, BmT], writes=[Bpo])
                            tm, Btm = tmp.get()
                            S.op("dve", lambda e: e.tensor_tensor(out=tm[:, :], in0=pso[:, :], in1=GA[:, half * 512:(half + 1) * 512],
                                                                  op=ALU.mult), reads=[Bpo, Bg], writes=[Btm])
                            S.op("pool", lambda e: e.tensor_tensor(out=x1[:, half * 512:(half + 1) * 512], in0=tm[:, :],
                                                                   in1=xt[:, half * 512:(half + 1) * 512], op=ALU.add),
                                 reads=[Btm, Bx], writes=[Bx1])
                        S.dma(xs_d[t * 128:(t + 1) * 128, :], x1[:], reads=[Bx1], writes=[B_xs[t]], q="pool")
                        g = norm_gen(P, x1, Bx1, Gf, SHf, Bg, hst, Bhst, tt)
                        next(g)
                        next(g)
                        if pend_n2:
                            for _ in pend_n2.pop():
                                pass
                        pend_n2.append(g)
                    for _ in pend_n2.pop():
                        pass
                    S.dma(hT_d[:, :, a:a + n], hst[:, :, 0:n], reads=[Bhst], writes=hbufs(a, a + n), q="pool")
                S.barrier()

        def phase_ffn(l, need_ctx, last):
            with ExitStack() as es:
                cw = sb(es, "ffcw", [128, 44, 3], F32)
                cb = sb(es, "ffcb", [128, 44], F32)
                GF = sb(es, "ffGF", [128, D], F32)
                Bc = Buf()
                S.dma(cw[:], cw_d[l].rearrange("p (c j) -> p c j", j=3), writes=[Bc])
                S.dma(cb[:], cb_d[l], writes=[Bc])
                Wus = [sb(es, f"ffWu{i}", [128, 8, 2, 1408], BF16) for i in range(2)]
                Wds = [sb(es, f"ffWd{i}", [128, 11, D], BF16) for i in range(2)]
                BWu = [[[Buf() for _ in range(2)] for _ in range(2)] for _ in range(2)]
                BWd = [[Buf() for _ in range(3)] for _ in range(2)]
                for ps_ in range(2):
                    for hf, (j0, j1) in enumerate(((0, 768), (768, 1408))):
                        for gv in range(2):
                            c0 = gv * DFF + ps_ * 1408
                            load_w(Wus[ps_][:, :, gv, j0:j1], w_up_d[l, :, c0 + j0:c0 + j1], BWu[ps_][gv][hf])
                    for ji, j in enumerate(range(0, 11, 4)):
                        je = min(11, j + 4)
                        load_w(Wds[ps_][:, j:je, :], w_dn_d[l, ps_ * 1408 + j * 128:ps_ * 1408 + je * 128, :], BWd[ps_][ji])
                aT = sb(es, "ffaT", [128, 11, 512], BF16)
                BaT = Buf()
                accp = rr(es, "ffacc", [128, 512], F32, 4)
                sgp = rr(es, "ffsg", [128, 512], F32, 2)
                tmp = rr(es, "fftmp", [128, 512], F32, 2)
                xin = rr(es, "ffx", [128, D], F32, 2)
                x1p = rr(es, "ffx1", [128, D], F32, 2)
                pu = RR(psb[0:4])
                po = RR(psb[4:7])
                hbp = rr(es, "ffhb", [128, 8, 512], BF16, 2)
                blocks = [(0, SEQ, i * 510, min(SEQ, (i + 1) * 510)) for i in range(9)]
                if need_ctx:
                    blocks.append((SEQ, T, SEQ, T))
                for ps_ in range(2):
                    Wu, Wd = Wus[ps_], Wds[ps_]
                    cur_row = None
                    def ff_load(bi):
                        s0, s1, a, b = blocks[bi]
                        ua, ub = max(a - 1, s0), min(b + 1, s1)
                        return hload(hbp, ua, ub - ua)
                    nxt = ff_load(0)
                    for bi, (s0, s1, a, b) in enumerate(blocks):
                        row = 0 if s0 == 0 else 1
                        if row != cur_row:
                            load_mod_bcast(GF, Bc, l, row, 5)
                            cur_row = row
                        ua, ub = max(a - 1, s0), min(b + 1, s1)
                        nu = ub - ua
                        n = b - a
                        off = a - ua
                        hT, Bh = nxt
                        if bi + 1 < len(blocks):
                            nxt = ff_load(bi + 1)
                        hb = [Bh]
                        for i in range(11):
                            ci = [ps_ * 11 + i, 22 + ps_ * 11 + i]
                            accs = []
                            for gv in range(2):
                                psu, Bpu = pu.get()
                                S.pe_group([mm(psu[:, 0:nu], Wu[:, k, gv, i * 128:(i + 1) * 128], hT[:, k, 0:nu], k == 0, k == 7)
                                            for k in range(8)], reads=[BWu[ps_][gv][0 if i < 6 else 1]] + hb, writes=[Bpu])
                                acc, Bacc = accp.get()
                                cc = ci[gv]
                                S.op("act", lambda e: e.activation(out=acc[:, 0:n], in_=psu[:, off:off + n], func=AF.Identity,
                                                                   scale=cw[:, cc, 1:2], bias=cb[:, cc:cc + 1]),
                                     reads=[Bpu, Bc], writes=[Bacc])
                                la = max(a, s0 + 1)
                                S.op("dve", lambda e: e.scalar_tensor_tensor(
                                    out=acc[:, la - a:n], in0=psu[:, la - 1 - ua:b - 1 - ua], scalar=cw[:, cc, 0:1],
                                    in1=acc[:, la - a:n], op0=ALU.mult, op1=ALU.add), reads=[Bpu, Bc, Bacc], writes=[Bacc])
                                rb = min(b, s1 - 1)
                                S.op("dve", lambda e: e.scalar_tensor_tensor(
                                    out=acc[:, 0:rb - a], in0=psu[:, a + 1 - ua:rb + 1 - ua], scalar=cw[:, cc, 2:3],
                                    in1=acc[:, 0:rb - a], op0=ALU.mult, op1=ALU.add), reads=[Bpu, Bc, Bacc], writes=[Bacc])
                                accs.append((acc, Bacc))
                            sg, Bsg = sgp.get()
                            S.op("act", lambda e: e.activation(out=sg[:, 0:n], in_=accs[0][0][:, 0:n], func=AF.Silu),
                                 reads=[accs[0][1]], writes=[Bsg])
                            S.op("pool", lambda e: e.tensor_tensor(out=aT[:, i, 0:n], in0=sg[:, 0:n], in1=accs[1][0][:, 0:n],
                                                                   op=ALU.mult), reads=[Bsg, accs[1][1]], writes=[BaT])
                        for m0 in range(0, n, 128):
                            msz = min(128, n - m0)
                            ta = a + m0
                            xt, Bx = xin.get()
                            if l == 0 and ps_ == 0:
                                pass
                            S.dma(xt[0:msz, :], xs_d[ta:ta + msz, :], reads=xbufs(ta, ta + msz), writes=[Bx])
                            x1, Bx1 = x1p.get()
                            for half in range(2):
                                pso, Bpo = po.get()
                                S.pe_group([mm(pso[0:msz, :], aT[:, i, m0:m0 + msz], Wd[:, i, half * 512:(half + 1) * 512],
                                               i == 0, i == 10) for i in range(11)], reads=BWd[ps_] + [BaT], writes=[Bpo])
                                tm, Btm = tmp.get()
                                S.op("dve", lambda e: e.tensor_tensor(out=tm[0:msz, :], in0=pso[0:msz, :],
                                                                      in1=GF[0:msz, half * 512:(half + 1) * 512], op=ALU.mult),
                                     reads=[Bpo, Bc], writes=[Btm])
                                S.op("pool", lambda e: e.tensor_tensor(out=x1[0:msz, half * 512:(half + 1) * 512], in0=tm[0:msz, :],
                                                                       in1=xt[0:msz, half * 512:(half + 1) * 512], op=ALU.add),
                                     reads=[Btm, Bx], writes=[Bx1])
                            if last and ps_ == 1:
                                S.dma(out_d[ta:ta + msz, :], x1[0:msz, :], reads=[Bx1], writes=[Buf()], q="pool")
                            else:
                                S.dma(xs_d[ta:ta + msz, :], x1[0:msz, :], reads=[Bx1], writes=xbufs(ta, ta + msz), q="pool")
                S.barrier()

        def dump_hT(l):
            pass

        for l in range(layers):
            need_ctx = l < DEPTH - 1
            phase_ada(l)
            phase_n1(l)
            dump_hT(l)
            if PH["na"]:
                phase_na(l, need_ctx)
            if PH["sw"]:
                phase_sw(l, need_ctx)
            if PH["mla"]:
                phase_mla(l, need_ctx)
            if PH["merge"]:
                phase_merge(l, need_ctx)
            if PH["ffn"]:
                phase_ffn(l, need_ctx, l == DEPTH - 1)
        S.barrier()
    nc._sched_stats = (S.n_ops, S.n_waits, S.nsem)
    return nc


PH = {"na": True, "sw": True, "mla": True, "merge": True, "ffn": True}


def _consts():
    ident = np.eye(128, dtype=np.float32)
    blockones = np.zeros((128, 128), np.float32)
    blockones[0:64, 0:64] = 1
    blockones[64:128, 64:128] = 1
    allones = np.ones((128, 128), np.float32)

    def partner64(d):
        return d + 16 if (d % 32) < 16 else d - 16
    perm_sw = np.zeros((128, 128), np.float32)
    for i in range(128):
        base = (i // 64) * 64
        perm_sw[base + partner64(i % 64), i] = 1
    perm_ml = np.zeros((128, 128), np.float32)
    for i in range(64, 96):
        dd = i - 64
        p = dd + 8 if (dd % 16) < 8 else dd - 8
        perm_ml[64 + p, i] = 1
    shiftm = np.zeros((128, 128), np.float32)
    for k in range(32):
        shiftm[k, 64 + k] = 1
    cmats = np.stack([ident, blockones, allones, perm_sw, perm_ml, shiftm])
    t = np.arange(SEQ)
    row = (t // GRID).astype(np.float32)
    col = (t % GRID).astype(np.float32)
    cs_sw = np.zeros((2, 128, SEQ), np.float32)
    inv16 = (10000.0 ** (-np.arange(16, dtype=np.float32) / 16)).astype(np.float32)
    for p in range(128):
        d = p % 64
        pos = row if d < 32 else col
        i = d % 16
        ang = (pos * inv16[i]).astype(np.float32)
        cs_sw[0, p] = np.cos(ang)
        cs_sw[1, p] = -np.sin(ang) if (d % 32) < 16 else np.sin(ang)
    cs_ml = np.zeros((2, 128, SEQ), np.float32)
    cs_ml[0, :, :] = 1.0
    inv8 = (10000.0 ** (-np.arange(8, dtype=np.float32) / 8)).astype(np.float32)
    for p in range(64, 96):
        dd = p - 64
        pos = row if dd < 16 else col
        i = dd % 8
        ang = (pos * inv8[i]).astype(np.float32)
        cs_ml[0, p] = np.cos(ang)
        cs_ml[1, p] = -np.sin(ang) if (dd % 16) < 8 else np.sin(ang)
    j = np.arange(128)[:, None]
    i = np.arange(128)[None, :]
    msk = np.stack([np.where(j >= i, 0.0, NEG), np.where(j <= i, 0.0, NEG)]).astype(np.float32)
    qc = np.arange(64)[None, :]
    kc = np.arange(64)[:, None]
    c0 = np.clip(qc - 8, 0, 48)
    inwin = (kc >= c0) & (kc < c0 + 16)
    negm = np.where(inwin, 0.0, NEG).astype(np.float32)
    negm2 = np.full((128, 22, 64), NEG, np.float32)
    for t in range(22):
        for half, dr in ((0, 17 - t), (1, 18 - t)):
            if 3 <= dr <= 10:
                negm2[half * 64:(half + 1) * 64, t, :] = negm
    negm2 = negm2.reshape(128, 22 * 64)
    negm = np.concatenate([negm, negm], axis=0)
    return dict(cmats=cmats, cs_sw=cs_sw, cs_ml=cs_ml, msk_sw=msk, negm=negm, negm2=negm2)


def _layouts(inp):
    w_in = np.ascontiguousarray(inp["w_in"]).copy()
    perm = [0, 4, 1, 5, 2, 6, 3, 7]
    swq = w_in[:, :, C_SW:C_SW + 512].reshape(DEPTH, D, 8, 64)[:, :, perm, :].reshape(DEPTH, D, 512)
    w_in[:, :, C_SW:C_SW + 512] = swq
    rpb = inp["na_rpb"]
    kc = np.arange(64)[:, None]
    qc = np.arange(64)[None, :]
    dc = np.clip(kc - qc, -15, 15) + 15
    rpbg = np.zeros((DEPTH, 128, 8, 16, 64), np.float32)
    for s in range(16):
        dr_lo, dr_up = s - 1, s
        if 0 <= dr_lo <= 14:
            rpbg[:, 0:64, :, s, :] = np.transpose(rpb[:, :, dr_lo, :][:, :, dc], (0, 2, 1, 3))
        if 0 <= dr_up <= 14:
            rpbg[:, 64:128, :, s, :] = np.transpose(rpb[:, :, dr_up, :][:, :, dc], (0, 2, 1, 3))
    rpbg = rpbg.reshape(DEPTH, 128, 8 * 16 * 64)
    rpbg2 = np.zeros((DEPTH, 128, 8, 22, 64), np.float32)
    for t in range(22):
        dr_lo, dr_up = 17 - t, 18 - t
        if 0 <= dr_lo <= 14:
            rpbg2[:, 0:64, :, t, :] = np.transpose(rpb[:, :, dr_lo, :][:, :, dc], (0, 2, 1, 3))
        if 0 <= dr_up <= 14:
            rpbg2[:, 64:128, :, t, :] = np.transpose(rpb[:, :, dr_up, :][:, :, dc], (0, 2, 1, 3))
    rpbg2 = rpbg2.reshape(DEPTH, 128, 8 * 22 * 64)
    svec = np.zeros((DEPTH, 128, 16), np.float32)
    svec[:, :, 0] = np.tile(inp["na_q_norm"], (1, 2))
    svec[:, :, 1] = np.tile(inp["na_k_norm"], (1, 2))
    svec[:, :, 2] = np.tile(inp["sw_q_norm"], (1, 2))
    svec[:, :, 3] = np.tile(inp["sw_k_norm"], (1, 2))
    svec[:, :, 4:7] = inp["mla_q_rank_norm"].reshape(DEPTH, 3, 128).transpose(0, 2, 1)
    svec[:, :, 7:9] = inp["mla_kv_rank_norm"].reshape(DEPTH, 2, 128).transpose(0, 2, 1)
    svec[:, 0:96, 9] = inp["mla_q_norm"]
    svec[:, 0:96, 10] = inp["mla_k_norm"]
    cw = inp["conv_w"].reshape(DEPTH, 3, 44, 128).transpose(0, 3, 2, 1).reshape(DEPTH, 128, 44 * 3)
    cb = inp["conv_b"].reshape(DEPTH, 44, 128).transpose(0, 2, 1)
    f = lambda a: np.ascontiguousarray(a, dtype=np.float32)
    return dict(w_in=f(w_in), rpbg=f(rpbg), rpbg2=f(rpbg2), svec=f(svec), cw=f(cw), cb=f(cb))


_CACHE = {}


def _in_maps(inp):
    consts = _consts()
    lay = _layouts(inp)
    f = lambda a: np.ascontiguousarray(a, dtype=np.float32)
    shared = dict(w_ada=f(inp["w_ada"]), b_ada=f(inp["b_ada"]), g_mix=f(inp["g_mix"]), g_ffn=f(inp["g_ffn"]),
                  sw_sink=f(inp["sw_sink"]), w_uq=f(inp["w_uq"]), w_ukv=f(inp["w_ukv"]), w_branch=f(inp["w_branch"]),
                  w_out=f(inp["w_out"]), w_up=f(inp["w_up"]), w_down=f(inp["w_down"]))
    shared.update(lay)
    shared.update(consts)
    maps = []
    for b in range(8):
        m = dict(shared)
        m["x"] = f(inp["x"][b])
        m["ctx"] = f(inp["ctx"][b])
        cc = np.stack([inp["c"][b], inp["c_ctx"]], axis=-1)
        m["cT"] = f(cc.reshape(8, 128, 2).transpose(1, 0, 2))
        maps.append(m)
    return maps


def kernel(**inputs):
    inp = {k: np.asarray(v) for k, v in inputs.items()}
    if "nc" not in _CACHE:
        _CACHE["nc"] = build_program(DEPTH, False)
    nc = _CACHE["nc"]
    maps = _in_maps(inp)
    res = run_bass_kernel_spmd(nc, maps, core_ids=list(range(8)))
    out = np.stack([np.asarray(r["out"], dtype=np.float32) for r in res.results], axis=0)
    return out
```

```python
import numpy as np
from contextlib import ExitStack
import concourse.bass as bass
import concourse.mybir as mybir
from concourse.bass_utils import run_bass_kernel_spmd

F32 = mybir.dt.float32
BF16 = mybir.dt.bfloat16
AF = mybir.ActivationFunctionType
ALU = mybir.AluOpType

D = 1024
SEQ = 4096
CTX = 256
T = SEQ + CTX
NT = T // 128
DEPTH = 2
D_IN = 6048
DFF = 2816
GRID = 64
NEG = -30000.0
EPS = 1e-6
C_NA = 0
C_SW = 1536
C_ML = 2304
C_G = 2976
SEM_EPOCH = 8000

DEBUG = False
LAYERS = DEPTH


class Buf:
    __slots__ = ("name", "w", "r")

    def __init__(self, name=""):
        self.name = name
        self.w = None
        self.r = {}


class Sched:
    def __init__(self, nc, n_dma_sems=24):
        self.nc = nc
        self.eng = {"pe": nc.tensor, "act": nc.scalar, "dve": nc.vector,
                    "pool": nc.gpsimd, "sp": nc.sync}
        self.cur = {}
        self.cnt = {}
        self.nsem = 0
        self.known = {e: {} for e in self.eng}
        self.allsems = []
        for e in self.eng:
            self._new_sem(e)
        self.dma_sems = []
        for i in range(n_dma_sems):
            s = nc.alloc_semaphore(f"dq{i}")
            self.dma_sems.append([s, 0])
        self.dma_i = 0
        self.bar_sem = nc.alloc_semaphore("barsem")
        self.bar_n = 0
        self.n_ops = 0
        self.n_waits = 0

    def _new_sem(self, e):
        self.cur[e] = self.nc.alloc_semaphore(f"s_{e}_{self.nsem}")
        self.nsem += 1
        self.cnt[e] = 0

    def _wait(self, e, tk):
        sem, val = tk
        k = id(sem)
        kn = self.known[e]
        if kn.get(k, 0) >= val:
            return
        self.eng[e].wait_ge(sem, val)
        kn[k] = val
        self.n_waits += 1

    def _deps(self, e, reads, writes, skip_same_engine=False):
        deps = {}
        for b in reads:
            tk = b.w
            if tk is not None:
                k = id(tk[0])
                if k not in deps or deps[k][1] < tk[1]:
                    deps[k] = tk
        for b in writes:
            tk = b.w
            if tk is not None:
                k = id(tk[0])
                if k not in deps or deps[k][1] < tk[1]:
                    deps[k] = tk
            for k, tk in b.r.items():
                if k not in deps or deps[k][1] < tk[1]:
                    deps[k] = tk
        for tk in deps.values():
            if skip_same_engine and tk[0] is self.cur[e]:
                continue
            self._wait(e, tk)

    def _commit(self, tk, reads, writes):
        k = id(tk[0])
        for b in reads:
            b.r[k] = tk
        for b in writes:
            b.w = tk
            b.r = {}

    def _ticket(self, e, ins):
        if self.cnt[e] >= SEM_EPOCH:
            self._new_sem(e)
        self.cnt[e] += 1
        tk = (self.cur[e], self.cnt[e])
        ins.then_inc(tk[0], 1)
        return tk

    def op(self, e, fn, reads=(), writes=()):
        self._deps(e, reads, writes, skip_same_engine=(e == "pe"))
        ins = fn(self.eng[e])
        self.n_ops += 1
        tk = self._ticket(e, ins)
        self._commit(tk, reads, writes)
        return tk

    def pe_group(self, fns, reads=(), writes=()):
        self._deps("pe", reads, writes, skip_same_engine=True)
        ins = None
        for fn in fns:
            ins = fn(self.eng["pe"])
        self.n_ops += len(fns)
        tk = self._ticket("pe", ins)
        self._commit(tk, reads, writes)
        return tk

    def dma(self, out, in_, reads=(), writes=(), q="sp"):
        slot = self.dma_sems[self.dma_i % len(self.dma_sems)]
        self.dma_i += 1
        sem, v = slot
        if v > 0:
            self._wait(q, (sem, v))
        self._deps(q, reads, writes)
        ins = self.eng[q].dma_start(out=out, in_=in_)
        slot[1] = v + 16
        tk = (sem, v + 16)
        ins.then_inc(sem, 16)
        self._commit(tk, reads, writes)
        self.n_ops += 1
        return tk

    def barrier(self):
        for sem, v in self.dma_sems:
            if v > 0:
                self._wait("sp", (sem, v))
        for e in self.eng:
            if e != "sp" and self.cnt[e] > 0:
                self._wait("sp", (self.cur[e], self.cnt[e]))
        self.bar_n += 1
        self.eng["sp"].sem_inc(self.bar_sem, 1)
        for e in self.eng:
            if e == "sp":
                continue
            self.eng[e].wait_ge(self.bar_sem, self.bar_n)
            kn = self.known[e]
            for sem, v in self.dma_sems:
                kn[id(sem)] = v
            for f in self.eng:
                kn[id(self.cur[f])] = self.cnt[f]


class RR:
    def __init__(self, tiles):
        self.t = tiles
        self.b = [Buf() for _ in tiles]
        self.i = 0

    def get(self):
        j = self.i % len(self.t)
        self.i += 1
        return self.t[j], self.b[j]


def mm(out, lhsT, rhs, start, stop):
    return lambda e: e.matmul(out, lhsT=lhsT, rhs=rhs, start=start, stop=stop)


def build_program(layers=DEPTH, debug=False):
    nc = bass.Bass("TRN2", target_bir_lowering=False)
    S = Sched(nc)
    okind = "ExternalOutput" if debug else "Internal"

    def din(name, shape, dt=F32):
        return nc.dram_tensor(name, list(shape), dt, kind="ExternalInput")

    x_d = din("x", [SEQ, D]).ap()
    ctx_d = din("ctx", [CTX, D]).ap()
    cT_d = din("cT", [128, 8, 2]).ap()
    w_ada_d = din("w_ada", [DEPTH, D, 6 * D]).ap()
    b_ada_h = din("b_ada", [DEPTH, 6 * D])
    g_mix_h = din("g_mix", [DEPTH, D])
    g_ffn_h = din("g_ffn", [DEPTH, D])
    w_in_d = din("w_in", [DEPTH, D, D_IN]).ap()
    rpbg_d = din("rpbg", [DEPTH, 128, 8 * 16 * 64]).ap()
    rpbg2_d = din("rpbg2", [DEPTH, 128, 8 * 22 * 64]).ap()
    negm2_d = din("negm2", [128, 22 * 64]).ap()
    svec_d = din("svec", [DEPTH, 128, 16]).ap()
    sink_h = din("sw_sink", [DEPTH, 8])
    w_uq_d = din("w_uq", [DEPTH, 384, 768]).ap()
    w_ukv_d = din("w_ukv", [DEPTH, 256, 1024]).ap()
    w_br_d = din("w_branch", [DEPTH, 3, 512, D]).ap()
    w_out_d = din("w_out", [DEPTH, D, D]).ap()
    w_up_d = din("w_up", [DEPTH, D, 2 * DFF]).ap()
    cw_d = din("cw", [DEPTH, 128, 44 * 3]).ap()
    cb_d = din("cb", [DEPTH, 128, 44]).ap()
    w_dn_d = din("w_down", [DEPTH, DFF, D]).ap()
    cm_d = din("cmats", [6, 128, 128]).ap()
    cs_sw_d = din("cs_sw", [2, 128, SEQ]).ap()
    cs_ml_d = din("cs_ml", [2, 128, SEQ]).ap()
    msk_sw_d = din("msk_sw", [2, 128, 128]).ap()
    negm_d = din("negm", [128, 64]).ap()

    out_d = nc.dram_tensor("out", [SEQ, D], F32, kind="ExternalOutput").ap()
    mod_h = nc.dram_tensor("mod_s", [DEPTH, 2, 6 * D], F32, kind=okind)
    mod_d = mod_h.ap()
    xs_d = nc.dram_tensor("xs_s", [T, D], F32, kind=okind).ap()
    nod = DEPTH if debug else 1
    o_d = nc.dram_tensor("o_s", [nod, 3, 512, T], BF16, kind=okind).ap()
    hT_d = nc.dram_tensor("hT_s", [128, 8, T], BF16, kind=okind).ap()
    NRS = 6
    rs_h = nc.dram_tensor("rs_s", [NRS, 512], F32, kind="Internal")
    rs_d = rs_h.ap()
    B_rsd = [Buf() for _ in range(NRS)]
    rs_i = [0]

    B_mod = [Buf() for _ in range(DEPTH)]
    B_xs = [Buf() for _ in range(NT)]
    B_o = [[Buf() for _ in range(NT)] for _ in range(3)]
    B_out = Buf()

    _uid = [0]

    def sb(es, name, shape, dt):
        _uid[0] += 1
        return es.enter_context(nc.sbuf_tensor(f"sb_{name}_{_uid[0]}", list(shape), dt))

    def rr(es, name, shape, dt, n):
        return RR([sb(es, f"{name}{i}", shape, dt) for i in range(n)])

    def bcast_ap(handle, offset, n):
        return bass.AP(tensor=handle, offset=offset, ap=[[0, 128], [1, n]])

    with ExitStack() as top:
        psb = [top.enter_context(nc.psum_tensor(f"psb{i}", [128, 512], F32)) for i in range(7)]
        psT = top.enter_context(nc.psum_tensor("psT", [128, 1024], BF16))
        B_psT = Buf()
        B_hT = [Buf() for _ in range(NT)]
        cm = sb(top, "cm", [128, 6, 128], BF16)
        B_cm = Buf()
        S.dma(cm[:], cm_d.rearrange("m p n -> p m n"), writes=[B_cm], q="pool")
        ident = cm[:, 0, :]
        blockones = cm[:, 1, :]
        allones = cm[:, 2, :]
        perm_sw = cm[:, 3, :]
        perm_ml = cm[:, 4, :]
        shiftm = cm[:, 5, :]
        onesf = sb(top, "onesf", [128, 64], F32)
        epst = sb(top, "epst", [128, 1], F32)
        B_const = Buf()
        S.op("dve", lambda e: e.memset(onesf[:], 1.0), writes=[B_const])
        S.op("dve", lambda e: e.memset(epst[:], EPS), writes=[B_const])
        svec = sb(top, "svec", [128, 16], F32)
        svq = sb(top, "svq", [128, 16], F32)
        B_sv = Buf()

        def hbufs(a, b):
            return B_hT[a // 128:(b + 127) // 128]

        def xbufs(a, b):
            return B_xs[a // 128:(b + 127) // 128]

        def hload(pool, a, n):
            ht, Bht = pool.get()
            S.dma(ht[:, :, 0:n], hT_d[:, :, a:a + n], reads=hbufs(a, a + n), writes=[Bht])
            return ht, Bht

        def hstore(hst, Bhst, a, n):
            S.dma(hT_d[:, :, a:a + n], hst[:, :, 0:n], reads=[Bhst], writes=hbufs(a, a + n))

        def norm_to_hT(es_pools, xt, Bx, G, SH, Bg, hst, Bhst, slot):
            for _ in norm_gen(es_pools, xt, Bx, G, SH, Bg, hst, Bhst, slot):
                pass

        def norm_gen(es_pools, xt, Bx, G, SH, Bg, hst, Bhst, slot, after=None):
            P = es_pools
            st, Bst = P["stat"].get()
            jk, Bjk = P["junk"].get()
            S.op("act", lambda e: e.activation(out=jk[:], in_=xt[:], func=AF.Square, accum_out=st[:, 0:1]),
                 reads=[Bx], writes=[Bjk, Bst])
            S.op("act", lambda e: e.activation(out=st[:, 1:2], in_=st[:, 0:1], func=AF.Sqrt,
                                               scale=1.0 / D, bias=epst[:, 0:1]),
                 reads=[Bst, B_const], writes=[Bst])
            yield
            S.op("dve", lambda e: e.reciprocal(out=st[:, 2:3], in_=st[:, 1:2]), reads=[Bst], writes=[Bst])
            tm, Btm = P["tmpf"].get()
            S.op("dve", lambda e: e.scalar_tensor_tensor(out=tm[:], in0=xt[:], scalar=st[:, 2:3], in1=G[:],
                                                         op0=ALU.mult, op1=ALU.mult),
                 reads=[Bx, Bst, Bg], writes=[Btm])
            hb, Bhb = P["hb"].get()
            S.op("pool", lambda e: e.tensor_tensor(out=hb[:], in0=tm[:], in1=SH[:], op=ALU.add),
                 reads=[Btm, Bg], writes=[Bhb])
            yield
            S.pe_group([(lambda e, k=k: e.transpose(psT[:, k * 128:(k + 1) * 128], hb[:, k * 128:(k + 1) * 128], ident))
                        for k in range(8)], reads=[Bhb, B_cm], writes=[B_psT])
            S.op("act", lambda e: e.activation(out=hst[:, :, slot * 128:(slot + 1) * 128],
                                               in_=psT[:, :].rearrange("p (k n) -> p k n", k=8), func=AF.Identity),
                 reads=[B_psT], writes=[Bhst])
            if after is not None:
                after()

        def load_mod_bcast(tile, Bt, l, row, which):
            S.dma(tile[:], bcast_ap(mod_h, (l * 2 + row) * 6 * D + which * D, D), reads=[B_mod[l]], writes=[Bt])

        def make_G(es, l, row, which_scale, ghandle, name):
            Gt = sb(es, name, [128, D], F32)
            Bg = Buf()
            gt = sb(es, name + "g", [128, D], F32)
            Bgt = Buf()
            load_mod_bcast(Gt, Bg, l, row, which_scale)
            S.dma(gt[:], bcast_ap(ghandle, l * D, D), writes=[Bgt])
            S.op("dve", lambda e: e.scalar_tensor_tensor(out=Gt[:], in0=Gt[:], scalar=1.0, in1=gt[:],
                                                         op0=ALU.add, op1=ALU.mult),
                 reads=[Bg, Bgt], writes=[Bg])
            return Gt, Bg

        def norm_pools(es, depth=2):
            return {"stat": rr(es, "nstat", [128, 4], F32, depth + 1),
                    "junk": rr(es, "njunk", [128, D], BF16, depth),
                    "tmpf": rr(es, "ntmpf", [128, D], F32, depth),
                    "hb": rr(es, "nhb", [128, D], BF16, depth)}

        def load_w(dst, src_rows, Bw):
            S.dma(dst, src_rows.rearrange("(kc p) n -> p kc n", p=128), writes=[Bw], q="pool")

        def phase_ada(l):
            with ExitStack() as es:
                cT = sb(es, "cT", [128, 8, 2], F32)
                Bc = Buf()
                S.dma(cT[:], cT_d, writes=[Bc])
                S.op("act", lambda e: e.activation(out=cT[:], in_=cT[:], func=AF.Silu), reads=[Bc], writes=[Bc])
                bada = sb(es, "bada", [2, 6 * D], F32)
                Bb = Buf()
                S.dma(bada[:], bass.AP(tensor=b_ada_h, offset=l * 6 * D, ap=[[0, 2], [1, 6 * D]]), writes=[Bb])
                modsb = sb(es, "modsb", [2, 6 * D], F32)
                Bm = Buf()
                wa = rr(es, "wa", [128, 8, 512], F32, 4)
                pp = RR(psb[0:2])
                for j in range(12):
                    wt, Bw = wa.get()
                    S.dma(wt[:], w_ada_d[l, :, j * 512:(j + 1) * 512].rearrange("(kc p) n -> p kc n", p=128),
                          writes=[Bw])
                    ps, Bp = pp.get()
                    S.pe_group([mm(ps[0:2, :], cT[:, k, :], wt[:, k, :], k == 0, k == 7) for k in range(8)],
                               reads=[Bc, Bw], writes=[Bp])
                    S.op("dve", lambda e: e.tensor_tensor(out=modsb[:, j * 512:(j + 1) * 512], in0=ps[0:2, :],
                                                          in1=bada[:, j * 512:(j + 1) * 512], op=ALU.add),
                         reads=[Bp, Bb], writes=[Bm])
                S.dma(mod_d[l], modsb[:], reads=[Bm], writes=[B_mod[l]])
                S.dma(svec[:], svec_d[l], writes=[B_sv])
                S.op("dve", lambda e: e.tensor_scalar(out=svq[:, 0:4], in0=svec[:, 0:4], scalar1=0.125, scalar2=None,
                                                      op0=ALU.mult), reads=[B_sv], writes=[B_sv])
                S.op("dve", lambda e: e.tensor_scalar(out=svq[:, 9:10], in0=svec[:, 9:10], scalar1=96.0 ** -0.5,
                                                      scalar2=None, op0=ALU.mult), reads=[B_sv], writes=[B_sv])
                S.barrier()

        def phase_n1(l):
            with ExitStack() as es:
                P = norm_pools(es, 4)
                xin = rr(es, "n1x", [128, D], F32, 6)
                Gl, Bgl = make_G(es, l, 0, 1, g_mix_h, "n1Gl")
                Gc, Bgc = make_G(es, l, 1, 1, g_mix_h, "n1Gc")
                SHl = sb(es, "n1SHl", [128, D], F32)
                SHc = sb(es, "n1SHc", [128, D], F32)
                load_mod_bcast(SHl, Bgl, l, 0, 0)
                load_mod_bcast(SHc, Bgc, l, 1, 0)
                hstp = rr(es, "n1hst", [128, 8, 512], BF16, 3)

                def n1_task(t, hst, Bhst):
                    xt, Bx = xin.get()
                    if l == 0:
                        src = x_d[t * 128:(t + 1) * 128, :] if t < 32 else ctx_d[(t - 32) * 128:(t - 31) * 128, :]
                        S.dma(xt[:], src, writes=[Bx])
                    else:
                        S.dma(xt[:], xs_d[t * 128:(t + 1) * 128, :], reads=[B_xs[t]], writes=[Bx])
                    after = None
                    if t % 4 == 3 or t == NT - 1:
                        a0 = (t // 4) * 512
                        after = (lambda: hstore(hst, Bhst, a0, (t + 1) * 128 - a0))
                    if t < 32:
                        yield from norm_gen(P, xt, Bx, Gl, SHl, Bgl, hst, Bhst, t % 4, after)
                    else:
                        yield from norm_gen(P, xt, Bx, Gc, SHc, Bgc, hst, Bhst, t % 4, after)
                tasks = []
                for t in range(NT):
                    if t % 4 == 0:
                        hst, Bhst = hstp.get()
                    tasks.append(n1_task(t, hst, Bhst))
                run_tasks(tasks)
                S.barrier()

        def run_tasks(gens, depth=3):
            active = []
            it = iter(gens)
            while True:
                while len(active) < depth:
                    g = next(it, None)
                    if g is None:
                        break
                    active.append(g)
                if not active:
                    break
                for g in list(active):
                    try:
                        next(g)
                    except StopIteration:
                        active.remove(g)

        def qk_task(P, mmf, mrd, npart, n, onesm, inv_d, gcol, dst, Bdst, rope=None):
            ps, Bps = P["ps1"].get()
            S.pe_group(mmf(ps), reads=mrd, writes=[Bps])
            sq, Bsq = P["sq"].get()
            S.op("act", lambda e: e.activation(out=sq[0:npart, 0:n], in_=ps[0:npart, 0:n], func=AF.Square),
                 reads=[Bps], writes=[Bsq])
            yield
            p2, Bp2 = P["ps2"].get()
            S.pe_group([mm(p2[0:npart, 0:n], onesm[0:npart, 0:npart], sq[0:npart, 0:n], True, True)],
                       reads=[Bsq, B_cm], writes=[Bp2])
            yield
            rt, Brt = P["rt"].get()
            S.op("act", lambda e: e.activation(out=rt[0:npart, 0:n], in_=p2[0:npart, 0:n], func=AF.Ln,
                                               scale=inv_d, bias=epst[0:npart, 0:1]),
                 reads=[Bp2, B_const], writes=[Brt])
            yield
            S.op("act", lambda e: e.activation(out=rt[0:npart, 0:n], in_=rt[0:npart, 0:n], func=AF.Exp, scale=-0.5),
                 reads=[Brt], writes=[Brt])
            if rope is None:
                S.op("dve", lambda e: e.scalar_tensor_tensor(out=dst, in0=ps[0:npart, 0:n], scalar=gcol,
                                                             in1=rt[0:npart, 0:n], op0=ALU.mult, op1=ALU.mult),
                     reads=[Bps, Brt, B_sv], writes=[Bdst])
                return
            perm, cos_ap, sin_ap, Bcs = rope
            qn, Bqn = P["qn"].get()
            S.op("dve", lambda e: e.scalar_tensor_tensor(out=qn[0:npart, 0:n], in0=ps[0:npart, 0:n], scalar=gcol,
                                                         in1=rt[0:npart, 0:n], op0=ALU.mult, op1=ALU.mult),
                 reads=[Bps, Brt, B_sv], writes=[Bqn])
            yield
            p3, Bp3 = P["ps2"].get()
            S.pe_group([mm(p3[0:npart, 0:n], perm[0:npart, 0:npart], qn[0:npart, 0:n], True, True)],
                       reads=[Bqn, B_cm], writes=[Bp3])
            t1, Bt1 = P["rt"].get()
            S.op("pool", lambda e: e.tensor_tensor(out=t1[0:npart, 0:n], in0=qn[0:npart, 0:n], in1=cos_ap, op=ALU.mult),
                 reads=[Bqn, Bcs], writes=[Bt1])
            yield
            t2, Bt2 = P["rt"].get()
            S.op("dve", lambda e: e.tensor_tensor(out=t2[0:npart, 0:n], in0=p3[0:npart, 0:n], in1=sin_ap, op=ALU.mult),
                 reads=[Bp3, Bcs], writes=[Bt2])
            S.op("pool", lambda e: e.tensor_tensor(out=dst, in0=t1[0:npart, 0:n], in1=t2[0:npart, 0:n], op=ALU.add),
                 reads=[Bt1, Bt2], writes=[Bdst])

        def v_task(P, mmf, mrd, ncol, dst, Bdst):
            ps, Bps = P["ps1"].get()
            S.pe_group(mmf(ps), reads=mrd, writes=[Bps])
            S.op("act", lambda e: e.activation(out=dst, in_=ps[:, 0:ncol].rearrange("p (h d) -> p h d", d=64),
                                               func=AF.Identity), reads=[Bps], writes=[Bdst])
            yield

        def proj_pools(es):
            return {"sq": rr(es, "psq", [128, 512], BF16, 4),
                    "rt": rr(es, "prt", [128, 512], F32, 8),
                    "qn": rr(es, "pqn", [128, 512], BF16, 4),
                    "ps1": RR(psb[0:3]),
                    "ps2": RR(psb[3:7])}

        def attn_pools(es, nsc):
            return {"sc": RR(psb[0:nsc]), "acc": RR(psb[4:7]),
                    "pT": rr(es, "apT", [128, 512], BF16, nsc + 2),
                    "rs": rr(es, "ars", [128, 512], F32, 3),
                    "bcs": rr(es, "abcs", [64, 512], F32, 3)}

        pend_pv = []
        LOOK = 3
        FILL = [0, None]

        def flush_pv(keep=0):
            while len(pend_pv) > keep:
                fns, rds, BpsO_ = pend_pv.pop(0)
                S.pe_group(fns, reads=rds, writes=[BpsO_])

        def attend_groups(P, psO, BpsO, col0, n, groups, first):
            started = not first
            ng = len(groups)
            for gi, grp in enumerate(groups):
                sc, Bsc = P["sc"].get()
                fns = []
                rds = []
                for ti, tl in enumerate(grp):
                    fns += tl[0](sc[:, ti * n:(ti + 1) * n])
                    rds += tl[1]
                S.pe_group(fns, reads=rds, writes=[Bsc])
                pT, BpT = P["pT"].get()
                w = len(grp) * n
                S.op("act", lambda e: e.activation(out=pT[:, 0:w], in_=sc[:, 0:w], func=AF.Exp),
                     reads=[Bsc], writes=[BpT])
                for ti, tl in enumerate(grp):
                    if len(tl) > 5 and tl[5] is not None:
                        b_ap, b_rd = tl[5]
                        S.op("dve", lambda e: e.tensor_tensor(out=pT[:, ti * n:(ti + 1) * n], in0=pT[:, ti * n:(ti + 1) * n],
                                                              in1=b_ap, op=ALU.mult), reads=[BpT] + b_rd, writes=[BpT])
                fns = []
                rds = [BpT]
                for ti, tl in enumerate(grp):
                    (p0, p1), vl, vrd = tl[2], tl[3], tl[4]
                    last = (gi == ng - 1) and (ti == len(grp) - 1)
                    fns.append(mm(psO[0:65, col0:col0 + n], vl, pT[p0:p1, ti * n:(ti + 1) * n], not started, last))
                    started = True
                    rds += vrd
                pend_pv.append((fns, rds, BpsO))
                flush_pv(LOOK)
                for _ in range(FILL[0]):
                    nc.tensor.matmul(psb[3][:, :], lhsT=ident, rhs=FILL[1], start=True, stop=True)

        pend_norm = []

        def flush_norm():
            while pend_norm:
                pend_norm.pop(0)()

        def normalize(P, psO, BpsO, n, dst, Bdst, sink_ap=None):
            flush_pv()
            rs, Brs = P["rs"].get()
            if sink_ap is not None:
                S.op("act", lambda e: e.activation(out=rs[64:65, 0:n], in_=psO[64:65, 0:n], func=AF.Ln, bias=sink_ap),
                     reads=[BpsO, B_sv], writes=[Brs])
            else:
                S.op("act", lambda e: e.activation(out=rs[64:65, 0:n], in_=psO[64:65, 0:n], func=AF.Ln),
                     reads=[BpsO], writes=[Brs])
            S.op("act", lambda e: e.activation(out=rs[64:65, 0:n], in_=rs[64:65, 0:n], func=AF.Exp, scale=-1.0),
                 reads=[Brs], writes=[Brs])
            slot = rs_i[0] % NRS
            rs_i[0] += 1
            S.dma(rs_d[slot:slot + 1, 0:n], rs[64:65, 0:n], reads=[Brs], writes=[B_rsd[slot]])
            bc, Bbc = P["bcs"].get()
            S.dma(bc[0:64, 0:n], bass.AP(tensor=rs_h, offset=slot * 512, ap=[[0, 64], [1, n]]),
                  reads=[B_rsd[slot]], writes=[Bbc])
            flush_norm()
            pend_norm.append(lambda: S.op("dve", lambda e: e.tensor_tensor(out=dst, in0=psO[0:64, 0:n], in1=bc[0:64, 0:n],
                                                                           op=ALU.mult),
                                          reads=[BpsO, Bbc], writes=[Bdst]))

        def close_acc(psO, BpsO, col0, n, vl_dummy=None):
            pass

        def store_o(l, br, ost, Bost, a, n):
            li = l if debug else 0
            flush_norm()
            dst = o_d[li, br, :, a:a + n].rearrange("(h d) n -> d h n", d=64)
            S.dma(dst, ost[0:64, :, 0:n], reads=[Bost], writes=B_o[br][a // 128:(a + n + 127) // 128])

        def phase_na(l, need_ctx):
            with ExitStack() as es:
                Wt = sb(es, "naW", [128, 8, 1536], BF16)
                Bw = Buf()
                for c3 in range(3):
                    load_w(Wt[:, :, c3 * 512:(c3 + 1) * 512], w_in_d[l, :, C_NA + c3 * 512:C_NA + (c3 + 1) * 512], Bw)
                QT = sb(es, "naQ", [128, 4, T], BF16)
                KT = sb(es, "naK", [128, 4, T], BF16)
                Vp = sb(es, "naV", [128, NT, 8, 65], BF16)
                Bq = [Buf() for _ in range(9)]
                Bk = [Buf() for _ in range(9)]
                Bv = [Buf() for _ in range(NT)]
                Bvo = Buf()
                S.op("pool", lambda e: e.memset(Vp[:, :, :, 64:65], 1.0), writes=[Bvo])
                Ct = sb(es, "naC", [128, 8 * 16 * 64], BF16)
                Ct2 = sb(es, "naC2", [128, 8 * 22 * 64], BF16)
                Bc = Buf()
                with ExitStack() as es2:
                    Cg = sb(es2, "naCg", [128, 8 * 16, 64], F32)
                    ng = sb(es2, "naNg", [128, 64], F32)
                    Bcg = Buf()
                    S.dma(Cg[:], rpbg_d[l].rearrange("p (a q) -> p a q", q=64), writes=[Bcg])
                    S.dma(ng[:], negm_d, writes=[Bcg])
                    ngb = bass.AP(tensor=ng, offset=0, ap=[[64, 128], [0, 128], [1, 64]])
                    S.op("dve", lambda e: e.tensor_tensor(out=Ct[:].rearrange("p (a q) -> p a q", q=64), in0=Cg[:],
                                                          in1=ngb, op=ALU.add),
                         reads=[Bcg], writes=[Bc])
                    ng2 = sb(es2, "naNg2", [128, 22 * 64], F32)
                    S.dma(ng2[:], negm2_d, writes=[Bcg])
                    TW = 22 * 64
                    for hh2 in range(2):
                        S.dma(Cg[:, 0:88, :], rpbg2_d[l, :, hh2 * 4 * TW:(hh2 + 1) * 4 * TW].rearrange("p (a q) -> p a q", q=64),
                              reads=[Bcg], writes=[Bcg])
                        ngb2 = bass.AP(tensor=ng2, offset=0, ap=[[TW, 128], [0, 4], [1, TW]])
                        S.op("dve", lambda e: e.tensor_tensor(
                            out=Ct2[:, hh2 * 4 * TW:(hh2 + 1) * 4 * TW].rearrange("p (h q) -> p h q", q=TW),
                            in0=Cg[:, 0:88, :].rearrange("p (h u) q -> p h (u q)", h=4), in1=ngb2, op=ALU.add),
                            reads=[Bcg], writes=[Bc])
                    for hh2 in range(4):
                        S.op("act", lambda e: e.activation(out=Ct2[:, hh2 * 2 * TW:(hh2 + 1) * 2 * TW],
                                                           in_=Ct2[:, hh2 * 2 * TW:(hh2 + 1) * 2 * TW], func=AF.Exp),
                             reads=[Bc], writes=[Bc])
                    S.barrier()
                with ExitStack() as es2:
                    P = proj_pools(es2)
                    hbp = rr(es2, "nahb", [128, 8, 512], BF16, 2)
                    for tb in range(9):
                        a = tb * 512
                        n = 512 if tb < 8 else 256
                        hT, Bh = hload(hbp, a, n)
                        hb = [Bh]
                        tasks = []
                        for which, dstT, Bd, gc in ((0, QT, Bq, svq[:, 0:1]), (1, KT, Bk, svec[:, 1:2])):
                            if which == 0 and tb == 8 and not need_ctx:
                                continue
                            for c in range(4):
                                col = which * 512 + c * 128
                                mmf = (lambda ps, col=col, hT=hT, n=n: [mm(ps[:, 0:n], Wt[:, k, col:col + 128], hT[:, k, 0:n], k == 0, k == 7)
                                                                       for k in range(8)])
                                tasks.append(qk_task(P, mmf, [Bw] + hb, 128, n, blockones, 1.0 / 64, gc,
                                                     dstT[:, c, a:a + n], Bd[tb]))
                        run_tasks(tasks)
                        tasks = []
                        for t in range(a // 128, (a + n) // 128):
                            mmf = (lambda ps, t=t, hT=hT, a=a: [mm(ps[:, :], hT[:, k, t * 128 - a:(t + 1) * 128 - a], Wt[:, k, 1024:1536],
                                                                 k == 0, k == 7) for k in range(8)])
                            tasks.append(v_task(P, mmf, [Bw, Bh], 512, Vp[:, t, :, 0:64], Bv[t]))
                        run_tasks(tasks)
                    S.barrier()
                with ExitStack() as es2:
                    P = attn_pools(es2, 3)
                    FILL[0], FILL[1] = 0, KT[:, 0, 0:512]
                    ostp = rr(es2, "naost", [64, 8, 512], BF16, 2)
                    for qb in range(8):
                        ost, Bost = ostp.get()
                        for h in range(8):
                            c, hp = h // 2, (h % 2) * 64
                            psO, BpsO = P["acc"].get()
                            TW = 22 * 64
                            if 1 <= qb <= 6:
                                q_rhs = QT[hp:hp + 64, c, qb * 512:(qb + 1) * 512]
                                groups = []
                                for j in range(4 * qb - 2, 4 * qb + 6):
                                    t0 = 10 + 8 * qb - 2 * j
                                    assert 0 <= t0 and t0 + 8 <= 22
                                    cb_ap = Ct2[:, h * TW + t0 * 64:h * TW + (t0 + 8) * 64]

                                    def sfn(o, j=j):
                                        return [mm(o, KT[hp:hp + 64, c, j * 128:(j + 1) * 128], q_rhs, True, True)]
                                    groups.append([(sfn, [Bk[j // 4], Bq[qb]], (0, 128), Vp[:, j, h, :], [Bv[j], Bvo], (cb_ap, [Bc]))])
                                for j in (32, 33):
                                    def sfn(o, j=j):
                                        return [mm(o, KT[hp:hp + 64, c, j * 128:(j + 1) * 128], q_rhs, True, True)]
                                    groups.append([(sfn, [Bk[8], Bq[qb]], (0, 128), Vp[:, j, h, :], [Bv[j], Bvo], None)])
                                attend_groups(P, psO, BpsO, 0, 512, groups, True)
                                normalize(P, psO, BpsO, 512, ost[0:64, h, :], Bost)
                                continue
                            for gg in range(2):
                                g = qb * 2 + gg
                                if 1 <= g <= 14:
                                    qa = g * 256
                                    q_rhs = QT[hp:hp + 64, c, qa:qa + 256]
                                    tiles = []
                                    for j in range(2 * g - 2, 2 * g + 4):
                                        t0 = 10 + 4 * g - 2 * j
                                        assert 0 <= t0 and t0 + 4 <= 22
                                        cb_ap = Ct2[:, h * TW + t0 * 64:h * TW + (t0 + 4) * 64]

                                        def sfn(o, j=j, q_rhs=q_rhs):
                                            return [mm(o, KT[hp:hp + 64, c, j * 128:(j + 1) * 128], q_rhs, True, True)]
                                        tiles.append((sfn, [Bk[j // 4], Bq[qb]], (0, 128), Vp[:, j, h, :], [Bv[j], Bvo], (cb_ap, [Bc])))
                                    for j in (32, 33):
                                        def sfn(o, j=j, q_rhs=q_rhs):
                                            return [mm(o, KT[hp:hp + 64, c, j * 128:(j + 1) * 128], q_rhs, True, True)]
                                        tiles.append((sfn, [Bk[8], Bq[qb]], (0, 128), Vp[:, j, h, :], [Bv[j], Bvo], None))
                                    attend_groups(P, psO, BpsO, gg * 256, 256, [tiles[0:2], tiles[2:4], tiles[4:6], tiles[6:8]], True)
                                    continue
                                for rr_ in range(4):
                                    r = g * 4 + rr_
                                    r0 = min(max(r - 4, 0), 56)
                                    grp = []
                                    qa = r * 64
                                    q_rhs = QT[hp:hp + 64, c, qa:qa + 64]
                                    for j in range(r0 // 2, (r0 + 7) // 2 + 1):
                                        lo_ok = r0 <= 2 * j < r0 + 8
                                        up_ok = r0 <= 2 * j + 1 < r0 + 8
                                        rows = (0 if lo_ok else 64, 128 if up_ok else 64)
                                        s_ = 2 * j - r + 8
                                        assert 0 <= s_ <= 15
                                        cb_ap = Ct[:, (h * 16 + s_) * 64:(h * 16 + s_ + 1) * 64]

                                        def sfn(o, j=j, cb_ap=cb_ap, q_rhs=q_rhs):
                                            return [mm(o, KT[hp:hp + 64, c, j * 128:(j + 1) * 128], q_rhs, True, False),
                                                    mm(o, ident, cb_ap, False, True)]
                                        grp.append((sfn, [Bk[j // 4], Bq[qb], Bc, B_cm], rows,
                                                    Vp[rows[0]:rows[1], j, h, :], [Bv[j], Bvo]))
                                    for j in (32, 33):
                                        def sfn(o, j=j, q_rhs=q_rhs):
                                            return [mm(o, KT[hp:hp + 64, c, j * 128:(j + 1) * 128], q_rhs, True, True)]
                                        grp.append((sfn, [Bk[8], Bq[qb]], (0, 128), Vp[:, j, h, :], [Bv[j], Bvo]))
                                    attend_groups(P, psO, BpsO, gg * 256 + rr_ * 64, 64, [grp], True)
                            normalize(P, psO, BpsO, 512, ost[0:64, h, :], Bost)
                        store_o(l, 0, ost, Bost, qb * 512, 512)
                    if need_ctx:
                        ost, Bost = ostp.get()
                        for h in range(8):
                            c, hp = h // 2, (h % 2) * 64
                            psO, BpsO = P["acc"].get()
                            q_rhs = QT[hp:hp + 64, c, SEQ:T]
                            groups = []
                            for j in (32, 33):
                                def sfn(o, j=j):
                                    return [mm(o, KT[hp:hp + 64, c, j * 128:(j + 1) * 128], q_rhs, True, True)]
                                groups.append([(sfn, [Bk[8], Bq[8]], (0, 128), Vp[:, j, h, :], [Bv[j], Bvo])])
                            attend_groups(P, psO, BpsO, 0, 256, groups, True)
                            normalize(P, psO, BpsO, 256, ost[0:64, h, 0:256], Bost)
                        store_o(l, 0, ost, Bost, SEQ, 256)
                    S.barrier()

        def phase_sw(l, need_ctx):
            with ExitStack() as es:
                Wt = sb(es, "swW", [128, 8, 768], BF16)
                Bw = Buf()
                load_w(Wt[:, :, 0:512], w_in_d[l, :, C_SW:C_SW + 512], Bw)
                load_w(Wt[:, :, 512:768], w_in_d[l, :, C_SW + 512:C_SW + 768], Bw)
                QT = sb(es, "swQ", [128, 4, T], BF16)
                KT = sb(es, "swK", [128, T], BF16)
                Vp = sb(es, "swV", [128, NT, 2, 65], BF16)
                msk = sb(es, "swmsk", [128, 2, 128], BF16)
                sinke = sb(es, "swsink", [128, 8], F32)
                Bcs = Buf()
                S.dma(msk[:], msk_sw_d.rearrange("m p n -> p m n"), writes=[Bcs], q="pool")
                S.op("act", lambda e: e.activation(out=msk[:], in_=msk[:], func=AF.Exp), reads=[Bcs], writes=[Bcs])
                S.dma(sinke[:], bcast_ap(sink_h, l * 8, 8), writes=[Bcs])
                S.op("act", lambda e: e.activation(out=sinke[:], in_=sinke[:], func=AF.Exp), reads=[Bcs], writes=[Bcs])
                Bq = [Buf() for _ in range(9)]
                Bk = [Buf() for _ in range(9)]
                Bv = [Buf() for _ in range(NT)]
                Bvo = Buf()
                S.op("pool", lambda e: e.memset(Vp[:, :, :, 64:65], 1.0), writes=[Bvo])
                with ExitStack() as es2:
                    P = proj_pools(es2)
                    hbp = rr(es2, "swhb", [128, 8, 512], BF16, 2)
                    csp = rr(es2, "swcs", [128, 2, 512], F32, 2)
                    for tb in range(9):
                        a = tb * 512
                        n = 512 if tb < 8 else 256
                        hT, Bh = hload(hbp, a, n)
                        hb = [Bh]
                        rope = None
                        if tb < 8:
                            cs, Bcsb = csp.get()
                            S.dma(cs[:], cs_sw_d[:, :, a:a + n].rearrange("m p n -> p m n"), writes=[Bcsb])
                            rope = (perm_sw, cs[:, 0, 0:n], cs[:, 1, 0:n], Bcsb)
                        tasks = []
                        for c in range(4):
                            if tb == 8 and not need_ctx:
                                continue
                            mmf = (lambda ps, c=c, hT=hT, n=n: [mm(ps[:, 0:n], Wt[:, k, c * 128:(c + 1) * 128], hT[:, k, 0:n], k == 0, k == 7)
                                                                for k in range(8)])
                            tasks.append(qk_task(P, mmf, [Bw] + hb, 128, n, blockones, 1.0 / 64, svq[:, 2:3],
                                                 QT[:, c, a:a + n], Bq[tb], rope))
                        mmf = (lambda ps, hT=hT, n=n: [mm(ps[:, 0:n], Wt[:, k, 512:640], hT[:, k, 0:n], k == 0, k == 7) for k in range(8)])
                        tasks.append(qk_task(P, mmf, [Bw] + hb, 128, n, blockones, 1.0 / 64, svec[:, 3:4], KT[:, a:a + n], Bk[tb], rope))
                        run_tasks(tasks)
                        tasks = []
                        for t in range(a // 128, (a + n) // 128):
                            mmf = (lambda ps, t=t, hT=hT, a=a: [mm(ps[:, 0:128], hT[:, k, t * 128 - a:(t + 1) * 128 - a], Wt[:, k, 640:768],
                                                                 k == 0, k == 7) for k in range(8)])
                            tasks.append(v_task(P, mmf, [Bw, Bh], 128, Vp[:, t, :, 0:64], Bv[t]))
                        run_tasks(tasks)
                    S.barrier()
                with ExitStack() as es2:
                    P = attn_pools(es2, 3)
                    FILL[0], FILL[1] = 0, QT[:, 0, 0:512]
                    ostp = rr(es2, "swost", [64, 8, 512], BF16, 2)
                    for qb in range(8):
                        ost, Bost = ostp.get()
                        for h in range(8):
                            c, hp, kv = h % 4, (h // 4) * 64, h // 4
                            psO, BpsO = P["acc"].get()
                            for nn in range(4):
                                nt = qb * 4 + nn
                                q_rhs = QT[hp:hp + 64, c, nt * 128:(nt + 1) * 128]
                                loc = []
                                for j, mi in ((nt - 1, 0), (nt, None), (nt + 1, 1)):
                                    if j < 0 or j > 31:
                                        continue

                                    def sfn(o, j=j, mi=mi):
                                        return [mm(o, KT[hp:hp + 64, j * 128:(j + 1) * 128], q_rhs, True, True)]
                                    loc.append((sfn, [Bk[j // 4], Bq[qb]], (0, 128), Vp[:, j, kv, :], [Bv[j], Bvo],
                                                None if mi is None else (msk[:, mi, :], [Bcs])))
                                cg = []
                                for j in (32, 33):
                                    def sfn(o, j=j):
                                        return [mm(o, KT[hp:hp + 64, j * 128:(j + 1) * 128], q_rhs, True, True)]
                                    cg.append((sfn, [Bk[8], Bq[qb]], (0, 128), Vp[:, j, kv, :], [Bv[j], Bvo]))
                                attend_groups(P, psO, BpsO, nn * 128, 128, [loc, cg], True)
                            normalize(P, psO, BpsO, 512, ost[0:64, h, :], Bost, sink_ap=sinke[64:65, h:h + 1])
                        store_o(l, 1, ost, Bost, qb * 512, 512)
                    if need_ctx:
                        ost, Bost = ostp.get()
                        for h in range(8):
                            c, hp, kv = h % 4, (h // 4) * 64, h // 4
                            psO, BpsO = P["acc"].get()
                            q_rhs = QT[hp:hp + 64, c, SEQ:T]
                            groups = []
                            for j in (32, 33):
                                def sfn(o, j=j):
                                    return [mm(o, KT[hp:hp + 64, j * 128:(j + 1) * 128], q_rhs, True, True)]
                                groups.append([(sfn, [Bk[8], Bq[8]], (0, 128), Vp[:, j, kv, :], [Bv[j], Bvo])])
                            attend_groups(P, psO, BpsO, 0, 256, groups, True)
                            normalize(P, psO, BpsO, 256, ost[0:64, h, 0:256], Bost, sink_ap=sinke[64:65, h:h + 1])
                        store_o(l, 1, ost, Bost, SEQ, 256)
                    S.barrier()

        def phase_mla(l, need_ctx):
            with ExitStack() as es:
                Wt = sb(es, "mlW", [128, 8, 672], BF16)
                Wq = sb(es, "mlWq", [128, 3, 768], BF16)
                Wkv = sb(es, "mlWkv", [128, 2, 1024], BF16)
                Wkp = sb(es, "mlWkp", [128, 8, 2, 96], BF16)
                Bw = Buf()
                load_w(Wt[:, :, 0:384], w_in_d[l, :, C_ML:C_ML + 384], Bw)
                load_w(Wt[:, :, 384:672], w_in_d[l, :, C_ML + 384:C_ML + 672], Bw)
                load_w(Wq[:], w_uq_d[l], Bw)
                load_w(Wkv[:], w_ukv_d[l], Bw)
                S.op("pool", lambda e: e.memset(Wkp[:], 0.0), writes=[Bw])
                for h in range(8):
                    S.op("pool", lambda e: e.tensor_copy(out=Wkp[:, h, :, 0:64], in_=Wkv[:, :, h * 128:h * 128 + 64]),
                         reads=[Bw], writes=[Bw])
                for hh in range(2):
                    with ExitStack() as esh:
                        QT = sb(esh, "mlQ", [96, 4, T], BF16)
                        KT = sb(esh, "mlK", [96, 4, T], BF16)
                        Vp = sb(esh, "mlV", [128, NT, 4, 65], BF16)
                        Bq = [Buf() for _ in range(9)]
                        Bk = [Buf() for _ in range(9)]
                        Bv = [Buf() for _ in range(NT)]
                        Bvo = Buf()
                        S.op("pool", lambda e: e.memset(Vp[:, :, :, 64:65], 1.0), writes=[Bvo])
                        with ExitStack() as es2:
                            P = proj_pools(es2)
                            rawp = rr(es2, "mlraw", [128, 3, 512], F32, 1)
                            sqp = rr(es2, "mlsq", [128, 3, 512], BF16, 1)
                            rawkp = rr(es2, "mlrawk", [128, 2, 512], F32, 1)
                            sqkp = rr(es2, "mlsqk", [128, 2, 512], BF16, 1)
                            cqp = rr(es2, "mlcq", [128, 3, 512], BF16, 2)
                            ckp = rr(es2, "mlck", [128, 2, 512], BF16, 2)
                            krp = rr(es2, "mlkr", [32, 512], BF16, 2)
                            hbp = rr(es2, "mlhb", [128, 8, 512], BF16, 2)
                            csp = rr(es2, "mlcs", [128, 2, 512], F32, 2)
                            for tb in range(9):
                                a = tb * 512
                                n = 512 if tb < 8 else 256
                                hT, Bh = hload(hbp, a, n)
                                hb = [Bh]
                                do_q = (tb < 8) or need_ctx
                                rope = None
                                if tb < 8:
                                    cs, Bcsb = csp.get()
                                    S.dma(cs[:], cs_ml_d[:, :, a:a + n].rearrange("m p n -> p m n"), writes=[Bcsb])
                                    rope = (perm_ml, cs[0:96, 0, 0:n], cs[0:96, 1, 0:n], Bcsb)

                                def ranknorm_g(ncx, col0, gcol0, dst, Bdst, inv_r, raw, Braw, sq, Bsq, hT=hT, n=n, hb=hb):
                                    for c in range(ncx):
                                        ps, Bps = P["ps1"].get()
                                        cc = col0 + c * 128
                                        S.pe_group([mm(ps[:, 0:n], Wt[:, k, cc:cc + 128], hT[:, k, 0:n], k == 0, k == 7)
                                                    for k in range(8)], reads=[Bw] + hb, writes=[Bps])
                                        S.op("act", lambda e: e.activation(out=raw[:, c, 0:n], in_=ps[:, 0:n], func=AF.Identity),
                                             reads=[Bps], writes=[Braw])
                                        S.op("act", lambda e: e.activation(out=sq[:, c, 0:n], in_=ps[:, 0:n], func=AF.Square),
                                             reads=[Bps], writes=[Bsq])
                                        yield
                                    p2, Bp2 = P["ps2"].get()
                                    S.pe_group([mm(p2[:, 0:n], allones, sq[:, c, 0:n], c == 0, c == ncx - 1) for c in range(ncx)],
                                               reads=[Bsq, B_cm], writes=[Bp2])
                                    yield
                                    rt, Brt = P["rt"].get()
                                    S.op("act", lambda e: e.activation(out=rt[:, 0:n], in_=p2[:, 0:n], func=AF.Ln,
                                                                       scale=inv_r, bias=epst[:, 0:1]),
                                         reads=[Bp2, B_const], writes=[Brt])
                                    yield
                                    S.op("act", lambda e: e.activation(out=rt[:, 0:n], in_=rt[:, 0:n], func=AF.Exp, scale=-0.5),
                                         reads=[Brt], writes=[Brt])
                                    for c in range(ncx):
                                        S.op("dve", lambda e: e.scalar_tensor_tensor(
                                            out=dst[:, c, 0:n], in0=raw[:, c, 0:n], scalar=svec[:, gcol0 + c:gcol0 + c + 1],
                                            in1=rt[:, 0:n], op0=ALU.mult, op1=ALU.mult),
                                            reads=[Braw, Brt, B_sv], writes=[Bdst])

                                def kr_g(kr, Bkr, hT=hT, n=n, hb=hb):
                                    ps, Bps = P["ps1"].get()
                                    S.pe_group([mm(ps[0:32, 0:n], Wt[:, k, 640:672], hT[:, k, 0:n], k == 0, k == 7)
                                                for k in range(8)], reads=[Bw] + hb, writes=[Bps])
                                    S.op("act", lambda e: e.activation(out=kr[0:32, 0:n], in_=ps[0:32, 0:n], func=AF.Identity),
                                         reads=[Bps], writes=[Bkr])
                                    yield

                                tasks = []
                                if do_q:
                                    cq, Bcq = cqp.get()
                                    raw, Braw = rawp.get()
                                    sq, Bsq = sqp.get()
                                    tasks.append(ranknorm_g(3, 0, 4, cq, Bcq, 1.0 / 384, raw, Braw, sq, Bsq))
                                ck, Bck = ckp.get()
                                raw2, Braw2 = rawkp.get()
                                sq2, Bsq2 = sqkp.get()
                                tasks.append(ranknorm_g(2, 384, 7, ck, Bck, 1.0 / 256, raw2, Braw2, sq2, Bsq2))
                                kr, Bkr = krp.get()
                                tasks.append(kr_g(kr, Bkr))
                                run_tasks(tasks)
                                tasks = []
                                for hl in range(4):
                                    h = hh * 4 + hl
                                    if do_q:
                                        mmf = (lambda ps, h=h, cq=cq, n=n: [mm(ps[0:96, 0:n], Wq[:, c, h * 96:(h + 1) * 96], cq[:, c, 0:n],
                                                                               c == 0, c == 2) for c in range(3)])
                                        tasks.append(qk_task(P, mmf, [Bw, Bcq], 96, n, allones, 1.0 / 96, svq[0:96, 9:10],
                                                             QT[0:96, hl, a:a + n], Bq[tb], rope))
                                    mmf = (lambda ps, h=h, ck=ck, kr=kr, n=n: [
                                        mm(ps[0:96, 0:n], Wkp[:, h, 0, :], ck[:, 0, 0:n], True, False),
                                        mm(ps[0:96, 0:n], Wkp[:, h, 1, :], ck[:, 1, 0:n], False, False),
                                        mm(ps[0:96, 0:n], shiftm[0:32, 0:96], kr[0:32, 0:n], False, True)])
                                    tasks.append(qk_task(P, mmf, [Bw, Bck, Bkr, B_cm], 96, n, allones, 1.0 / 96, svec[0:96, 10:11],
                                                         KT[0:96, hl, a:a + n], Bk[tb], rope))
                                run_tasks(tasks)
                                tasks = []
                                for ti, t in enumerate(range(a // 128, (a + n) // 128)):
                                    def mmf(ps, ti=ti, ck=ck):
                                        fns = []
                                        for hl in range(4):
                                            h = hh * 4 + hl
                                            for c in range(2):
                                                fns.append(mm(ps[:, hl * 64:(hl + 1) * 64], ck[:, c, ti * 128:(ti + 1) * 128],
                                                              Wkv[:, c, h * 128 + 64:h * 128 + 128], c == 0, c == 1))
                                        return fns
                                    tasks.append(v_task(P, mmf, [Bw, Bck], 256, Vp[:, t, :, 0:64], Bv[t]))
                                run_tasks(tasks)
                            S.barrier()
                        with ExitStack() as es2:
                            P = attn_pools(es2, 3)
                            FILL[0] = 0
                            ostp = rr(es2, "mlost", [64, 4, 512], BF16, 2)
                            li = l if debug else 0
                            for qb in range(9):
                                if qb == 8 and not need_ctx:
                                    continue
                                a = qb * 512
                                n = 512 if qb < 8 else 256
                                ost, Bost = ostp.get()
                                for hl in range(4):
                                    psO, BpsO = P["acc"].get()
                                    q_rhs = QT[0:96, hl, a:a + n]
                                    groups = []
                                    for j in (range(NT) if qb < 8 else (32, 33)):
                                        def sfn(o, j=j):
                                            return [mm(o, KT[0:96, hl, j * 128:(j + 1) * 128], q_rhs, True, True)]
                                        groups.append([(sfn, [Bk[j // 4], Bq[qb]], (0, 128), Vp[:, j, hl, :], [Bv[j], Bvo])])
                                    attend_groups(P, psO, BpsO, 0, n, groups, True)
                                    normalize(P, psO, BpsO, n, ost[0:64, hl, 0:n], Bost)
                                flush_norm()
                                dst = o_d[li, 2, hh * 256:(hh + 1) * 256, a:a + n].rearrange("(h d) n -> d h n", d=64)
                                S.dma(dst, ost[0:64, :, 0:n], reads=[Bost], writes=B_o[2][a // 128:(a + n) // 128])
                            S.barrier()

        def phase_merge(l, need_ctx):
            li = l if debug else 0
            with ExitStack() as es:
                Wg = sb(es, "mgWg", [128, 8, 3072], BF16)
                Wb = sb(es, "mgWb", [128, 3, 4, D], BF16)
                Wo = sb(es, "mgWo", [128, 8, D], BF16)
                BWg = [Buf() for _ in range(12)]
                BWb = [Buf() for _ in range(3)]
                BWo = Buf()

                def ldg(j):
                    load_w(Wg[:, :, j * 256:(j + 1) * 256], w_in_d[l, :, C_G + j * 256:C_G + (j + 1) * 256], BWg[j])
                for j in (0, 4, 8):
                    ldg(j)
                for nbr in range(3):
                    load_w(Wb[:, nbr, :, :], w_br_d[l, nbr], BWb[nbr])
                for j in (1, 5, 9, 2, 6, 10, 3, 7, 11):
                    ldg(j)
                load_w(Wo[:], w_out_d[l], BWo)
                P = norm_pools(es)
                GA = sb(es, "mgGA", [128, D], F32)
                SHf = sb(es, "mgSHf", [128, D], F32)
                Gf = sb(es, "mgGf", [128, D], F32)
                gt = sb(es, "mgg", [128, D], F32)
                Bg = Buf()
                S.dma(gt[:], bcast_ap(g_ffn_h, l * D, D), writes=[Bg])
                oTp = [rr(es, f"mgo{i}", [128, 4, 512], BF16, 2) for i in range(3)]
                mT = sb(es, "mgmT", [128, 8, 512], BF16)
                BmT = Buf()
                macc = sb(es, "mgacc", [128, 512], F32)
                Bmacc = Buf()
                sgp = rr(es, "mgsg", [128, 512], F32, 2)
                tmp = rr(es, "mgtmp", [128, 512], F32, 2)
                xin = rr(es, "mgx", [128, D], F32, 2)
                x1p = rr(es, "mgx1", [128, D], F32, 2)
                pg = RR(psb[0:2])
                py = RR(psb[2:4])
                po = RR(psb[4:7])
                hbp = rr(es, "mghb", [128, 8, 512], BF16, 2)
                hstp = rr(es, "mghst", [128, 8, 512], BF16, 1)
                nblk = 9 if need_ctx else 8

                def mg_loads(tb):
                    a = tb * 512
                    n = 512 if tb < 8 else 256
                    hT, Bh = hload(hbp, a, n)
                    oTs = []
                    for nbr in range(3):
                        ot, Bot = oTp[nbr].get()
                        S.dma(ot[:, :, 0:n], o_d[li, nbr, :, a:a + n].rearrange("(c p) n -> p c n", p=128),
                              reads=B_o[nbr][a // 128:(a + n) // 128], writes=[Bot])
                        oTs.append((ot, Bot))
                    return hT, Bh, oTs
                nxt = mg_loads(0)
                pend_n2 = []
                for tb in range(nblk):
                    a = tb * 512
                    n = 512 if tb < 8 else 256
                    row = 0 if tb < 8 else 1
                    hT, Bh, oTs = nxt
                    if tb + 1 < nblk:
                        nxt = mg_loads(tb + 1)
                    oT = [x[0] for x in oTs]
                    Bo = [x[1] for x in oTs]
                    if tb == 0 or tb == 8:
                        load_mod_bcast(GA, Bg, l, row, 2)
                        load_mod_bcast(SHf, Bg, l, row, 3)
                        load_mod_bcast(Gf, Bg, l, row, 4)
                        S.op("dve", lambda e: e.scalar_tensor_tensor(out=Gf[:], in0=Gf[:], scalar=1.0, in1=gt[:],
                                                                     op0=ALU.add, op1=ALU.mult), reads=[Bg], writes=[Bg])
                    hb = [Bh]
                    hst, Bhst = hstp.get()
                    for f in range(8):
                        for nbr in range(3):
                            psg, Bpg = pg.get()
                            col = nbr * D + f * 128
                            S.pe_group([mm(psg[:, 0:n], Wg[:, k, col:col + 128], hT[:, k, 0:n], k == 0, k == 7)
                                        for k in range(8)], reads=[BWg[col // 256]] + hb, writes=[Bpg])
                            psy, Bpy = py.get()
                            S.pe_group([mm(psy[:, 0:n], Wb[:, nbr, c, f * 128:(f + 1) * 128], oT[nbr][:, c, 0:n], c == 0, c == 3)
                                        for c in range(4)], reads=[BWb[nbr], Bo[nbr]], writes=[Bpy])
                            sg, Bsg = sgp.get()
                            S.op("act", lambda e: e.activation(out=sg[:, 0:n], in_=psg[:, 0:n], func=AF.Sigmoid),
                                 reads=[Bpg], writes=[Bsg])
                            if nbr == 0:
                                S.op("dve", lambda e: e.tensor_tensor(out=macc[:, 0:n], in0=psy[:, 0:n], in1=sg[:, 0:n], op=ALU.mult),
                                     reads=[Bpy, Bsg], writes=[Bmacc])
                            else:
                                tm, Btm = tmp.get()
                                S.op("dve", lambda e: e.tensor_tensor(out=tm[:, 0:n], in0=psy[:, 0:n], in1=sg[:, 0:n], op=ALU.mult),
                                     reads=[Bpy, Bsg], writes=[Btm])
                                if nbr == 1:
                                    S.op("pool", lambda e: e.tensor_tensor(out=macc[:, 0:n], in0=macc[:, 0:n], in1=tm[:, 0:n], op=ALU.add),
                                         reads=[Btm, Bmacc], writes=[Bmacc])
                                else:
                                    S.op("pool", lambda e: e.tensor_tensor(out=mT[:, f, 0:n], in0=macc[:, 0:n], in1=tm[:, 0:n], op=ALU.add),
                                         reads=[Btm, Bmacc], writes=[BmT])
                    for tt in range(n // 128):
                        t = a // 128 + tt
                        xt, Bx = xin.get()
                        if l == 0:
                            src = x_d[t * 128:(t + 1) * 128, :] if t < 32 else ctx_d[(t - 32) * 128:(t - 31) * 128, :]
                            S.dma(xt[:], src, writes=[Bx])
                        else:
                            S.dma(xt[:], xs_d[t * 128:(t + 1) * 128, :], reads=[B_xs[t]], writes=[Bx])
                        x1, Bx1 = x1p.get()
                        for half in range(2):
                            pso, Bpo = po.get()
                            S.pe_group([mm(pso[:, :], mT[:, k, tt * 128:(tt + 1) * 128], Wo[:, k, half * 512:(half + 1) * 512],
                                           k == 0, k == 7) for k in range(8)], reads=[BWo, BmT], writes=[Bpo])
                            tm, Btm = tmp.get()
                            S.op("dve", lambda e: e.tensor_tensor(out=tm[:, :], in0=pso[:, :], in1=GA[:, half * 512:(half + 1) * 512],
                                                                  op=ALU.mult), reads=[Bpo, Bg], writes=[Btm])
                            S.op("pool", lambda e: e.tensor_tensor(out=x1[:, half * 512:(half + 1) * 512], in0=tm[:, :],
                                                                   in1=xt[:, half * 512:(half + 1) * 512], op=ALU.add),
                                 reads=[Btm, Bx], writes=[Bx1])
                        S.dma(xs_d[t * 128:(t + 1) * 128, :], x1[:], reads=[Bx1], writes=[B_xs[t]], q="pool")
                        g = norm_gen(P, x1, Bx1, Gf, SHf, Bg, hst, Bhst, tt)
                        next(g)
                        next(g)
                        if pend_n2:
                            for _ in pend_n2.pop():
                                pass
                        pend_n2.append(g)
                    for _ in pend_n2.pop():
                        pass
                    S.dma(hT_d[:, :, a:a + n], hst[:, :, 0:n], reads=[Bhst], writes=hbufs(a, a + n), q="pool")
                S.barrier()

        def phase_ffn(l, need_ctx, last):
            with ExitStack() as es:
                cw = sb(es, "ffcw", [128, 44, 3], F32)
                cb = sb(es, "ffcb", [128, 44], F32)
                GF = sb(es, "ffGF", [128, D], F32)
                Bc = Buf()
                S.dma(cw[:], cw_d[l].rearrange("p (c j) -> p c j", j=3), writes=[Bc])
                S.dma(cb[:], cb_d[l], writes=[Bc])
                Wus = [sb(es, f"ffWu{i}", [128, 8, 2, 1408], BF16) for i in range(2)]
                Wds = [sb(es, f"ffWd{i}", [128, 11, D], BF16) for i in range(2)]
                BWu = [[[Buf() for _ in range(2)] for _ in range(2)] for _ in range(2)]
                BWd = [[Buf() for _ in range(3)] for _ in range(2)]
                for ps_ in range(2):
                    for hf, (j0, j1) in enumerate(((0, 768), (768, 1408))):
                        for gv in range(2):
                            c0 = gv * DFF + ps_ * 1408
                            load_w(Wus[ps_][:, :, gv, j0:j1], w_up_d[l, :, c0 + j0:c0 + j1], BWu[ps_][gv][hf])
                    for ji, j in enumerate(range(0, 11, 4)):
                        je = min(11, j + 4)
                        load_w(Wds[ps_][:, j:je, :], w_dn_d[l, ps_ * 1408 + j * 128:ps_ * 1408 + je * 128, :], BWd[ps_][ji])
                aT = sb(es, "ffaT", [128, 11, 512], BF16)
                BaT = Buf()
                accp = rr(es, "ffacc", [128, 512], F32, 4)
                sgp = rr(es, "ffsg", [128, 512], F32, 2)
                tmp = rr(es, "fftmp", [128, 512], F32, 2)
                xin = rr(es, "ffx", [128, D], F32, 2)
                x1p = rr(es, "ffx1", [128, D], F32, 2)
                pu = RR(psb[0:4])
                po = RR(psb[4:7])
                hbp = rr(es, "ffhb", [128, 8, 512], BF16, 2)
                blocks = [(0, SEQ, i * 510, min(SEQ, (i + 1) * 510)) for i in range(9)]
                if need_ctx:
                    blocks.append((SEQ, T, SEQ, T))
                for ps_ in range(2):
                    Wu, Wd = Wus[ps_], Wds[ps_]
                    cur_row = None
                    def ff_load(bi):
                        s0, s1, a, b = blocks[bi]
                        ua, ub = max(a - 1, s0), min(b + 1, s1)
                        return hload(hbp, ua, ub - ua)
                    nxt = ff_load(0)
                    for bi, (s0, s1, a, b) in enumerate(blocks):
                        row = 0 if s0 == 0 else 1
                        if row != cur_row:
                            load_mod_bcast(GF, Bc, l, row, 5)
                            cur_row = row
                        ua, ub = max(a - 1, s0), min(b + 1, s1)
                        nu = ub - ua
                        n = b - a
                        off = a - ua
                        hT, Bh = nxt
                        if bi + 1 < len(blocks):
                            nxt = ff_load(bi + 1)
                        hb = [Bh]
                        for i in range(11):
                            ci = [ps_ * 11 + i, 22 + ps_ * 11 + i]
                            accs = []
                            for gv in range(2):
                                psu, Bpu = pu.get()
                                S.pe_group([mm(psu[:, 0:nu], Wu[:, k, gv, i * 128:(i + 1) * 128], hT[:, k, 0:nu], k == 0, k == 7)
                                            for k in range(8)], reads=[BWu[ps_][gv][0 if i < 6 else 1]] + hb, writes=[Bpu])
                                acc, Bacc = accp.get()
                                cc = ci[gv]
                                S.op("act", lambda e: e.activation(out=acc[:, 0:n], in_=psu[:, off:off + n], func=AF.Identity,
                                                                   scale=cw[:, cc, 1:2], bias=cb[:, cc:cc + 1]),
                                     reads=[Bpu, Bc], writes=[Bacc])
                                la = max(a, s0 + 1)
                                S.op("dve", lambda e: e.scalar_tensor_tensor(
                                    out=acc[:, la - a:n], in0=psu[:, la - 1 - ua:b - 1 - ua], scalar=cw[:, cc, 0:1],
                                    in1=acc[:, la - a:n], op0=ALU.mult, op1=ALU.add), reads=[Bpu, Bc, Bacc], writes=[Bacc])
                                rb = min(b, s1 - 1)
                                S.op("dve", lambda e: e.scalar_tensor_tensor(
                                    out=acc[:, 0:rb - a], in0=psu[:, a + 1 - ua:rb + 1 - ua], scalar=cw[:, cc, 2:3],
                                    in1=acc[:, 0:rb - a], op0=ALU.mult, op1=ALU.add), reads=[Bpu, Bc, Bacc], writes=[Bacc])
                                accs.append((acc, Bacc))
                            sg, Bsg = sgp.get()
                            S.op("act", lambda e: e.activation(out=sg[:, 0:n], in_=accs[0][0][:, 0:n], func=AF.Silu),
                                 reads=[accs[0][1]], writes=[Bsg])
                            S.op("pool", lambda e: e.tensor_tensor(out=aT[:, i, 0:n], in0=sg[:, 0:n], in1=accs[1][0][:, 0:n],
                                                                   op=ALU.mult), reads=[Bsg, accs[1][1]], writes=[BaT])
                        for m0 in range(0, n, 128):
                            msz = min(128, n - m0)
                            ta = a + m0
                            xt, Bx = xin.get()
                            if l == 0 and ps_ == 0:
                                pass
                            S.dma(xt[0:msz, :], xs_d[ta:ta + msz, :], reads=xbufs(ta, ta + msz), writes=[Bx])
                            x1, Bx1 = x1p.get()
                            for half in range(2):
                                pso, Bpo = po.get()
                                S.pe_group([mm(pso[0:msz, :], aT[:, i, m0:m0 + msz], Wd[:, i, half * 512:(half + 1) * 512],
                                               i == 0, i == 10) for i in range(11)], reads=BWd[ps_] + [BaT], writes=[Bpo])
                                tm, Btm = tmp.get()
                                S.op("dve", lambda e: e.tensor_tensor(out=tm[0:msz, :], in0=pso[0:msz, :],
                                                                      in1=GF[0:msz, half * 512:(half + 1) * 512], op=ALU.mult),
                                     reads=[Bpo, Bc], writes=[Btm])
                                S.op("pool", lambda e: e.tensor_tensor(out=x1[0:msz, half * 512:(half + 1) * 512], in0=tm[0:msz, :],
                                                                       in1=xt[0:msz, half * 512:(half + 1) * 512], op=ALU.add),
                                     reads=[Btm, Bx], writes=[Bx1])
                            if last and ps_ == 1:
                                S.dma(out_d[ta:ta + msz, :], x1[0:msz, :], reads=[Bx1], writes=[Buf()], q="pool")
                            else:
                                S.dma(xs_d[ta:ta + msz, :], x1[0:msz, :], reads=[Bx1], writes=xbufs(ta, ta + msz), q="pool")
                S.barrier()

        def dump_hT(l):
            pass

        for l in range(layers):
            need_ctx = l < DEPTH - 1
            phase_ada(l)
            phase_n1(l)
            dump_hT(l)
            if PH["na"]:
                phase_na(l, need_ctx)
            if PH["sw"]:
                phase_sw(l, need_ctx)
            if PH["mla"]:
                phase_mla(l, need_ctx)
            if PH["merge"]:
                phase_merge(l, need_ctx)
            if PH["ffn"]:
                phase_ffn(l, need_ctx, l == DEPTH - 1)
        S.barrier()
    nc._sched_stats = (S.n_ops, S.n_waits, S.nsem)
    return nc


PH = {"na": True, "sw": True, "mla": True, "merge": True, "ffn": True}


def _consts():
    ident = np.eye(128, dtype=np.float32)
    blockones = np.zeros((128, 128), np.float32)
    blockones[0:64, 0:64] = 1
    blockones[64:128, 64:128] = 1
    allones = np.ones((128, 128), np.float32)

    def partner64(d):
        return d + 16 if (d % 32) < 16 else d - 16
    perm_sw = np.zeros((128, 128), np.float32)
    for i in range(128):
        base = (i // 64) * 64
        perm_sw[base + partner64(i % 64), i] = 1
    perm_ml = np.zeros((128, 128), np.float32)
    for i in range(64, 96):
        dd = i - 64
        p = dd + 8 if (dd % 16) < 8 else dd - 8
        perm_ml[64 + p, i] = 1
    shiftm = np.zeros((128, 128), np.float32)
    for k in range(32):
        shiftm[k, 64 + k] = 1
    cmats = np.stack([ident, blockones, allones, perm_sw, perm_ml, shiftm])
    t = np.arange(SEQ)
    row = (t // GRID).astype(np.float32)
    col = (t % GRID).astype(np.float32)
    cs_sw = np.zeros((2, 128, SEQ), np.float32)
    inv16 = (10000.0 ** (-np.arange(16, dtype=np.float32) / 16)).astype(np.float32)
    for p in range(128):
        d = p % 64
        pos = row if d < 32 else col
        i = d % 16
        ang = (pos * inv16[i]).astype(np.float32)
        cs_sw[0, p] = np.cos(ang)
        cs_sw[1, p] = -np.sin(ang) if (d % 32) < 16 else np.sin(ang)
    cs_ml = np.zeros((2, 128, SEQ), np.float32)
    cs_ml[0, :, :] = 1.0
    inv8 = (10000.0 ** (-np.arange(8, dtype=np.float32) / 8)).astype(np.float32)
    for p in range(64, 96):
        dd = p - 64
        pos = row if dd < 16 else col
        i = dd % 8
        ang = (pos * inv8[i]).astype(np.float32)
        cs_ml[0, p] = np.cos(ang)
        cs_ml[1, p] = -np.sin(ang) if (dd % 16) < 8 else np.sin(ang)
    j = np.arange(128)[:, None]
    i = np.arange(128)[None, :]
    msk = np.stack([np.where(j >= i, 0.0, NEG), np.where(j <= i, 0.0, NEG)]).astype(np.float32)
    qc = np.arange(64)[None, :]
    kc = np.arange(64)[:, None]
    c0 = np.clip(qc - 8, 0, 48)
    inwin = (kc >= c0) & (kc < c0 + 16)
    negm = np.where(inwin, 0.0, NEG).astype(np.float32)
    negm2 = np.full((128, 22, 64), NEG, np.float32)
    for t in range(22):
        for half, dr in ((0, 17 - t), (1, 18 - t)):
            if 3 <= dr <= 10:
                negm2[half * 64:(half + 1) * 64, t, :] = negm
    negm2 = negm2.reshape(128, 22 * 64)
    negm = np.concatenate([negm, negm], axis=0)
    return dict(cmats=cmats, cs_sw=cs_sw, cs_ml=cs_ml, msk_sw=msk, negm=negm, negm2=negm2)


def _layouts(inp):
    w_in = np.ascontiguousarray(inp["w_in"]).copy()
    perm = [0, 4, 1, 5, 2, 6, 3, 7]
    swq = w_in[:, :, C_SW:C_SW + 512].reshape(DEPTH, D, 8, 64)[:, :, perm, :].reshape(DEPTH, D, 512)
    w_in[:, :, C_SW:C_SW + 512] = swq
    rpb = inp["na_rpb"]
    kc = np.arange(64)[:, None]
    qc = np.arange(64)[None, :]
    dc = np.clip(kc - qc, -15, 15) + 15
    rpbg = np.zeros((DEPTH, 128, 8, 16, 64), np.float32)
    for s in range(16):
        dr_lo, dr_up = s - 1, s
        if 0 <= dr_lo <= 14:
            rpbg[:, 0:64, :, s, :] = np.transpose(rpb[:, :, dr_lo, :][:, :, dc], (0, 2, 1, 3))
        if 0 <= dr_up <= 14:
            rpbg[:, 64:128, :, s, :] = np.transpose(rpb[:, :, dr_up, :][:, :, dc], (0, 2, 1, 3))
    rpbg = rpbg.reshape(DEPTH, 128, 8 * 16 * 64)
    rpbg2 = np.zeros((DEPTH, 128, 8, 22, 64), np.float32)
    for t in range(22):
        dr_lo, dr_up = 17 - t, 18 - t
        if 0 <= dr_lo <= 14:
            rpbg2[:, 0:64, :, t, :] = np.transpose(rpb[:, :, dr_lo, :][:, :, dc], (0, 2, 1, 3))
        if 0 <= dr_up <= 14:
            rpbg2[:, 64:128, :, t, :] = np.transpose(rpb[:, :, dr_up, :][:, :, dc], (0, 2, 1, 3))
    rpbg2 = rpbg2.reshape(DEPTH, 128, 8 * 22 * 64)
    svec = np.zeros((DEPTH, 128, 16), np.float32)
    svec[:, :, 0] = np.tile(inp["na_q_norm"], (1, 2))
    svec[:, :, 1] = np.tile(inp["na_k_norm"], (1, 2))
    svec[:, :, 2] = np.tile(inp["sw_q_norm"], (1, 2))
    svec[:, :, 3] = np.tile(inp["sw_k_norm"], (1, 2))
    svec[:, :, 4:7] = inp["mla_q_rank_norm"].reshape(DEPTH, 3, 128).transpose(0, 2, 1)
    svec[:, :, 7:9] = inp["mla_kv_rank_norm"].reshape(DEPTH, 2, 128).transpose(0, 2, 1)
    svec[:, 0:96, 9] = inp["mla_q_norm"]
    svec[:, 0:96, 10] = inp["mla_k_norm"]
    cw = inp["conv_w"].reshape(DEPTH, 3, 44, 128).transpose(0, 3, 2, 1).reshape(DEPTH, 128, 44 * 3)
    cb = inp["conv_b"].reshape(DEPTH, 44, 128).transpose(0, 2, 1)
    f = lambda a: np.ascontiguousarray(a, dtype=np.float32)
    return dict(w_in=f(w_in), rpbg=f(rpbg), rpbg2=f(rpbg2), svec=f(svec), cw=f(cw), cb=f(cb))


_CACHE = {}


def _in_maps(inp):
    consts = _consts()
    lay = _layouts(inp)
    f = lambda a: np.ascontiguousarray(a, dtype=np.float32)
    shared = dict(w_ada=f(inp["w_ada"]), b_ada=f(inp["b_ada"]), g_mix=f(inp["g_mix"]), g_ffn=f(inp["g_ffn"]),
                  sw_sink=f(inp["sw_sink"]), w_uq=f(inp["w_uq"]), w_ukv=f(inp["w_ukv"]), w_branch=f(inp["w_branch"]),
                  w_out=f(inp["w_out"]), w_up=f(inp["w_up"]), w_down=f(inp["w_down"]))
    shared.update(lay)
    shared.update(consts)
    maps = []
    for b in range(8):
        m = dict(shared)
        m["x"] = f(inp["x"][b])
        m["ctx"] = f(inp["ctx"][b])
        cc = np.stack([inp["c"][b], inp["c_ctx"]], axis=-1)
        m["cT"] = f(cc.reshape(8, 128, 2).transpose(1, 0, 2))
        maps.append(m)
    return maps


def kernel(**inputs):
    inp = {k: np.asarray(v) for k, v in inputs.items()}
    if "nc" not in _CACHE:
        _CACHE["nc"] = build_program(DEPTH, False)
    nc = _CACHE["nc"]
    maps = _in_maps(inp)
    res = run_bass_kernel_spmd(nc, maps, core_ids=list(range(8)))
    out = np.stack([np.asarray(r["out"], dtype=np.float32) for r in res.results], axis=0)
    return out
```
